# Optimizing a Trainium2 kernel written in Bass

```python
import math
import jax, jax.numpy as jnp
from jax import lax
import numpy as np

D_MODEL = 1024
BATCH = 16
SEQ = 4096
DEPTH = 1

N_META = 16
D_RNN = D_MODEL
N_RNN_BLOCKS = 8
RNN_BLOCK = D_RNN // N_RNN_BLOCKS
CONV_W = 4
LRU_C = 8.0
N_HEADS = 8
HEAD_DIM = D_MODEL // (2 * N_HEADS)
V_DIM = 2 * HEAD_DIM
ATTN_WIDTH = N_HEADS * V_DIM
QK_WIDTH = N_HEADS * 2 * HEAD_DIM
ROPE_THETA = 10000.0
Q_BLOCK = 128
D_FF = 4 * D_MODEL
EPS = 1e-6
IN_COLS = 2 * D_RNN + 2 * QK_WIDTH + ATTN_WIDTH + 2 * D_MODEL

kernel_name = "hybrid_rglru_diffattn_gated_block"


def _rmsnorm(x, g):
    xf = x.astype(jnp.float32)
    y = xf * lax.rsqrt(jnp.mean(xf * xf, axis=-1, keepdims=True) + EPS)
    return (y * g.astype(jnp.float32)).astype(x.dtype)


def _rope_tables(n_pos):
    inv = 1.0 / (ROPE_THETA ** (jnp.arange(0, HEAD_DIM, 2, dtype=jnp.float32) / HEAD_DIM))
    ang = jnp.arange(n_pos, dtype=jnp.float32)[:, None] * inv[None, :]
    return jnp.cos(ang), jnp.sin(ang)


def _rope(x, cos, sin):
    xf = x.astype(jnp.float32)
    x1, x2 = jnp.split(xf, 2, axis=-1)
    c = cos[None, :, None, None, :]
    s = sin[None, :, None, None, :]
    return jnp.concatenate([x1 * c - x2 * s, x2 * c + x1 * s], axis=-1).astype(x.dtype)


def _rglru_branch(xr, gr, conv_w, conv_b, w_a, b_a, w_x, b_x, lru_lambda):
    B, T, _ = xr.shape
    xf = xr.astype(jnp.float32)
    xpad = jnp.pad(xf, ((0, 0), (CONV_W - 1, 0), (0, 0)))
    xc = conv_b.astype(jnp.float32)
    for k in range(CONV_W):
        xc = xc + conv_w[k].astype(jnp.float32) * xpad[:, k:k + T]
    xb = xc.reshape(B, T, N_RNN_BLOCKS, RNN_BLOCK)
    r = jax.nn.sigmoid(jnp.einsum('btnc,ncd->btnd', xb, w_a.astype(jnp.float32)) + b_a.astype(jnp.float32))
    i = jax.nn.sigmoid(jnp.einsum('btnc,ncd->btnd', xb, w_x.astype(jnp.float32)) + b_x.astype(jnp.float32))
    r = r.reshape(B, T, D_RNN)
    i = i.reshape(B, T, D_RNN)
    log_a = -LRU_C * r * jax.nn.softplus(-lru_lambda.astype(jnp.float32))
    a = jnp.exp(log_a)
    u = jnp.sqrt(-jnp.expm1(2.0 * log_a)) * (i * xc)

    def step(h, inp):
        a_t, u_t = inp
        h = a_t * h + u_t
        return h, h

    h0 = jnp.zeros((B, D_RNN), jnp.float32)
    _, hs = lax.scan(step, h0, (jnp.swapaxes(a, 0, 1), jnp.swapaxes(u, 0, 1)))
    h = jnp.swapaxes(hs, 0, 1)
    y = jax.nn.gelu(gr.astype(jnp.float32)) * h
    return y.astype(xr.dtype)


def _diff_attn_block(q, k, v, q_pos, k_pos, lam):
    scale = 1.0 / math.sqrt(HEAD_DIM)
    s = jnp.einsum('bqhcd,bkhcd->bhcqk', q.astype(jnp.float32), k.astype(jnp.float32)) * scale
    mask = k_pos[None, :] <= q_pos[:, None]
    s = jnp.where(mask[None, None, None], s, jnp.finfo(jnp.float32).min)
    p = jax.nn.softmax(s, axis=-1)
    p_diff = p[:, :, 0] - lam * p[:, :, 1]
    return jnp.einsum('bhqk,bkhe->bqhe', p_diff, v.astype(jnp.float32))


def _diff_attention(q, k, v, lam, lam_init, g_subln, n_real):
    B, T = q.shape[0], q.shape[1]
    pos = jnp.arange(T, dtype=jnp.int32)
    outs = [_diff_attn_block(q[:, :N_META], k[:, :N_META], v[:, :N_META],
                             pos[:N_META], pos[:N_META], lam)]
    for bi in range(n_real // Q_BLOCK):
        start = N_META + bi * Q_BLOCK
        end = start + Q_BLOCK
        outs.append(_diff_attn_block(q[:, start:end], k[:, :end], v[:, :end],
                                     pos[start:end], pos[:end], lam))
    o = jnp.concatenate(outs, axis=1)
    o = o * lax.rsqrt(jnp.mean(o * o, axis=-1, keepdims=True) + EPS) * g_subln.astype(jnp.float32)
    o = o * (1.0 - lam_init)
    return o.reshape(B, T, ATTN_WIDTH).astype(q.dtype)


def setup_inputs(seed: int = 0) -> dict:
    key = jax.random.key(seed)
    ks = jax.random.split(key, 24)
    f32 = jnp.float32
    nrm = lambda k, shape, s: jax.random.normal(k, shape, f32) * s
    u = jax.random.uniform(ks[10], (DEPTH, D_RNN), f32, minval=0.9, maxval=0.999)
    base = u ** (1.0 / LRU_C)
    lru_lambda = jnp.log(base) - jnp.log1p(-base)
    return {
        "x": nrm(ks[0], (BATCH, SEQ, D_MODEL), 1.0),
        "meta_tokens": nrm(ks[1], (N_META, D_MODEL), 1.0),
        "g_mix": 1.0 + nrm(ks[2], (DEPTH, D_MODEL), 0.01),
        "w_in": nrm(ks[3], (DEPTH, D_MODEL, IN_COLS), D_MODEL ** -0.5),
        "conv_w": nrm(ks[4], (DEPTH, CONV_W, D_RNN), CONV_W ** -0.5),
        "conv_b": nrm(ks[5], (DEPTH, D_RNN), 0.01),
        "w_a": nrm(ks[6], (DEPTH, N_RNN_BLOCKS, RNN_BLOCK, RNN_BLOCK), RNN_BLOCK ** -0.5),
        "b_a": nrm(ks[7], (DEPTH, N_RNN_BLOCKS, RNN_BLOCK), 0.01),
        "w_x": nrm(ks[8], (DEPTH, N_RNN_BLOCKS, RNN_BLOCK, RNN_BLOCK), RNN_BLOCK ** -0.5),
        "b_x": nrm(ks[9], (DEPTH, N_RNN_BLOCKS, RNN_BLOCK), 0.01),
        "lru_lambda": lru_lambda,
        "lam_q1": nrm(ks[11], (DEPTH, HEAD_DIM), 0.1),
        "lam_k1": nrm(ks[12], (DEPTH, HEAD_DIM), 0.1),
        "lam_q2": nrm(ks[13], (DEPTH, HEAD_DIM), 0.1),
        "lam_k2": nrm(ks[14], (DEPTH, HEAD_DIM), 0.1),
        "g_subln": 1.0 + nrm(ks[15], (DEPTH, V_DIM), 0.01),
        "w_rnn_out": nrm(ks[16], (DEPTH, D_RNN, D_MODEL), D_RNN ** -0.5),
        "w_attn_out": nrm(ks[17], (DEPTH, ATTN_WIDTH, D_MODEL), ATTN_WIDTH ** -0.5),
        "w_o": nrm(ks[18], (DEPTH, D_MODEL, D_MODEL), D_MODEL ** -0.5),
        "g_mlp": 1.0 + nrm(ks[19], (DEPTH, D_MODEL), 0.01),
        "w_ff1": nrm(ks[20], (DEPTH, D_MODEL, D_FF), D_MODEL ** -0.5),
        "w_ff2": nrm(ks[21], (DEPTH, D_FF, D_MODEL), D_FF ** -0.5),
        "g_final": 1.0 + nrm(ks[22], (D_MODEL,), 0.01),
    }


def reference(x, meta_tokens, g_mix, w_in, conv_w, conv_b, w_a, b_a, w_x, b_x, lru_lambda,
              lam_q1, lam_k1, lam_q2, lam_k2, g_subln, w_rnn_out, w_attn_out, w_o,
              g_mlp, w_ff1, w_ff2, g_final):
    B, S, _ = x.shape
    meta = jnp.broadcast_to(meta_tokens.astype(x.dtype)[None], (B, N_META, D_MODEL))
    h = jnp.concatenate([meta, x], axis=1)
    T = h.shape[1]
    cos, sin = _rope_tables(T)
    c0 = D_RNN
    c1 = c0 + D_RNN
    c2 = c1 + QK_WIDTH
    c3 = c2 + QK_WIDTH
    c4 = c3 + ATTN_WIDTH
    c5 = c4 + D_MODEL
    for l in range(DEPTH):
        hn = _rmsnorm(h, g_mix[l])
        proj = jnp.einsum('btd,dc->btc', hn, w_in[l])
        xr, gr = proj[..., :c0], proj[..., c0:c1]
        q = proj[..., c1:c2].reshape(B, T, N_HEADS, 2, HEAD_DIM)
        k = proj[..., c2:c3].reshape(B, T, N_HEADS, 2, HEAD_DIM)
        v = proj[..., c3:c4].reshape(B, T, N_HEADS, V_DIM)
        gate_r, gate_a = proj[..., c4:c5], proj[..., c5:]

        y_r = _rglru_branch(xr, gr, conv_w[l], conv_b[l], w_a[l], b_a[l], w_x[l], b_x[l], lru_lambda[l])
        y_r = jnp.einsum('btc,cd->btd', y_r, w_rnn_out[l])

        q = _rope(q, cos, sin)
        k = _rope(k, cos, sin)
        lam_init = 0.8 - 0.6 * math.exp(-0.3 * l)
        lam = (jnp.exp(jnp.sum(lam_q1[l].astype(jnp.float32) * lam_k1[l].astype(jnp.float32)))
               - jnp.exp(jnp.sum(lam_q2[l].astype(jnp.float32) * lam_k2[l].astype(jnp.float32)))
               + lam_init)
        y_a = _diff_attention(q, k, v, lam, lam_init, g_subln[l], S)
        y_a = jnp.einsum('btc,cd->btd', y_a, w_attn_out[l])

        merged = jax.nn.sigmoid(gate_r) * y_r + jax.nn.sigmoid(gate_a) * y_a
        h = h + jnp.einsum('btd,de->bte', merged, w_o[l])

        hn2 = _rmsnorm(h, g_mlp[l])
        ff = jnp.square(jax.nn.relu(jnp.einsum('btd,df->btf', hn2, w_ff1[l])))
        h = h + jnp.einsum('btf,fd->btd', ff, w_ff2[l])
    out = _rmsnorm(h, g_final)
    return out[:, N_META:]
```

```python
import contextlib
import numpy as np
import concourse.bass as bass
import concourse.mybir as mybir
from concourse.bass_utils import run_bass_kernel_spmd

F32 = mybir.dt.float32
BF16 = mybir.dt.bfloat16
AF = mybir.ActivationFunctionType
ALU = mybir.AluOpType
AX = mybir.AxisListType

D = 1024
NMETA = 16
EPS = 1e-6
LAM_INIT = 0.8 - 0.6 * 1.0
GC0 = 0.7978845608028654
GC1 = 0.044715
NSLOT = 36


class Reg:
    __slots__ = ("name", "w", "rs", "excl")

    def __init__(self, name, excl=False):
        self.name = name
        self.w = None
        self.rs = {}
        self.excl = excl


class Op:
    __slots__ = ("eng", "fn", "chan", "idx", "eidx", "deps", "sig", "tok", "stream")


class Prog:
    ENGS = ("pe", "act", "dve", "pool", "sp")

    def __init__(self):
        self.ops = []
        self.eng_ops = {e: [] for e in self.ENGS}
        self.chan_last = {}
        self.chan_cnt = {}

    def _add(self, eng, fn, r, w, chan=None):
        op = Op()
        op.eng = eng
        op.fn = fn
        op.chan = chan
        op.idx = len(self.ops)
        op.eidx = len(self.eng_ops[eng])
        op.sig = chan is not None
        op.tok = None
        op.stream = ("c", chan) if chan is not None else eng
        if any(R.excl for R in r):
            w = list(w) + [R for R in r if R.excl and R not in w]
            r = [R for R in r if not R.excl]
        deps = set()
        for R in r:
            if R.w is not None:
                deps.add(R.w)
        for R in w:
            if R.w is not None:
                deps.add(R.w)
            for x in R.rs.values():
                deps.add(x)
        if chan is not None and chan in self.chan_last:
            deps.add(self.chan_last[chan])
        deps.discard(op.idx)
        best = {}
        for d in deps:
            p = self.ops[d]
            if p.stream not in best or best[p.stream] < d:
                best[p.stream] = d
        final = []
        for st, d in best.items():
            p = self.ops[d]
            if chan is None and st == eng:
                if eng == "pe":
                    continue
            final.append(d)
            p.sig = True
        op.deps = final
        for R in r:
            R.rs[op.stream] = op.idx
        for R in w:
            R.w = op.idx
            R.rs = {}
        if chan is not None:
            self.chan_last[chan] = op.idx
            self.chan_cnt[chan] = self.chan_cnt.get(chan, 0) + 1
        self.ops.append(op)
        self.eng_ops[eng].append(op)
        return op

    def op(self, eng, fn, r=(), w=()):
        return self._add(eng, fn, r, w)

    def dma(self, eng, chan, fn, r=(), w=()):
        return self._add(eng, fn, r, w, chan=chan)

    def emit(self, nc, final_waits):
        with contextlib.ExitStack() as st:
            esem = {e: st.enter_context(nc.semaphore("s_" + e)) for e in ("pe", "act", "dve", "pool")}
            csem = {c: st.enter_context(nc.semaphore("c_" + c)) for c in self.chan_cnt}
            cnt = {e: 0 for e in esem}
            ccnt = {c: 0 for c in csem}
            for op in self.ops:
                if op.chan is not None:
                    ccnt[op.chan] += 16
                    op.tok = (csem[op.chan], ccnt[op.chan], "c_" + op.chan)
                elif op.sig:
                    cnt[op.eng] += 1
                    op.tok = (esem[op.eng], cnt[op.eng], "s_" + op.eng)
            ops = self.ops
            eng_ops = self.eng_ops

            def run(engname, e, extra=None):
                waited = {}
                for op in eng_ops[engname]:
                    for d in op.deps:
                        sem, val, key = ops[d].tok
                        if waited.get(key, 0) >= val:
                            continue
                        e.wait_ge(sem, val)
                        waited[key] = val
                    ins = op.fn(e)
                    if op.chan is not None:
                        ins.then_inc(op.tok[0], 16)
                    elif op.sig:
                        ins.then_inc(op.tok[0], 1)
                if extra:
                    for o in extra:
                        sem, val, key = o.tok
                        if waited.get(key, 0) >= val:
                            continue
                        e.wait_ge(sem, val)
                        waited[key] = val

            with nc.Block() as block:
                @block.tensor
                def _(e):
                    run("pe", e)

                @block.scalar
                def _(e):
                    run("act", e)

                @block.vector
                def _(e):
                    run("dve", e)

                @block.gpsimd
                def _(e):
                    run("pool", e)

                @block.sync
                def _(e):
                    run("sp", e, extra=final_waits)


class Rot:
    def __init__(self, items):
        self.items = items
        self.i = 0

    def get(self):
        it = self.items[self.i % len(self.items)]
        self.i += 1
        return it


class StopBuild(Exception):
    pass


def build(S, NB, dbg=0):
    NCH = S // 512
    NKB = S // 128
    T = NMETA + S
    nc = bass.Bass("TRN2", target_bir_lowering=False)
    P = Prog()

    def din(name, shape, dt=F32):
        return nc.dram_tensor(name, list(shape), dt, kind="ExternalInput").ap()

    x_d = din("x", [NB, S, D])
    meta_d = din("meta", [NMETA, D])
    w_in_d = din("w_in", [D, 7 * D])
    w_rnn_d = din("w_rnn_out", [D, D])
    w_att_d = din("w_attn_out", [D, D])
    w_o_d = din("w_o", [D, D])
    w_ff1_d = din("w_ff1", [D, 4 * D])
    w_ff2_d = din("w_ff2", [4 * D, D])
    wg_d = din("wgate", [128, 2 * 8 * 128])
    pv_d = din("pvec", [128, 96])
    pb_d = din("pbc", [128, 1024 + 128 + 256])
    cm_d = din("cmat", [128, 3 * 128])
    cos_d = din("rcos", [128, T])
    sin_d = din("rsin", [128, T])
    y_d = nc.dram_tensor("y", [NB, S, D], F32, kind="ExternalOutput").ap()
    wbf_d = nc.dram_tensor("wbf", [NSLOT, 128, 4096], BF16, kind="Internal").ap()
    kc_d = nc.dram_tensor("kcache", [8, 128, S], BF16, kind="Internal").ap()

    with contextlib.ExitStack() as es:
        def sb(name, cols, dt):
            return es.enter_context(nc.sbuf_tensor(name, [128, cols], dt))

        vc = sb("vc", NKB * 8 * 129, BF16)
        vmeta = sb("vmeta", 8 * 129, BF16)
        kmeta = sb("kmeta", 8 * 32, BF16)
        wg = sb("wg", 2 * 8 * 128, BF16)
        cb = sb("cb", 3 * 128, BF16)
        identf = sb("identf", 128, F32)
        pv = sb("pv", 96, F32)
        pd = sb("pd", 64, F32)
        pbt = sb("pbt", 1024 + 128 + 256, F32)
        kb = sb("kb", 2 * S, BF16)
        hbuf = sb("hbuf", 4 * 1024, F32)
        hnT = sb("hnT", 8 * 512, BF16)
        big = sb("big", 32 * 512, BF16)
        wr = sb("wr", 3 * 4096, BF16)
        WT = 520
        Wt = sb("Wt", 8 * WT, F32)
        rc = sb("rc", 2 * 512, F32)
        et = sb("et", 2 * 1024, BF16)
        sm = sb("sm", 256, F32)
        xn = sb("xn", 2 * 1024, BF16)
        kth = sb("kth", 2 * 512, BF16)
        state = sb("state", 8 + 24 + 8 + 24, F32)
        psb = [es.enter_context(nc.psum_tensor("ps%d" % i, [128, 512], F32)) for i in range(8)]

        R_ps = [Reg("ps%d" % i, excl=True) for i in range(8)]
        R_h = [Reg("h%d" % i) for i in range(4)]
        R_hnT = [Reg("hnT%d" % i) for i in range(8)]
        R_big = [Reg("big%d" % i) for i in range(32)]
        R_wr = [Reg("wr%d" % i) for i in range(3)]
        R_W = [Reg("W%d" % i) for i in range(8)]
        R_rc = Reg("rc")
        R_et = [Reg("et0"), Reg("et1")]
        R_xn = [Reg("xn0"), Reg("xn1")]
        R_kth = [Reg("kth0"), Reg("kth1")]
        R_kb = [Reg("kb0"), Reg("kb1")]
        R_kd = [Reg("kd%d" % i) for i in range(8)]
        R_wbf = [Reg("wbf%d" % i) for i in range(NSLOT)]
        R_vc = [Reg("vc%d" % i) for i in range(NKB)]
        R_vmeta = Reg("vmeta")
        R_kmeta = Reg("kmeta")
        R_const = Reg("const")
        R_state = [Reg("st%d" % i) for i in range(8)]
        R_state0 = Reg("state0")
        R_sm = [Reg("sm%d" % i) for i in range(64)]

        h3 = hbuf[:, :].rearrange("p (t d) -> p t d", t=4)
        hnT3 = hnT[:, :].rearrange("p (k t) -> p k t", k=8)
        big3 = big[:, :].rearrange("p (i t) -> p i t", i=32)
        wr3 = [wr[:, s * 4096:(s + 1) * 4096] for s in range(3)]
        Wv = [Wt[:, i * WT:(i + 1) * WT] for i in range(8)]
        vc4 = vc[:, :].rearrange("p (b h e) -> p b h e", b=NKB, h=8)
        vmeta3 = vmeta[:, :].rearrange("p (h e) -> p h e", h=8)
        kmeta3 = kmeta[:, :].rearrange("p (h t) -> p h t", h=8)
        wg4 = wg[:, :].rearrange("p (w n d) -> p w n d", w=2, n=8)
        ident_b = cb[:, 0:128]
        tri_b = cb[:, 128:256]
        perm_b = cb[:, 256:384]
        gfin = pbt[:, 0:1024]
        gsub = pbt[:, 1024:1152]
        g_mix = pv[:, 0:8]
        g_mlp = pv[:, 8:16]
        conv_w = pv[:, 16:48].rearrange("p (t n) -> p t n", t=4)
        conv_b = pv[:, 48:56]
        hcl = pd[:, 0:8]
        hba = pd[:, 8:16]
        hbx = pd[:, 16:24]
        nlam = pd[:, 24:25]
        negh = pd[:, 25:26]
        halfc = pd[:, 26:27]
        hstate = state[:, 0:8]
        carry = state[:, 8:32].rearrange("p (n k) -> p n k", n=8)
        hstate0 = state[:, 32:40]
        carry0 = state[:, 40:64].rearrange("p (n k) -> p n k", n=8)
        ps_bf = [p[:, :].bitcast(BF16) for p in psb]

        sm_rot = Rot([(sm[:, i:i + 1], R_sm[i]) for i in range(64)])
        W_rot = Rot([(Wv[i], R_W[i]) for i in range(8)])
        ps_rot = Rot([(psb[i], R_ps[i], ps_bf[i]) for i in range(8)])
        xn_rot = Rot([(xn[:, i * 1024:(i + 1) * 1024], R_xn[i]) for i in range(2)])
        kth_rot = Rot([(kth[:, i * 512:(i + 1) * 512], R_kth[i]) for i in range(2)])

        def act(out, in_, func, r, w, bias=None, scale=None, accum=None):
            kw = {}
            if bias is not None:
                kw["bias"] = bias
            if scale is not None:
                kw["scale"] = scale
            if accum is not None:
                kw["accum_out"] = accum
            return P.op("act", lambda e: e.activation(out=out, in_=in_, func=func, **kw), r=r, w=w)

        def ts(eng, out, in0, s1, s2, op0, op1, r, w):
            if op1 is None:
                return P.op(eng, lambda e: e.tensor_scalar(out=out, in0=in0, scalar1=s1, scalar2=None, op0=op0), r=r, w=w)
            return P.op(eng, lambda e: e.tensor_scalar(out=out, in0=in0, scalar1=s1, scalar2=s2, op0=op0, op1=op1), r=r, w=w)

        def stt(out, in0, scalar, in1, op0, op1, r, w):
            return P.op("dve", lambda e: e.scalar_tensor_tensor(out=out, in0=in0, scalar=scalar, in1=in1, op0=op0, op1=op1), r=r, w=w)

        def tt(eng, out, in0, in1, op, r, w):
            return P.op(eng, lambda e: e.tensor_tensor(out=out, in0=in0, in1=in1, op=op), r=r, w=w)

        def cp(eng, out, in_, r, w):
            if eng == "act":
                return P.op("act", lambda e: e.copy(out=out, in_=in_), r=r, w=w)
            return P.op(eng, lambda e: e.tensor_copy(out=out, in_=in_), r=r, w=w)

        def mm(out, lhsT, rhs, start, stop, r, w, skip=False):
            return P.op("pe", lambda e: e.matmul(out, lhsT, rhs, start=start, stop=stop, skip_group_check=skip), r=r, w=w)

        def tr(out, in_, ident, r, w):
            return P.op("pe", lambda e: e.transpose(out, in_, ident), r=r, w=w)

        def dma(eng, chan, out, in_, r, w, slow=False):
            if slow:
                return P.dma(eng, chan, lambda e: e.dma_start(out=out, in_=in_, allow_slow_non_contiguous=True), r=r, w=w)
            return P.dma(eng, chan, lambda e: e.dma_start(out=out, in_=in_), r=r, w=w)

        def rstd_from_ss(ss_ap, ss_reg, inv_n, npart=128):
            v, rv = sm_rot.get()
            ts("pool", v[:npart, :], ss_ap, inv_n, EPS, ALU.mult, ALU.add, r=[ss_reg], w=[rv])
            o, ro = sm_rot.get()
            tt("pool", o[:npart, :], v[:npart, :], negh[:npart, :], ALU.pow, r=[rv, R_const], w=[ro])
            return o, ro

        final_ops = []
        def stage(k):
            if dbg == k:
                raise StopBuild()
        try:
            st_f = Wv[0][:, 0:384]
            dma("pool", "su0", st_f, cm_d, r=[], w=[R_W[0]])
            cp("dve", cb[:, :], st_f, r=[R_W[0]], w=[R_const])
            dma("pool", "su1", identf[:, :], cm_d[:, 0:128], r=[], w=[R_const])
            dma("pool", "su0", pv[:, :], pv_d, r=[], w=[R_const])
            dma("pool", "su1", pbt[:, :], pb_d, r=[], w=[R_const])
            for half in range(2):
                for q in range(2):
                    wt_, rw_ = W_rot.get()
                    dma("pool", "su%d" % q, wt_[:, 0:512], wg_d[:, half * 1024 + q * 512: half * 1024 + (q + 1) * 512], r=[], w=[rw_])
                    cp("dve", wg[:, half * 1024 + q * 512: half * 1024 + (q + 1) * 512], wt_[:, 0:512], r=[rw_], w=[R_const])
            P.op("dve", lambda e: e.memset(negh, -0.5), r=[], w=[R_const])
            P.op("dve", lambda e: e.memset(halfc, 0.5), r=[], w=[R_const])
            lam_raw = pv[:, 56:64]
            tmp8 = pd[:, 32:40]
            act(tmp8, lam_raw, AF.Exp, r=[R_const], w=[R_const], scale=-1.0)
            act(tmp8, tmp8, AF.Ln, r=[R_const], w=[R_const], bias=1.0)
            ts("dve", hcl, tmp8, -4.0, None, ALU.mult, None, r=[R_const], w=[R_const])
            ts("dve", hba, pv[:, 64:72], 0.5, None, ALU.mult, None, r=[R_const], w=[R_const])
            ts("dve", hbx, pv[:, 72:80], 0.5, None, ALU.mult, None, r=[R_const], w=[R_const])
            lq = pbt[:, 1152:1408]
            prod = pd[:, 40:42]
            tl = Wv[1][:, 0:128]
            tt("dve", tl[:, 0:64], lq[:, 0:64], lq[:, 64:128], ALU.mult, r=[R_const], w=[R_W[1]])
            tt("dve", tl[:, 64:128], lq[:, 128:192], lq[:, 192:256], ALU.mult, r=[R_const], w=[R_W[1]])
            P.op("dve", lambda e: e.tensor_reduce(out=prod, in_=tl.rearrange("p (a b) -> p a b", a=2), axis=AX.X, op=ALU.add), r=[R_W[1]], w=[R_const])
            act(prod, prod, AF.Exp, r=[R_const], w=[R_const])
            tt("dve", nlam, prod[:, 1:2], prod[:, 0:1], ALU.subtract, r=[R_const], w=[R_const])
            ts("dve", nlam, nlam, -LAM_INIT, None, ALU.add, None, r=[R_const], w=[R_const])
            ts("dve", gsub, gsub, 1.0 - LAM_INIT, None, ALU.mult, None, r=[R_const], w=[R_const])
            P.op("pool", lambda e: e.memset(vc4[:, :, :, 128:129], 1.0), r=[], w=R_vc)

            stage(1)
            w_in3 = w_in_d.rearrange("(k p) c -> p k c", p=128)
            w_rnn3 = w_rnn_d.rearrange("(k p) c -> p k c", p=128)
            w_att3 = w_att_d.rearrange("(k p) c -> p k c", p=128)
            w_o3 = w_o_d.rearrange("(k p) c -> p k c", p=128)
            w_ff13 = w_ff1_d.rearrange("(k p) c -> p k c", p=128)
            w_ff23 = w_ff2_d.rearrange("(k p) c -> p k c", p=128)

            def win(c0, n=512):
                return [(w_in3[:, :, c0:c0 + n], 8, n)]

            slot_src = []
            for j in range(4):
                slot_src.append([(w_in3[:, :, 256 * j:256 * j + 256], 8, 256), (w_in3[:, :, 1024 + 256 * j:1024 + 256 * j + 256], 8, 256)])
            for j in range(2):
                slot_src.append(win(5120 + 512 * j))
                slot_src.append([(w_rnn3[:, :, 512 * j:512 * j + 512], 8, 512)])
            for j in range(2):
                slot_src.append(win(2048 + 512 * j))
            for j in range(2):
                slot_src.append(win(3072 + 512 * j))
            for j in range(2):
                slot_src.append(win(4096 + 512 * j))
            for j in range(2):
                slot_src.append(win(6144 + 512 * j))
                slot_src.append([(w_att3[:, :, 512 * j:512 * j + 512], 8, 512)])
            for j in range(2):
                slot_src.append([(w_o3[:, :, 512 * j:512 * j + 512], 8, 512)])
            for j in range(8):
                slot_src.append([(w_ff13[:, :, 512 * j:512 * j + 512], 8, 512)])
            for j in range(8):
                slot_src.append([(w_ff23[:, 8 * q:8 * q + 8, 128 * j:128 * j + 128], 8, 128) for q in range(4)])
            assert len(slot_src) == NSLOT
            SL_A, SL_G, SL_R, SL_Q, SL_K, SL_V, SL_GA, SL_AO, SL_O, SL_F, SL_H = 0, 4, 5, 8, 10, 12, 14, 15, 18, 20, 28

            stg = [(hbuf[:, :], R_h), (Wt[:, 0:4096], R_W)]
            cast_eng = ["act", "dve", "pool"]
            for s in range(NSLOT):
                sbuf_f, regs_f = stg[s % 2]
                off = 0
                if s < 4:
                    st3 = sbuf_f[:, 0:4096].rearrange("p (a b) -> p a b", a=8)
                    for i_, (src, a, b) in enumerate(slot_src[s]):
                        dma("sp", "w%d" % (s % 2), st3[:, :, 256 * i_:256 * i_ + 256], src, r=[], w=regs_f)
                else:
                    for (src, a, b) in slot_src[s]:
                        dst = sbuf_f[:, off:off + a * b].rearrange("p (a b) -> p a b", a=a)
                        dma("sp", "w%d" % (s % 2), dst, src, r=[], w=regs_f)
                        off += a * b
                    assert off == 4096
                ring = s % 3
                for q in range(3):
                    lo, hi = [0, 1408, 2752][q], [1408, 2752, 4096][q]
                    cp(cast_eng[q], wr3[ring][:, lo:hi], sbuf_f[:, lo:hi], r=regs_f, w=[R_wr[ring]])
                dma("pool", "su%d" % (s % 2), wbf_d[s], wr3[ring], r=[R_wr[ring]], w=[R_wbf[s]])

            stage(2)
            wcount = [0]

            def wload(slot):
                ring = wcount[0] % 3
                wcount[0] += 1
                dma("sp", "w%d" % ring, wr3[ring], wbf_d[slot], r=[R_wbf[slot]], w=[R_wr[ring]])
                return wr3[ring].rearrange("p (k c) -> p k c", k=8), R_wr[ring], wr3[ring]

            def norm_to_hnT(src_tiles, ntok, gvec, nblk):
                for tb in range(nblk):
                    src, rsrc = src_tiles[tb]
                    xt, rx = xn_rot.get()
                    ss, rss = sm_rot.get()
                    act(xt[:ntok, :], src, AF.Square, r=[rsrc], w=[rx, rss], accum=ss[:ntok, :])
                    rstd, rr = rstd_from_ss(ss[:ntok, :], rss, 1.0 / D, npart=ntok)
                    ts("dve", xt[:ntok, :], src, rstd[:ntok, :], None, ALU.mult, None, r=[rsrc, rr], w=[rx])
                    pst, rps, psbf = ps_rot.get()
                    for kc in range(8):
                        tr(psbf[:, kc * ntok:(kc + 1) * ntok], xt[:ntok, kc * 128:(kc + 1) * 128], ident_b[:ntok, :ntok], r=[rx, R_const], w=[rps])
                    outv = hnT3[:, :, tb * 128:tb * 128 + ntok]
                    inv = psbf[:, 0:8 * ntok].rearrange("p (k t) -> p k t", k=8)
                    gb = gvec.unsqueeze(2).to_broadcast([128, 8, ntok])
                    tt("dve", outv, inv, gb, ALU.mult, r=[rps, R_const], w=R_hnT)

            def proj_fm(slot3, rslot, cb_, ntok, pst, rps):
                for kc in range(8):
                    mm(pst[:, 0:ntok], slot3[:, kc, cb_ * 128:(cb_ + 1) * 128], hnT3[:, kc, 0:ntok], kc == 0, kc == 7, r=[rslot, R_hnT[kc]], w=[rps])

            def rnn_block(n, ntok, xr_slot, xr_cb, gr_slot, gr_cb, rx_slot, rg_slot, with_y):
                pst, rps, _ = ps_rot.get()
                proj_fm(xr_slot, rx_slot, xr_cb, ntok, pst, rps)
                xr, rxr = W_rot.get()
                cp("pool", xr[:, 0:3], carry[:, n, :], r=[R_state[n]], w=[rxr])
                cp("act", xr[:, 3:3 + ntok], pst[:, 0:ntok], r=[rps], w=[rxr])
                xc, rxc = W_rot.get()
                ts("dve", xc[:, 0:ntok], xr[:, 3:3 + ntok], conv_w[:, 3, n:n + 1], conv_b[:, n:n + 1], ALU.mult, ALU.add, r=[rxr, R_const], w=[rxc])
                for k in range(3):
                    stt(xc[:, 0:ntok], xr[:, k:k + ntok], conv_w[:, k, n:n + 1], xc[:, 0:ntok], ALU.mult, ALU.add, r=[rxr, rxc, R_const], w=[rxc])
                cp("pool", carry[:, n, :], xr[:, ntok:ntok + 3], r=[rxr], w=[R_state[n]])
                xcb, rxcb = xn_rot.get()
                cp("pool", xcb[:, 0:ntok], xc[:, 0:ntok], r=[rxc], w=[rxcb])
                pa, rpa, _ = ps_rot.get()
                mm(pa[:, 0:ntok], wg4[:, 0, n, :], xcb[:, 0:ntok], True, True, r=[rxcb, R_const], w=[rpa])
                pi, rpi, _ = ps_rot.get()
                mm(pi[:, 0:ntok], wg4[:, 1, n, :], xcb[:, 0:ntok], True, True, r=[rxcb, R_const], w=[rpi])
                ta, rta = W_rot.get()
                act(ta[:, 0:ntok], pa[:, 0:ntok], AF.Tanh, r=[rpa, R_const], w=[rta], bias=hba[:, n:n + 1], scale=0.5)
                ti, rti = W_rot.get()
                act(ti[:, 0:ntok], pi[:, 0:ntok], AF.Tanh, r=[rpi, R_const], w=[rti], bias=hbx[:, n:n + 1], scale=0.5)
                act(ta[:, 0:ntok], ta[:, 0:ntok], AF.Exp, r=[rta, R_const], w=[rta], bias=hcl[:, n:n + 1], scale=hcl[:, n:n + 1])
                sq, rsq = W_rot.get()
                tt("pool", sq[:, 0:ntok], ta[:, 0:ntok], ta[:, 0:ntok], ALU.mult, r=[rta], w=[rsq])
                act(sq[:, 0:ntok], sq[:, 0:ntok], AF.Sqrt, r=[rsq], w=[rsq], bias=1.0, scale=-1.0)
                stt(ti[:, 0:ntok], ti[:, 0:ntok], 1.0, xc[:, 0:ntok], ALU.add, ALU.mult, r=[rti, rxc], w=[rti])
                stt(ti[:, 0:ntok], ti[:, 0:ntok], 0.5, sq[:, 0:ntok], ALU.mult, ALU.mult, r=[rti, rsq], w=[rti])
                hh, rhh = W_rot.get()
                P.op("dve", lambda e: e.tensor_tensor_scan(out=hh[:, 0:ntok], data0=ta[:, 0:ntok], data1=ti[:, 0:ntok], initial=hstate[:, n:n + 1], op0=ALU.mult, op1=ALU.add),
                     r=[rta, rti, R_state[n]], w=[rhh])
                cp("pool", hstate[:, n:n + 1], hh[:, ntok - 1:ntok], r=[rhh], w=[R_state[n]])
                if not with_y:
                    return
                pg, rpg, _ = ps_rot.get()
                proj_fm(gr_slot, rg_slot, gr_cb, ntok, pg, rpg)
                g2, rg2 = W_rot.get()
                act(g2[:, 0:ntok], pg[:, 0:ntok], AF.Square, r=[rpg], w=[rg2])
                ts("dve", g2[:, 0:ntok], g2[:, 0:ntok], GC1 * GC0, GC0, ALU.mult, ALU.add, r=[rg2], w=[rg2])
                tt("dve", g2[:, 0:ntok], g2[:, 0:ntok], pg[:, 0:ntok], ALU.mult, r=[rg2, rpg], w=[rg2])
                act(g2[:, 0:ntok], g2[:, 0:ntok], AF.Tanh, r=[rg2], w=[rg2])
                stt(g2[:, 0:ntok], g2[:, 0:ntok], 1.0, pg[:, 0:ntok], ALU.add, ALU.mult, r=[rg2, rpg], w=[rg2])
                stt(big3[:, 8 + n, :], g2[:, 0:ntok], 0.5, hh[:, 0:ntok], ALU.mult, ALU.mult, r=[rg2, rhh], w=[R_big[8 + n]])

            def rope_block(pst, rps, ntok, dst, rdst):
                qb, rqb = xn_rot.get()
                cp("act", qb[:, 0:ntok], pst[:, 0:ntok], r=[rps], w=[rqb])
                p2, rp2, _ = ps_rot.get()
                mm(p2[:, 0:ntok], perm_b, qb[:, 0:ntok], True, True, r=[rqb, R_const], w=[rp2])
                t1, rt1 = W_rot.get()
                tt("dve", t1[:, 0:ntok], rc[:, 0:ntok], pst[:, 0:ntok], ALU.mult, r=[rps, rqb, R_rc], w=[rt1])
                t2, rt2 = W_rot.get()
                tt("dve", t2[:, 0:ntok], rc[:, 512:512 + ntok], p2[:, 0:ntok], ALU.mult, r=[rp2, R_rc], w=[rt2])
                tt("pool", dst, t1[:, 0:ntok], t2[:, 0:ntok], ALU.add, r=[rt1, rt2], w=rdst)

            P.op("pool", lambda e: e.memset(hbuf[:32, 0:1024], 0.0), r=[], w=[R_h[0]])
            P.op("pool", lambda e: e.memset(kmeta[:, :], 0.0), r=[], w=[R_kmeta])
            P.op("pool", lambda e: e.memset(vmeta[:32, :], 0.0), r=[], w=[R_vmeta])
            P.op("pool", lambda e: e.memset(vmeta3[:NMETA, :, 128:129], 1.0), r=[], w=[R_vmeta])
            dma("pool", "x0", hbuf[:NMETA, 0:1024], meta_d, r=[], w=[R_h[0]])
            dma("pool", "rc", rc[:, 0:NMETA], cos_d[:, 0:NMETA], r=[], w=[R_rc])
            dma("pool", "rc", rc[:, 512:512 + NMETA], sin_d[:, 0:NMETA], r=[], w=[R_rc])
            P.op("pool", lambda e: e.memset(state[:, :], 0.0), r=[], w=R_state)
            norm_to_hnT([(hbuf[:32, 0:1024], R_h[0])], 32, g_mix, 1)
            stage(31)
            for j in range(4):
                s3, rs_, _ = wload(SL_A + j)
                for q in range(2):
                    rnn_block(2 * j + q, NMETA, s3, q, None, None, rs_, None, False)
            stage(33)
            cp("pool", state[:, 32:64], state[:, 0:32], r=R_state, w=[R_state0])
            for j in range(2):
                s3, rs_, _ = wload(SL_K + j)
                for q in range(4):
                    hd = 4 * j + q
                    pst, rps, _ = ps_rot.get()
                    proj_fm(s3, rs_, q, NMETA, pst, rps)
                    rope_block(pst, rps, NMETA, kmeta3[:, hd, 0:NMETA], [R_kmeta])
            stage(34)
            for j in range(2):
                s3, rs_, _ = wload(SL_V + j)
                pst, rps, _ = ps_rot.get()
                for kc in range(8):
                    mm(pst[:32, :], hnT3[:, kc, 0:32], s3[:, kc, :], kc == 0, kc == 7, r=[rs_, R_hnT[kc]], w=[rps])
                cp("act", vmeta3[:NMETA, 4 * j:4 * j + 4, 0:128], pst[:NMETA, :].rearrange("p (h e) -> p h e", h=4), r=[rps], w=[R_vmeta])

            stage(3)
            def oacc(c, qs):
                i = c * 4 + qs
                bank = 5 + i // 3
                off = (i % 3) * 129
                return psb[bank][:, off:off + 129], R_ps[bank], (i % 3 == 0)

            for b in range(NB):
                cp("pool", state[:, 0:32], state[:, 32:64], r=[R_state0], w=R_state)
                for ci in range(NCH):
                    r0 = 512 * ci
                    p0 = NMETA + r0
                    for tb in range(4):
                        dma("pool", "x%d" % tb, h3[:, tb, :], x_d[b, r0 + tb * 128:r0 + (tb + 1) * 128, :], r=[], w=[R_h[tb]])
                    dma("pool", "rc", rc[:, 0:512], cos_d[:, p0:p0 + 512], r=[], w=[R_rc])
                    dma("pool", "rc", rc[:, 512:1024], sin_d[:, p0:p0 + 512], r=[], w=[R_rc])
                    norm_to_hnT([(h3[:, tb, :], R_h[tb]) for tb in range(4)], 128, g_mix, 4)
                    stage(40)
                    for j in range(4):
                        s3, rs_, _ = wload(SL_A + j)
                        for q in range(2):
                            rnn_block(2 * j + q, 512, s3, q, s3, 2 + q, rs_, rs_, True)
                    stage(41)
                    for j in range(2):
                        g3, rg_, _ = wload(SL_G + 2 * j)
                        o3, ro_, _ = wload(SL_R + 2 * j)
                        for q in range(4):
                            db = 4 * j + q
                            pg, rpg, _ = ps_rot.get()
                            proj_fm(g3, rg_, q, 512, pg, rpg)
                            tg, rtg = W_rot.get()
                            act(tg[:, 0:512], pg[:, 0:512], AF.Tanh, r=[rpg], w=[rtg], scale=0.5)
                            py, rpy, _ = ps_rot.get()
                            for kc in range(8):
                                mm(py[:, :], o3[:, kc, q * 128:(q + 1) * 128], big3[:, 8 + kc, :], kc == 0, kc == 7, r=[ro_, R_big[8 + kc]], w=[rpy])
                            stt(big3[:, 24 + db, :], tg[:, 0:512], 1.0, py[:, :], ALU.add, ALU.mult, r=[rtg, rpy], w=[R_big[24 + db]])
                    stage(42)
                    for j in range(2):
                        s3, rs_, _ = wload(SL_Q + j)
                        for q in range(4):
                            hd = 4 * j + q
                            pst, rps, _ = ps_rot.get()
                            proj_fm(s3, rs_, q, 512, pst, rps)
                            rope_block(pst, rps, 512, big3[:, hd, :], [R_big[hd]])
                    stage(43)
                    for j in range(2):
                        s3, rs_, _ = wload(SL_K + j)
                        for q in range(4):
                            hd = 4 * j + q
                            pst, rps, _ = ps_rot.get()
                            proj_fm(s3, rs_, q, 512, pst, rps)
                            kt, rkt = kth_rot.get()
                            rope_block(pst, rps, 512, kt, [rkt])
                            dma("sp", "kw%d" % (hd % 2), kc_d[hd, :, r0:r0 + 512], kt, r=[rkt], w=[R_kd[hd]])
                    stage(44)
                    for j in range(2):
                        s3, rs_, _ = wload(SL_V + j)
                        for tb in range(4):
                            pst, rps, _ = ps_rot.get()
                            for kc in range(8):
                                mm(pst[:, :], hnT3[:, kc, tb * 128:(tb + 1) * 128], s3[:, kc, :], kc == 0, kc == 7, r=[rs_, R_hnT[kc]], w=[rps])
                            blk = 4 * ci + tb
                            cp("act" if tb % 2 == 0 else "dve", vc4[:, blk, 4 * j:4 * j + 4, 0:128], pst[:, :].rearrange("p (h e) -> p h e", h=4), r=[rps], w=[R_vc[blk]])
                    stage(45)
                    nreal = 4 * ci + 4
                    kend = r0 + 512
                    for hd in range(8):
                        ks = hd % 2
                        kbv = kb[:, ks * S:ks * S + kend]
                        dma("sp", "k%d" % ks, kbv, kc_d[hd, :, 0:kend], r=[R_kd[hd]], w=[R_kb[ks]])
                        qT = big3[:, hd, :]
                        rq = R_big[hd]
                        for kbk in range(-1, nreal):
                            if kbk < 0:
                                kk, q_lo, kr = 32, 0, [R_kmeta]
                            else:
                                jj = kbk - 4 * ci
                                q_lo = 128 * jj if jj > 0 else 0
                                kk, kr = 128, [R_kb[ks]]
                            nq = 512 - q_lo
                            eslot = (kbk + 1) % 2
                            ev = et[:, eslot * 1024:(eslot + 1) * 1024].rearrange("p (c t) -> p c t", c=2)
                            for c in range(2):
                                sbank = 2 * eslot + c
                                if kbk < 0:
                                    lhs = kmeta3[c * 64:(c + 1) * 64, hd, :]
                                else:
                                    lhs = kbv[c * 64:(c + 1) * 64, kbk * 128:(kbk + 1) * 128]
                                mm(psb[sbank][:kk, 0:nq], lhs, qT[c * 64:(c + 1) * 64, q_lo:512], True, True, r=kr + [rq], w=[R_ps[sbank]])
                            for c in range(2):
                                sbank = 2 * eslot + c
                                act(ev[:kk, c, q_lo:512], psb[sbank][:kk, 0:nq], AF.Exp, r=[R_ps[sbank]], w=[R_et[eslot]], scale=0.125)
                            if kbk >= 0 and kbk >= 4 * ci:
                                jj = kbk - 4 * ci
                                for c in range(2):
                                    tt("pool", ev[:, c, jj * 128:(jj + 1) * 128], ev[:, c, jj * 128:(jj + 1) * 128], tri_b, ALU.mult, r=[R_et[eslot], R_const], w=[R_et[eslot]])
                                qs_list = list(range(jj, 4))
                            else:
                                jj = -1
                                qs_list = [0, 1, 2, 3]
                            for c in range(2):
                                for qs in qs_list:
                                    oa, ro, first_in_bank = oacc(c, qs)
                                    if kbk < 0:
                                        rhs = vmeta3[:32, hd, :]
                                        rv = R_vmeta
                                    else:
                                        rhs = vc4[:, kbk, hd, :]
                                        rv = R_vc[kbk]
                                    last = (kbk == 4 * ci + qs)
                                    mm(oa, ev[:kk, c, qs * 128:(qs + 1) * 128], rhs, (kbk < 0) and first_in_bank, last, r=[R_et[eslot], rv], w=[ro], skip=True)
                        stage(46)
                        ptr, rptr, ptr_bf = ps_rot.get()
                        while rptr in (R_ps[5], R_ps[6], R_ps[7]):
                            ptr, rptr, ptr_bf = ps_rot.get()
                        for qs in range(4):
                            a0, ra0, _ = oacc(0, qs)
                            a1, ra1, _ = oacc(1, qs)
                            ri0, rri0 = sm_rot.get()
                            P.op("dve", lambda e, o=ri0, i=a0[:, 128:129]: e.reciprocal(out=o, in_=i), r=[ra0], w=[rri0])
                            ri1, rri1 = sm_rot.get()
                            P.op("dve", lambda e, o=ri1, i=a1[:, 128:129]: e.reciprocal(out=o, in_=i), r=[ra1], w=[rri1])
                            ts("dve", ri1, ri1, nlam, None, ALU.mult, None, r=[rri1, R_const], w=[rri1])
                            t1, rt1 = W_rot.get()
                            act(t1[:, 0:128], a1[:, 0:128], AF.Copy, r=[ra1, rri1], w=[rt1], scale=ri1)
                            stt(t1[:, 128:256], a0[:, 0:128], ri0, t1[:, 0:128], ALU.mult, ALU.add, r=[ra0, rri0, rt1], w=[rt1])
                            ss, rss = sm_rot.get()
                            act(t1[:, 256:384], t1[:, 128:256], AF.Square, r=[rt1], w=[rt1, rss], accum=ss)
                            rstd, rr = rstd_from_ss(ss, rss, 1.0 / 128)
                            xt, rx = xn_rot.get()
                            stt(xt[:, 0:128], t1[:, 128:256], rstd, gsub, ALU.mult, ALU.mult, r=[rt1, rr, R_const], w=[rx])
                            tr(ptr_bf[:, qs * 128:(qs + 1) * 128], xt[:, 0:128], ident_b, r=[rx, R_const], w=[rptr])
                        cp("act", big3[:, 16 + hd, :], ptr_bf[:, 0:512], r=[rptr], w=[R_big[16 + hd]])
                    stage(47)
                    for j in range(2):
                        g3, rg_, _ = wload(SL_GA + 2 * j)
                        o3, ro_, _ = wload(SL_AO + 2 * j)
                        for q in range(4):
                            db = 4 * j + q
                            pg, rpg, _ = ps_rot.get()
                            proj_fm(g3, rg_, q, 512, pg, rpg)
                            tg, rtg = W_rot.get()
                            act(tg[:, 0:512], pg[:, 0:512], AF.Tanh, r=[rpg], w=[rtg], scale=0.5)
                            py, rpy, _ = ps_rot.get()
                            for kc in range(8):
                                mm(py[:, :], o3[:, kc, q * 128:(q + 1) * 128], big3[:, 16 + kc, :], kc == 0, kc == 7, r=[ro_, R_big[16 + kc]], w=[rpy])
                            stt(tg[:, 0:512], tg[:, 0:512], 1.0, py[:, :], ALU.add, ALU.mult, r=[rtg, rpy], w=[rtg])
                            tt("pool", big3[:, 24 + db, :], tg[:, 0:512], big3[:, 24 + db, :], ALU.add, r=[rtg, R_big[24 + db]], w=[R_big[24 + db]])
                    stage(48)
                    for j in range(2):
                        s3, rs_, _ = wload(SL_O + j)
                        for tb in range(4):
                            pst, rps, _ = ps_rot.get()
                            for kc in range(8):
                                mm(pst[:, :], big3[:, 24 + kc, tb * 128:(tb + 1) * 128], s3[:, kc, :], kc == 0, kc == 7, r=[rs_, R_big[24 + kc]], w=[rps])
                            hv = h3[:, tb, j * 512:(j + 1) * 512]
                            stt(hv, pst[:, :], 0.5, hv, ALU.mult, ALU.add, r=[rps, R_h[tb]], w=[R_h[tb]])
                    stage(49)
                    norm_to_hnT([(h3[:, tb, :], R_h[tb]) for tb in range(4)], 128, g_mlp, 4)
                    for f in range(8):
                        s3, rs_, _ = wload(SL_F + f)
                        for q in range(4):
                            fc = 4 * f + q
                            pst, rps, _ = ps_rot.get()
                            proj_fm(s3, rs_, q, 512, pst, rps)
                            sq, rsq = W_rot.get()
                            act(sq[:, 0:512], pst[:, :], AF.Square, r=[rps], w=[rsq])
                            stt(big3[:, fc, :], pst[:, :], 0.0, sq[:, 0:512], ALU.is_gt, ALU.mult, r=[rps, rsq], w=[R_big[fc]])
                    for j in range(8):
                        _, rs_, flat = wload(SL_H + j)
                        s32 = flat.rearrange("p (f d) -> p f d", f=32)
                        pst, rps, _ = ps_rot.get()
                        for fc in range(32):
                            mm(pst[:, :], s32[:, fc, :], big3[:, fc, :], fc == 0, fc == 31, r=[rs_, R_big[fc]], w=[rps])
                        ot, rot_ = W_rot.get()
                        cp("act", ot[:, 0:512], pst[:, :], r=[rps], w=[rot_])
                        p2, rp2, _ = ps_rot.get()
                        for tb in range(4):
                            tr(p2[:, tb * 128:(tb + 1) * 128], ot[:, tb * 128:(tb + 1) * 128], identf[:, :], r=[rot_, R_const], w=[rp2])
                        hv = h3[:, :, j * 128:(j + 1) * 128]
                        tt("dve", hv, hv, p2[:, :].rearrange("p (t d) -> p t d", t=4), ALU.add, r=[rp2] + R_h, w=R_h)
                    stage(51)
                    for tb in range(4):
                        src = h3[:, tb, :]
                        xt, rx = xn_rot.get()
                        ss, rss = sm_rot.get()
                        act(xt[:, :], src, AF.Square, r=[R_h[tb]], w=[rx, rss], accum=ss)
                        rstd, rr = rstd_from_ss(ss, rss, 1.0 / D)
                        stt(src, src, rstd, gfin, ALU.mult, ALU.mult, r=[R_h[tb], rr, R_const], w=[R_h[tb]])
                        o = dma("pool", "y%d" % tb, y_d[b, r0 + tb * 128:r0 + (tb + 1) * 128, :], src, r=[R_h[tb]], w=[])
                        final_ops.append(o)

        except StopBuild:
            final_ops = [dma("pool", "y0", y_d[0, 0:128, :], hbuf[:, 0:1024], r=R_h, w=[])]
        P.emit(nc, final_ops[-4:])
    return nc


def host_consts(S):
    T = NMETA + S
    ident = np.eye(128, dtype=np.float32)
    kk = np.arange(128)
    tri = (kk[None, :] >= kk[:, None]).astype(np.float32)
    perm = np.zeros((128, 128), np.float32)
    perm[kk ^ 32, kk] = 1.0
    cmat = np.concatenate([ident, tri, perm], axis=1)
    inv = (1.0 / (np.float32(10000.0) ** (np.arange(0, 64, 2, dtype=np.float32) / np.float32(64)))).astype(np.float32)
    ang = (np.arange(T, dtype=np.float32)[:, None] * inv[None, :]).astype(np.float32)
    cos = np.cos(ang).astype(np.float32).T
    sin = np.sin(ang).astype(np.float32).T
    i = kk % 32
    sgn = np.where((kk % 64) < 32, -1.0, 1.0).astype(np.float32)
    rcos = np.ascontiguousarray(cos[i, :])
    rsin = np.ascontiguousarray(sin[i, :] * sgn[:, None])
    return cmat, rcos, rsin


def host_layout(inputs):
    f = lambda a: np.asarray(a, dtype=np.float32)
    fm = lambda v: np.ascontiguousarray(f(v).reshape(8, 128).T)
    pvec = np.zeros((128, 96), np.float32)
    pvec[:, 0:8] = fm(inputs["g_mix"][0])
    pvec[:, 8:16] = fm(inputs["g_mlp"][0])
    cw = f(inputs["conv_w"][0])
    for t in range(4):
        pvec[:, 16 + 8 * t:24 + 8 * t] = fm(cw[t])
    pvec[:, 48:56] = fm(inputs["conv_b"][0])
    pvec[:, 56:64] = fm(inputs["lru_lambda"][0])
    pvec[:, 64:72] = np.ascontiguousarray(f(inputs["b_a"][0]).T)
    pvec[:, 72:80] = np.ascontiguousarray(f(inputs["b_x"][0]).T)
    pbc = np.zeros((128, 1024 + 128 + 256), np.float32)
    pbc[:, 0:1024] = f(inputs["g_final"])[None, :]
    pbc[:, 1024:1152] = f(inputs["g_subln"][0])[None, :]
    pbc[:, 1152:1216] = f(inputs["lam_q1"][0])[None, :]
    pbc[:, 1216:1280] = f(inputs["lam_k1"][0])[None, :]
    pbc[:, 1280:1344] = f(inputs["lam_q2"][0])[None, :]
    pbc[:, 1344:1408] = f(inputs["lam_k2"][0])[None, :]
    wa = np.transpose(f(inputs["w_a"][0]), (1, 0, 2)).reshape(128, 1024)
    wx = np.transpose(f(inputs["w_x"][0]), (1, 0, 2)).reshape(128, 1024)
    wgate = np.ascontiguousarray(np.concatenate([wa, wx], axis=1))
    return pvec, pbc, wgate


_CACHE = {}


def run(inputs, S, NB, ncores):
    key = (S, NB)
    if key not in _CACHE:
        _CACHE[key] = build(S, NB)
    nc = _CACHE[key]
    cmat, rcos, rsin = host_consts(S)
    pvec, pbc, wgate = host_layout(inputs)
    f = lambda a: np.ascontiguousarray(np.asarray(a, dtype=np.float32))
    x = f(inputs["x"])
    common = {
        "meta": f(inputs["meta_tokens"]),
        "w_in": f(inputs["w_in"][0]), "w_rnn_out": f(inputs["w_rnn_out"][0]),
        "w_attn_out": f(inputs["w_attn_out"][0]), "w_o": f(inputs["w_o"][0]),
        "w_ff1": f(inputs["w_ff1"][0]), "w_ff2": f(inputs["w_ff2"][0]),
        "wgate": wgate, "pvec": pvec, "pbc": pbc, "cmat": cmat, "rcos": rcos, "rsin": rsin,
    }
    in_maps = []
    for c in range(ncores):
        m = dict(common)
        m["x"] = np.ascontiguousarray(x[c * NB:(c + 1) * NB])
        in_maps.append(m)
    res = run_bass_kernel_spmd(nc, in_maps, core_ids=list(range(ncores)))
    return np.concatenate([np.asarray(r["y"], dtype=np.float32) for r in res.results], axis=0)


def kernel(**inputs):
    return run(inputs, 4096, 2, 8)
```

```python
import contextlib
import numpy as np
import concourse.bass as bass
import concourse.mybir as mybir
from concourse.bass_utils import run_bass_kernel_spmd

F32 = mybir.dt.float32
BF16 = mybir.dt.bfloat16
AF = mybir.ActivationFunctionType
ALU = mybir.AluOpType
AX = mybir.AxisListType

D = 1024
NMETA = 16
EPS = 1e-6
LAM_INIT = 0.8 - 0.6 * 1.0
GC0 = 0.7978845608028654
GC1 = 0.044715
NSLOT = 36


class Reg:
    __slots__ = ("name", "w", "rs", "excl")

    def __init__(self, name, excl=False):
        self.name = name
        self.w = None
        self.rs = {}
        self.excl = excl


class Op:
    __slots__ = ("eng", "fn", "chan", "idx", "eidx", "deps", "sig", "tok", "stream")


class Prog:
    ENGS = ("pe", "act", "dve", "pool", "sp")

    def __init__(self):
        self.ops = []
        self.eng_ops = {e: [] for e in self.ENGS}
        self.chan_last = {}
        self.chan_cnt = {}

    def _add(self, eng, fn, r, w, chan=None):
        op = Op()
        op.eng = eng
        op.fn = fn
        op.chan = chan
        op.idx = len(self.ops)
        op.eidx = len(self.eng_ops[eng])
        op.sig = chan is not None
        op.tok = None
        op.stream = ("c", chan) if chan is not None else eng
        if any(R.excl for R in r):
            w = list(w) + [R for R in r if R.excl and R not in w]
            r = [R for R in r if not R.excl]
        deps = set()
        for R in r:
            if R.w is not None:
                deps.add(R.w)
        for R in w:
            if R.w is not None:
                deps.add(R.w)
            for x in R.rs.values():
                deps.add(x)
        if chan is not None and chan in self.chan_last:
            deps.add(self.chan_last[chan])
        deps.discard(op.idx)
        best = {}
        for d in deps:
            p = self.ops[d]
            if p.stream not in best or best[p.stream] < d:
                best[p.stream] = d
        final = []
        for st, d in best.items():
            p = self.ops[d]
            if chan is None and st == eng:
                if eng == "pe":
                    continue
            final.append(d)
            p.sig = True
        op.deps = final
        for R in r:
            R.rs[op.stream] = op.idx
        for R in w:
            R.w = op.idx
            R.rs = {}
        if chan is not None:
            self.chan_last[chan] = op.idx
            self.chan_cnt[chan] = self.chan_cnt.get(chan, 0) + 1
        self.ops.append(op)
        self.eng_ops[eng].append(op)
        return op

    def op(self, eng, fn, r=(), w=()):
        return self._add(eng, fn, r, w)

    def dma(self, eng, chan, fn, r=(), w=()):
        return self._add(eng, fn, r, w, chan=chan)

    def emit(self, nc, final_waits):
        with contextlib.ExitStack() as st:
            esem = {e: st.enter_context(nc.semaphore("s_" + e)) for e in ("pe", "act", "dve", "pool")}
            csem = {c: st.enter_context(nc.semaphore("c_" + c)) for c in self.chan_cnt}
            cnt = {e: 0 for e in esem}
            ccnt = {c: 0 for c in csem}
            for op in self.ops:
                if op.chan is not None:
                    ccnt[op.chan] += 16
                    op.tok = (csem[op.chan], ccnt[op.chan], "c_" + op.chan)
                elif op.sig:
                    cnt[op.eng] += 1
                    op.tok = (esem[op.eng], cnt[op.eng], "s_" + op.eng)
            ops = self.ops
            eng_ops = self.eng_ops

            def run(engname, e, extra=None):
                waited = {}
                for op in eng_ops[engname]:
                    for d in op.deps:
                        sem, val, key = ops[d].tok
                        if waited.get(key, 0) >= val:
                            continue
                        e.wait_ge(sem, val)
                        waited[key] = val
                    ins = op.fn(e)
                    if op.chan is not None:
                        ins.then_inc(op.tok[0], 16)
                    elif op.sig:
                        ins.then_inc(op.tok[0], 1)
                if extra:
                    for o in extra:
                        sem, val, key = o.tok
                        if waited.get(key, 0) >= val:
                            continue
                        e.wait_ge(sem, val)
                        waited[key] = val

            with nc.Block() as block:
                @block.tensor
                def _(e):
                    run("pe", e)

                @block.scalar
                def _(e):
                    run("act", e)

                @block.vector
                def _(e):
                    run("dve", e)

                @block.gpsimd
                def _(e):
                    run("pool", e)

                @block.sync
                def _(e):
                    run("sp", e, extra=final_waits)


class Rot:
    def __init__(self, items):
        self.items = items
        self.i = 0

    def get(self):
        it = self.items[self.i % len(self.items)]
        self.i += 1
        return it


class StopBuild(Exception):
    pass


def build(S, NB, dbg=0):
    NCH = S // 512
    NKB = S // 128
    T = NMETA + S
    nc = bass.Bass("TRN2", target_bir_lowering=False)
    P = Prog()

    def din(name, shape, dt=F32):
        return nc.dram_tensor(name, list(shape), dt, kind="ExternalInput").ap()

    x_d = din("x", [NB, S, D])
    meta_d = din("meta", [NMETA, D])
    w_in_d = din("w_in", [D, 7 * D])
    w_rnn_d = din("w_rnn_out", [D, D])
    w_att_d = din("w_attn_out", [D, D])
    w_o_d = din("w_o", [D, D])
    w_ff1_d = din("w_ff1", [D, 4 * D])
    w_ff2_d = din("w_ff2", [4 * D, D])
    wg_d = din("wgate", [128, 2 * 8 * 128])
    pv_d = din("pvec", [128, 96])
    pb_d = din("pbc", [128, 1024 + 128 + 256])
    cm_d = din("cmat", [128, 3 * 128])
    cos_d = din("rcos", [128, T])
    sin_d = din("rsin", [128, T])
    y_d = nc.dram_tensor("y", [NB, S, D], F32, kind="ExternalOutput").ap()
    wbf_d = nc.dram_tensor("wbf", [NSLOT, 128, 4096], BF16, kind="Internal").ap()
    kc_d = nc.dram_tensor("kcache", [8, 128, S], BF16, kind="Internal").ap()

    with contextlib.ExitStack() as es:
        def sb(name, cols, dt):
            return es.enter_context(nc.sbuf_tensor(name, [128, cols], dt))

        vc = sb("vc", NKB * 8 * 129, BF16)
        vmeta = sb("vmeta", 8 * 129, BF16)
        kmeta = sb("kmeta", 8 * 32, BF16)
        wg = sb("wg", 2 * 8 * 128, BF16)
        cb = sb("cb", 3 * 128, BF16)
        identf = sb("identf", 128, F32)
        pv = sb("pv", 96, F32)
        pd = sb("pd", 64, F32)
        pbt = sb("pbt", 1024 + 128 + 256, F32)
        kb = sb("kb", 2 * S, BF16)
        hbuf = sb("hbuf", 4 * 1024, F32)
        hnT = sb("hnT", 8 * 512, BF16)
        big = sb("big", 32 * 512, BF16)
        wr = sb("wr", 3 * 4096, BF16)
        WT = 520
        Wt = sb("Wt", 8 * WT, F32)
        rc = sb("rc", 2 * 512, F32)
        et = sb("et", 2 * 1024, BF16)
        sm = sb("sm", 64, F32)
        xn = sb("xn", 2 * 1024, BF16)
        kth = sb("kth", 2 * 512, BF16)
        xcbt = sb("xcbt", 2 * 512, BF16)
        state = sb("state", 8 + 24 + 8 + 24, F32)
        psb = [es.enter_context(nc.psum_tensor("ps%d" % i, [128, 512], F32)) for i in range(8)]

        R_ps = [Reg("ps%d" % i, excl=True) for i in range(8)]
        R_h = [Reg("h%d" % i) for i in range(4)]
        R_hnT = [Reg("hnT%d" % i) for i in range(8)]
        R_big = [Reg("big%d" % i) for i in range(32)]
        R_wr = [Reg("wr%d" % i) for i in range(3)]
        R_W = [Reg("W%d" % i) for i in range(8)]
        R_rc = Reg("rc")
        R_et = [Reg("et0"), Reg("et1")]
        R_xn = [Reg("xn0"), Reg("xn1")]
        R_kth = [Reg("kth0"), Reg("kth1")]
        R_xcb = [Reg("xcb0"), Reg("xcb1")]
        R_kb = [Reg("kb0"), Reg("kb1")]
        R_kd = [Reg("kd%d" % i) for i in range(8)]
        R_wbf = [Reg("wbf%d" % i) for i in range(NSLOT)]
        R_vc = [Reg("vc%d" % i) for i in range(NKB)]
        R_vmeta = Reg("vmeta")
        R_kmeta = Reg("kmeta")
        R_const = Reg("const")
        R_state = [Reg("st%d" % i) for i in range(8)]
        R_state0 = Reg("state0")
        R_sm = [Reg("sm%d" % i) for i in range(64)]

        h3 = hbuf[:, :].rearrange("p (t d) -> p t d", t=4)
        hnT3 = hnT[:, :].rearrange("p (k t) -> p k t", k=8)
        big3 = big[:, :].rearrange("p (i t) -> p i t", i=32)
        wr3 = [wr[:, s * 4096:(s + 1) * 4096] for s in range(3)]
        Wv = [Wt[:, i * WT:(i + 1) * WT] for i in range(8)]
        vc4 = vc[:, :].rearrange("p (b h e) -> p b h e", b=NKB, h=8)
        vmeta3 = vmeta[:, :].rearrange("p (h e) -> p h e", h=8)
        kmeta3 = kmeta[:, :].rearrange("p (h t) -> p h t", h=8)
        wg4 = wg[:, :].rearrange("p (w n d) -> p w n d", w=2, n=8)
        ident_b = cb[:, 0:128]
        tri_b = cb[:, 128:256]
        perm_b = cb[:, 256:384]
        gfin = pbt[:, 0:1024]
        gsub = pbt[:, 1024:1152]
        g_mix = pv[:, 0:8]
        g_mlp = pv[:, 8:16]
        conv_w = pv[:, 16:48].rearrange("p (t n) -> p t n", t=4)
        conv_b = pv[:, 48:56]
        hcl = pd[:, 0:8]
        hba = pd[:, 8:16]
        hbx = pd[:, 16:24]
        nlam = pd[:, 24:25]
        negh = pd[:, 25:26]
        halfc = pd[:, 26:27]
        hstate = state[:, 0:8]
        carry = state[:, 8:32].rearrange("p (n k) -> p n k", n=8)
        hstate0 = state[:, 32:40]
        carry0 = state[:, 40:64].rearrange("p (n k) -> p n k", n=8)
        ps_bf = [p[:, :].bitcast(BF16) for p in psb]

        sm_rot = Rot([(sm[:, i:i + 1], R_sm[i]) for i in range(64)])
        W_rot = Rot([(Wv[i], R_W[i]) for i in range(8)])
        ps_rot = Rot([(psb[i], R_ps[i], ps_bf[i]) for i in range(8)])
        xn_rot = Rot([(xn[:, i * 1024:(i + 1) * 1024], R_xn[i]) for i in range(2)])
        kth_rot = Rot([(kth[:, i * 512:(i + 1) * 512], R_kth[i]) for i in range(2)])

        def act(out, in_, func, r, w, bias=None, scale=None, accum=None):
            kw = {}
            if bias is not None:
                kw["bias"] = bias
            if scale is not None:
                kw["scale"] = scale
            if accum is not None:
                kw["accum_out"] = accum
            return P.op("act", lambda e: e.activation(out=out, in_=in_, func=func, **kw), r=r, w=w)

        def ts(eng, out, in0, s1, s2, op0, op1, r, w):
            if op1 is None:
                return P.op(eng, lambda e: e.tensor_scalar(out=out, in0=in0, scalar1=s1, scalar2=None, op0=op0), r=r, w=w)
            return P.op(eng, lambda e: e.tensor_scalar(out=out, in0=in0, scalar1=s1, scalar2=s2, op0=op0, op1=op1), r=r, w=w)

        def stt(out, in0, scalar, in1, op0, op1, r, w):
            return P.op("dve", lambda e: e.scalar_tensor_tensor(out=out, in0=in0, scalar=scalar, in1=in1, op0=op0, op1=op1), r=r, w=w)

        def tt(eng, out, in0, in1, op, r, w):
            return P.op(eng, lambda e: e.tensor_tensor(out=out, in0=in0, in1=in1, op=op), r=r, w=w)

        def cp(eng, out, in_, r, w):
            if eng == "act":
                return P.op("act", lambda e: e.copy(out=out, in_=in_), r=r, w=w)
            return P.op(eng, lambda e: e.tensor_copy(out=out, in_=in_), r=r, w=w)

        def mm(out, lhsT, rhs, start, stop, r, w, skip=False):
            return P.op("pe", lambda e: e.matmul(out, lhsT, rhs, start=start, stop=stop, skip_group_check=skip), r=r, w=w)

        def tr(out, in_, ident, r, w):
            return P.op("pe", lambda e: e.transpose(out, in_, ident), r=r, w=w)

        def dma(eng, chan, out, in_, r, w, slow=False):
            if slow:
                return P.dma(eng, chan, lambda e: e.dma_start(out=out, in_=in_, allow_slow_non_contiguous=True), r=r, w=w)
            return P.dma(eng, chan, lambda e: e.dma_start(out=out, in_=in_), r=r, w=w)

        def rstd_from_ss(ss_ap, ss_reg, inv_n, npart=128):
            v, rv = sm_rot.get()
            ts("pool", v[:npart, :], ss_ap, inv_n, EPS, ALU.mult, ALU.add, r=[ss_reg], w=[rv])
            o, ro = sm_rot.get()
            tt("pool", o[:npart, :], v[:npart, :], negh[:npart, :], ALU.pow, r=[rv, R_const], w=[ro])
            return o, ro

        final_ops = []
        def stage(k):
            if dbg == k:
                raise StopBuild()
        try:
            st_f = Wv[0][:, 0:384]
            dma("pool", "su0", st_f, cm_d, r=[], w=[R_W[0]])
            cp("dve", cb[:, :], st_f, r=[R_W[0]], w=[R_const])
            dma("pool", "su1", identf[:, :], cm_d[:, 0:128], r=[], w=[R_const])
            dma("pool", "su0", pv[:, :], pv_d, r=[], w=[R_const])
            dma("pool", "su1", pbt[:, :], pb_d, r=[], w=[R_const])
            for half in range(2):
                for q in range(2):
                    wt_, rw_ = W_rot.get()
                    dma("pool", "su%d" % q, wt_[:, 0:512], wg_d[:, half * 1024 + q * 512: half * 1024 + (q + 1) * 512], r=[], w=[rw_])
                    cp("dve", wg[:, half * 1024 + q * 512: half * 1024 + (q + 1) * 512], wt_[:, 0:512], r=[rw_], w=[R_const])
            P.op("dve", lambda e: e.memset(negh, -0.5), r=[], w=[R_const])
            P.op("dve", lambda e: e.memset(halfc, 0.5), r=[], w=[R_const])
            lam_raw = pv[:, 56:64]
            tmp8 = pd[:, 32:40]
            act(tmp8, lam_raw, AF.Exp, r=[R_const], w=[R_const], scale=-1.0)
            act(tmp8, tmp8, AF.Ln, r=[R_const], w=[R_const], bias=1.0)
            ts("dve", hcl, tmp8, -4.0, None, ALU.mult, None, r=[R_const], w=[R_const])
            ts("dve", hba, pv[:, 64:72], 0.5, None, ALU.mult, None, r=[R_const], w=[R_const])
            ts("dve", hbx, pv[:, 72:80], 0.5, None, ALU.mult, None, r=[R_const], w=[R_const])
            lq = pbt[:, 1152:1408]
            prod = pd[:, 40:42]
            tl = Wv[1][:, 0:128]
            tt("dve", tl[:, 0:64], lq[:, 0:64], lq[:, 64:128], ALU.mult, r=[R_const], w=[R_W[1]])
            tt("dve", tl[:, 64:128], lq[:, 128:192], lq[:, 192:256], ALU.mult, r=[R_const], w=[R_W[1]])
            P.op("dve", lambda e: e.tensor_reduce(out=prod, in_=tl.rearrange("p (a b) -> p a b", a=2), axis=AX.X, op=ALU.add), r=[R_W[1]], w=[R_const])
            act(prod, prod, AF.Exp, r=[R_const], w=[R_const])
            tt("dve", nlam, prod[:, 1:2], prod[:, 0:1], ALU.subtract, r=[R_const], w=[R_const])
            ts("dve", nlam, nlam, -LAM_INIT, None, ALU.add, None, r=[R_const], w=[R_const])
            ts("dve", gsub, gsub, 1.0 - LAM_INIT, None, ALU.mult, None, r=[R_const], w=[R_const])
            P.op("pool", lambda e: e.memset(vc4[:, :, :, 128:129], 1.0), r=[], w=R_vc)

            stage(1)
            w_in3 = w_in_d.rearrange("(k p) c -> p k c", p=128)
            w_rnn3 = w_rnn_d.rearrange("(k p) c -> p k c", p=128)
            w_att3 = w_att_d.rearrange("(k p) c -> p k c", p=128)
            w_o3 = w_o_d.rearrange("(k p) c -> p k c", p=128)
            w_ff13 = w_ff1_d.rearrange("(k p) c -> p k c", p=128)
            w_ff23 = w_ff2_d.rearrange("(k p) c -> p k c", p=128)

            def win(c0, n=512):
                return [(w_in3[:, :, c0:c0 + n], 8, n)]

            slot_src = []
            for j in range(4):
                slot_src.append([(w_in3[:, :, 256 * j:256 * j + 256], 8, 256), (w_in3[:, :, 1024 + 256 * j:1024 + 256 * j + 256], 8, 256)])
            for j in range(2):
                slot_src.append(win(5120 + 512 * j))
                slot_src.append([(w_rnn3[:, :, 512 * j:512 * j + 512], 8, 512)])
            for j in range(2):
                slot_src.append(win(2048 + 512 * j))
            for j in range(2):
                slot_src.append(win(3072 + 512 * j))
            for j in range(2):
                slot_src.append(win(4096 + 512 * j))
            for j in range(2):
                slot_src.append(win(6144 + 512 * j))
                slot_src.append([(w_att3[:, :, 512 * j:512 * j + 512], 8, 512)])
            for j in range(2):
                slot_src.append([(w_o3[:, :, 512 * j:512 * j + 512], 8, 512)])
            for j in range(8):
                slot_src.append([(w_ff13[:, :, 512 * j:512 * j + 512], 8, 512)])
            for j in range(8):
                slot_src.append([(w_ff23[:, 8 * q:8 * q + 8, 128 * j:128 * j + 128], 8, 128) for q in range(4)])
            assert len(slot_src) == NSLOT
            SL_A, SL_G, SL_R, SL_Q, SL_K, SL_V, SL_GA, SL_AO, SL_O, SL_F, SL_H = 0, 4, 5, 8, 10, 12, 14, 15, 18, 20, 28

            stg = [(hbuf[:, :], R_h), (Wt[:, 0:4096], R_W)]
            cast_eng = ["act", "dve", "pool"]
            for s in range(NSLOT):
                sbuf_f, regs_f = stg[s % 2]
                off = 0
                if s < 4:
                    st3 = sbuf_f[:, 0:4096].rearrange("p (a b) -> p a b", a=8)
                    for i_, (src, a, b) in enumerate(slot_src[s]):
                        dma("sp", "w%d" % (s % 2), st3[:, :, 256 * i_:256 * i_ + 256], src, r=[], w=regs_f)
                else:
                    for (src, a, b) in slot_src[s]:
                        dst = sbuf_f[:, off:off + a * b].rearrange("p (a b) -> p a b", a=a)
                        dma("sp", "w%d" % (s % 2), dst, src, r=[], w=regs_f)
                        off += a * b
                    assert off == 4096
                ring = s % 3
                for q in range(3):
                    lo, hi = [0, 1408, 2752][q], [1408, 2752, 4096][q]
                    cp(cast_eng[q], wr3[ring][:, lo:hi], sbuf_f[:, lo:hi], r=regs_f, w=[R_wr[ring]])
                dma("pool", "su%d" % (s % 2), wbf_d[s], wr3[ring], r=[R_wr[ring]], w=[R_wbf[s]])

            stage(2)
            wcount = [0]

            def wload(slot):
                ring = wcount[0] % 3
                wcount[0] += 1
                dma("sp", "w%d" % ring, wr3[ring], wbf_d[slot], r=[R_wbf[slot]], w=[R_wr[ring]])
                return wr3[ring].rearrange("p (k c) -> p k c", k=8), R_wr[ring], wr3[ring]

            def norm_to_hnT(src_tiles, ntok, gvec, nblk):
                for tb in range(nblk):
                    src, rsrc = src_tiles[tb]
                    xt, rx = xn_rot.get()
                    ss, rss = sm_rot.get()
                    act(xt[:ntok, :], src, AF.Square, r=[rsrc], w=[rx, rss], accum=ss[:ntok, :])
                    rstd, rr = rstd_from_ss(ss[:ntok, :], rss, 1.0 / D, npart=ntok)
                    ts("dve", xt[:ntok, :], src, rstd[:ntok, :], None, ALU.mult, None, r=[rsrc, rr], w=[rx])
                    pst, rps, psbf = ps_rot.get()
                    for kc in range(8):
                        tr(psbf[:, kc * ntok:(kc + 1) * ntok], xt[:ntok, kc * 128:(kc + 1) * 128], ident_b[:ntok, :ntok], r=[rx, R_const], w=[rps])
                    outv = hnT3[:, :, tb * 128:tb * 128 + ntok]
                    inv = psbf[:, 0:8 * ntok].rearrange("p (k t) -> p k t", k=8)
                    gb = gvec.unsqueeze(2).to_broadcast([128, 8, ntok])
                    tt("dve", outv, inv, gb, ALU.mult, r=[rps, R_const], w=R_hnT)

            def proj_fm(slot3, rslot, cb_, ntok, pst, rps):
                for kc in range(8):
                    mm(pst[:, 0:ntok], slot3[:, kc, cb_ * 128:(cb_ + 1) * 128], hnT3[:, kc, 0:ntok], kc == 0, kc == 7, r=[rslot, R_hnT[kc]], w=[rps])

            def rnn_gen(n, ntok, xr_slot, xr_cb, gr_slot, gr_cb, rx_slot, rg_slot, with_y, lane):
                (T1, r1), (T2, r2), (T3, r3), (T4, r4) = lane["W"]
                xcb, rxcb = lane["xcb"]
                psr = lane["ps"]
                pst, rps, _ = psr.get()
                proj_fm(xr_slot, rx_slot, xr_cb, ntok, pst, rps)
                yield
                xr, rxr = T1, r1
                cp("pool", xr[:, 0:3], carry[:, n, :], r=[R_state[n]], w=[rxr])
                cp("act", xr[:, 3:3 + ntok], pst[:, 0:ntok], r=[rps], w=[rxr])
                yield
                xc, rxc = T2, r2
                ts("dve", xc[:, 0:ntok], xr[:, 3:3 + ntok], conv_w[:, 3, n:n + 1], conv_b[:, n:n + 1], ALU.mult, ALU.add, r=[rxr, R_const], w=[rxc])
                for k in range(3):
                    stt(xc[:, 0:ntok], xr[:, k:k + ntok], conv_w[:, k, n:n + 1], xc[:, 0:ntok], ALU.mult, ALU.add, r=[rxr, rxc, R_const], w=[rxc])
                cp("pool", carry[:, n, :], xr[:, ntok:ntok + 3], r=[rxr], w=[R_state[n]])
                cp("pool", xcb[:, 0:ntok], xc[:, 0:ntok], r=[rxc], w=[rxcb])
                yield
                pa, rpa, _ = psr.get()
                mm(pa[:, 0:ntok], wg4[:, 0, n, :], xcb[:, 0:ntok], True, True, r=[rxcb, R_const], w=[rpa])
                pi, rpi, _ = psr.get()
                mm(pi[:, 0:ntok], wg4[:, 1, n, :], xcb[:, 0:ntok], True, True, r=[rxcb, R_const], w=[rpi])
                yield
                ta, rta = T1, r1
                ti, rti = T3, r3
                act(ta[:, 0:ntok], pa[:, 0:ntok], AF.Tanh, r=[rpa, R_const], w=[rta], bias=hba[:, n:n + 1], scale=0.5)
                act(ti[:, 0:ntok], pi[:, 0:ntok], AF.Tanh, r=[rpi, R_const], w=[rti], bias=hbx[:, n:n + 1], scale=0.5)
                act(ta[:, 0:ntok], ta[:, 0:ntok], AF.Exp, r=[rta, R_const], w=[rta], bias=hcl[:, n:n + 1], scale=hcl[:, n:n + 1])
                yield
                sq, rsq = T4, r4
                tt("pool", sq[:, 0:ntok], ta[:, 0:ntok], ta[:, 0:ntok], ALU.mult, r=[rta], w=[rsq])
                act(sq[:, 0:ntok], sq[:, 0:ntok], AF.Sqrt, r=[rsq], w=[rsq], bias=1.0, scale=-1.0)
                stt(ti[:, 0:ntok], ti[:, 0:ntok], 1.0, xc[:, 0:ntok], ALU.add, ALU.mult, r=[rti, rxc], w=[rti])
                yield
                stt(ti[:, 0:ntok], ti[:, 0:ntok], 0.5, sq[:, 0:ntok], ALU.mult, ALU.mult, r=[rti, rsq], w=[rti])
                hh, rhh = T2, r2
                P.op("dve", lambda e: e.tensor_tensor_scan(out=hh[:, 0:ntok], data0=ta[:, 0:ntok], data1=ti[:, 0:ntok], initial=hstate[:, n:n + 1], op0=ALU.mult, op1=ALU.add),
                     r=[rta, rti, R_state[n]], w=[rhh])
                cp("pool", hstate[:, n:n + 1], hh[:, ntok - 1:ntok], r=[rhh], w=[R_state[n]])
                yield
                if not with_y:
                    return
                pg, rpg, _ = psr.get()
                proj_fm(gr_slot, rg_slot, gr_cb, ntok, pg, rpg)
                yield
                g2, rg2 = T4, r4
                act(g2[:, 0:ntok], pg[:, 0:ntok], AF.Square, r=[rpg], w=[rg2])
                ts("dve", g2[:, 0:ntok], g2[:, 0:ntok], GC1 * GC0, GC0, ALU.mult, ALU.add, r=[rg2], w=[rg2])
                tt("dve", g2[:, 0:ntok], g2[:, 0:ntok], pg[:, 0:ntok], ALU.mult, r=[rg2, rpg], w=[rg2])
                yield
                act(g2[:, 0:ntok], g2[:, 0:ntok], AF.Tanh, r=[rg2], w=[rg2])
                stt(g2[:, 0:ntok], g2[:, 0:ntok], 1.0, pg[:, 0:ntok], ALU.add, ALU.mult, r=[rg2, rpg], w=[rg2])
                stt(big3[:, 8 + n, :], g2[:, 0:ntok], 0.5, hh[:, 0:ntok], ALU.mult, ALU.mult, r=[rg2, rhh], w=[R_big[8 + n]])
                yield

            def lockstep(*gens):
                gens = list(gens)
                while gens:
                    for g in list(gens):
                        try:
                            next(g)
                        except StopIteration:
                            gens.remove(g)
                    yield

            def run_all(*gens):
                for _ in lockstep(*gens):
                    pass

            ring_owner = [None, None, None]

            def wload_g(slot, sid):
                for r_ in range(3):
                    if ring_owner[r_] == sid:
                        ring_owner[r_] = None
                while ring_owner[wcount[0] % 3] is not None:
                    yield
                ring = wcount[0] % 3
                res = wload(slot)
                ring_owner[ring] = sid
                return res

            def release(sid):
                for r_ in range(3):
                    if ring_owner[r_] == sid:
                        ring_owner[r_] = None

            def mk_lane(i):
                return {"W": [(Wv[4 * i + k], R_W[4 * i + k]) for k in range(4)],
                        "xcb": (xcbt[:, i * 512:(i + 1) * 512], R_xcb[i]),
                        "ps": Rot([(psb[3 * i + k], R_ps[3 * i + k], ps_bf[3 * i + k]) for k in range(3)])}
            lanes = [mk_lane(0), mk_lane(1)]
            et_f = et[:, :].bitcast(F32)
            qkv_res = {"W": Rot([(et_f[:, i * 512:(i + 1) * 512], R_et[i]) for i in range(2)]),
                       "ps": Rot([(psb[6 + k], R_ps[6 + k], ps_bf[6 + k]) for k in range(2)])}
            dflt_res = {"W": W_rot, "ps": ps_rot}

            def rope_block(pst, rps, ntok, dst, rdst, res=None):
                res = res or dflt_res
                qb, rqb = xn_rot.get()
                cp("act", qb[:, 0:ntok], pst[:, 0:ntok], r=[rps], w=[rqb])
                p2, rp2, _ = res["ps"].get()
                mm(p2[:, 0:ntok], perm_b, qb[:, 0:ntok], True, True, r=[rqb, R_const], w=[rp2])
                t1, rt1 = res["W"].get()
                tt("dve", t1[:, 0:ntok], rc[:, 0:ntok], pst[:, 0:ntok], ALU.mult, r=[rps, rqb, R_rc], w=[rt1])
                t2, rt2 = res["W"].get()
                tt("dve", t2[:, 0:ntok], rc[:, 512:512 + ntok], p2[:, 0:ntok], ALU.mult, r=[rp2, R_rc], w=[rt2])
                tt("pool", dst, t1[:, 0:ntok], t2[:, 0:ntok], ALU.add, r=[rt1, rt2], w=rdst)

            P.op("pool", lambda e: e.memset(hbuf[:32, 0:1024], 0.0), r=[], w=[R_h[0]])
            P.op("pool", lambda e: e.memset(kmeta[:, :], 0.0), r=[], w=[R_kmeta])
            P.op("pool", lambda e: e.memset(vmeta[:32, :], 0.0), r=[], w=[R_vmeta])
            P.op("pool", lambda e: e.memset(vmeta3[:NMETA, :, 128:129], 1.0), r=[], w=[R_vmeta])
            dma("pool", "x0", hbuf[:NMETA, 0:1024], meta_d, r=[], w=[R_h[0]])
            dma("pool", "rc", rc[:, 0:NMETA], cos_d[:, 0:NMETA], r=[], w=[R_rc])
            dma("pool", "rc", rc[:, 512:512 + NMETA], sin_d[:, 0:NMETA], r=[], w=[R_rc])
            P.op("pool", lambda e: e.memset(state[:, :], 0.0), r=[], w=R_state)
            norm_to_hnT([(hbuf[:32, 0:1024], R_h[0])], 32, g_mix, 1)
            stage(31)
            for j in range(4):
                s3, rs_, _ = wload(SL_A + j)
                for q in range(2):
                    run_all(rnn_gen(2 * j + q, NMETA, s3, q, None, None, rs_, None, False, lanes[q]))
            stage(33)
            cp("pool", state[:, 32:64], state[:, 0:32], r=R_state, w=[R_state0])
            for j in range(2):
                s3, rs_, _ = wload(SL_K + j)
                for q in range(4):
                    hd = 4 * j + q
                    pst, rps, _ = ps_rot.get()
                    proj_fm(s3, rs_, q, NMETA, pst, rps)
                    rope_block(pst, rps, NMETA, kmeta3[:, hd, 0:NMETA], [R_kmeta])
            stage(34)
            for j in range(2):
                s3, rs_, _ = wload(SL_V + j)
                pst, rps, _ = ps_rot.get()
                for kc in range(8):
                    mm(pst[:32, :], hnT3[:, kc, 0:32], s3[:, kc, :], kc == 0, kc == 7, r=[rs_, R_hnT[kc]], w=[rps])
                cp("act", vmeta3[:NMETA, 4 * j:4 * j + 4, 0:128], pst[:NMETA, :].rearrange("p (h e) -> p h e", h=4), r=[rps], w=[R_vmeta])

            stage(3)
            def oacc(c, qs):
                i = c * 4 + qs
                bank = 5 + i // 3
                off = (i % 3) * 129
                return psb[bank][:, off:off + 129], R_ps[bank], (i % 3 == 0)

            for b in range(NB):
                cp("pool", state[:, 0:32], state[:, 32:64], r=[R_state0], w=R_state)
                for ci in range(NCH):
                    r0 = 512 * ci
                    p0 = NMETA + r0
                    for tb in range(4):
                        dma("pool", "x%d" % tb, h3[:, tb, :], x_d[b, r0 + tb * 128:r0 + (tb + 1) * 128, :], r=[], w=[R_h[tb]])
                    dma("pool", "rc", rc[:, 0:512], cos_d[:, p0:p0 + 512], r=[], w=[R_rc])
                    dma("pool", "rc", rc[:, 512:1024], sin_d[:, p0:p0 + 512], r=[], w=[R_rc])
                    norm_to_hnT([(h3[:, tb, :], R_h[tb]) for tb in range(4)], 128, g_mix, 4)
                    stage(40)
                    def rnn_stream(sid):
                        for j in range(4):
                            s3, rs_, _ = yield from wload_g(SL_A + j, sid)
                            yield from lockstep(rnn_gen(2 * j, 512, s3, 0, s3, 2, rs_, rs_, True, lanes[0]),
                                                rnn_gen(2 * j + 1, 512, s3, 1, s3, 3, rs_, rs_, True, lanes[1]))
                        release(sid)

                    def qkv_stream(sid):
                        for j in range(2):
                            s3, rs_, _ = yield from wload_g(SL_Q + j, sid)
                            for q in range(4):
                                hd = 4 * j + q
                                pst, rps, _ = qkv_res["ps"].get()
                                proj_fm(s3, rs_, q, 512, pst, rps)
                                yield
                                rope_block(pst, rps, 512, big3[:, hd, :], [R_big[hd]], qkv_res)
                                yield
                        for j in range(2):
                            s3, rs_, _ = yield from wload_g(SL_K + j, sid)
                            for q in range(4):
                                hd = 4 * j + q
                                pst, rps, _ = qkv_res["ps"].get()
                                proj_fm(s3, rs_, q, 512, pst, rps)
                                yield
                                kt, rkt = kth_rot.get()
                                rope_block(pst, rps, 512, kt, [rkt], qkv_res)
                                dma("sp", "kw%d" % (hd % 2), kc_d[hd, :, r0:r0 + 512], kt, r=[rkt], w=[R_kd[hd]])
                                yield
                        for j in range(2):
                            s3, rs_, _ = yield from wload_g(SL_V + j, sid)
                            for tb in range(4):
                                pst, rps, _ = qkv_res["ps"].get()
                                for kc in range(8):
                                    mm(pst[:, :], hnT3[:, kc, tb * 128:(tb + 1) * 128], s3[:, kc, :], kc == 0, kc == 7, r=[rs_, R_hnT[kc]], w=[rps])
                                blk = 4 * ci + tb
                                cp("act" if tb % 2 == 0 else "dve", vc4[:, blk, 4 * j:4 * j + 4, 0:128], pst[:, :].rearrange("p (h e) -> p h e", h=4), r=[rps], w=[R_vc[blk]])
                                yield
                        release(sid)

                    run_all(rnn_stream(1), qkv_stream(2))
                    stage(41)
                    for j in range(2):
                        g3, rg_, _ = wload(SL_G + 2 * j)
                        o3, ro_, _ = wload(SL_R + 2 * j)
                        for q in range(4):
                            db = 4 * j + q
                            pg, rpg, _ = ps_rot.get()
                            proj_fm(g3, rg_, q, 512, pg, rpg)
                            tg, rtg = W_rot.get()
                            act(tg[:, 0:512], pg[:, 0:512], AF.Tanh, r=[rpg], w=[rtg], scale=0.5)
                            py, rpy, _ = ps_rot.get()
                            for kc in range(8):
                                mm(py[:, :], o3[:, kc, q * 128:(q + 1) * 128], big3[:, 8 + kc, :], kc == 0, kc == 7, r=[ro_, R_big[8 + kc]], w=[rpy])
                            stt(big3[:, 24 + db, :], tg[:, 0:512], 1.0, py[:, :], ALU.add, ALU.mult, r=[rtg, rpy], w=[R_big[24 + db]])
                    stage(42)
                    stage(45)
                    nreal = 4 * ci + 4
                    kend = r0 + 512
                    for hd in range(8):
                        ks = hd % 2
                        kbv = kb[:, ks * S:ks * S + kend]
                        dma("sp", "k%d" % ks, kbv, kc_d[hd, :, 0:kend], r=[R_kd[hd]], w=[R_kb[ks]])
                        qT = big3[:, hd, :]
                        rq = R_big[hd]
                        def blk_params(kbk):
                            if kbk < 0:
                                kk, q_lo, kr = 32, 0, [R_kmeta]
                            else:
                                jj = kbk - 4 * ci
                                q_lo = 128 * jj if jj > 0 else 0
                                kk, kr = 128, [R_kb[ks]]
                            eslot = (kbk + 1) % 2
                            ev = et[:, eslot * 1024:(eslot + 1) * 1024].rearrange("p (c t) -> p c t", c=2)
                            return kk, q_lo, kr, eslot, ev

                        def qk_stage(kbk):
                            kk, q_lo, kr, eslot, ev = blk_params(kbk)
                            nq = 512 - q_lo
                            for c in range(2):
                                sbank = 2 * eslot + c
                                if kbk < 0:
                                    lhs = kmeta3[c * 64:(c + 1) * 64, hd, :]
                                else:
                                    lhs = kbv[c * 64:(c + 1) * 64, kbk * 128:(kbk + 1) * 128]
                                mm(psb[sbank][:kk, 0:nq], lhs, qT[c * 64:(c + 1) * 64, q_lo:512], True, True, r=kr + [rq], w=[R_ps[sbank]])
                            for c in range(2):
                                sbank = 2 * eslot + c
                                act(ev[:kk, c, q_lo:512], psb[sbank][:kk, 0:nq], AF.Exp, r=[R_ps[sbank]], w=[R_et[eslot]], scale=0.125)
                            if kbk >= 0 and kbk >= 4 * ci:
                                jj = kbk - 4 * ci
                                for c in range(2):
                                    tt("pool", ev[:, c, jj * 128:(jj + 1) * 128], ev[:, c, jj * 128:(jj + 1) * 128], tri_b, ALU.mult, r=[R_et[eslot], R_const], w=[R_et[eslot]])

                        def pv_stage(kbk):
                            kk, q_lo, kr, eslot, ev = blk_params(kbk)
                            if kbk >= 0 and kbk >= 4 * ci:
                                qs_list = list(range(kbk - 4 * ci, 4))
                            else:
                                qs_list = [0, 1, 2, 3]
                            for c in range(2):
                                for qs in qs_list:
                                    oa, ro, first_in_bank = oacc(c, qs)
                                    if kbk < 0:
                                        rhs = vmeta3[:32, hd, :]
                                        rv = R_vmeta
                                    else:
                                        rhs = vc4[:, kbk, hd, :]
                                        rv = R_vc[kbk]
                                    last = (kbk == 4 * ci + qs)
                                    mm(oa, ev[:kk, c, qs * 128:(qs + 1) * 128], rhs, (kbk < 0) and first_in_bank, last, r=[R_et[eslot], rv], w=[ro], skip=True)

                        blocks = list(range(-1, nreal))
                        qk_stage(blocks[0])
                        for bi_, kbk in enumerate(blocks):
                            if bi_ + 1 < len(blocks):
                                qk_stage(blocks[bi_ + 1])
                            pv_stage(kbk)
                        stage(46)
                        ptr, rptr, ptr_bf = ps_rot.get()
                        while rptr in (R_ps[5], R_ps[6], R_ps[7]):
                            ptr, rptr, ptr_bf = ps_rot.get()
                        for qs in range(4):
                            a0, ra0, _ = oacc(0, qs)
                            a1, ra1, _ = oacc(1, qs)
                            ri0, rri0 = sm_rot.get()
                            P.op("dve", lambda e, o=ri0, i=a0[:, 128:129]: e.reciprocal(out=o, in_=i), r=[ra0], w=[rri0])
                            ri1, rri1 = sm_rot.get()
                            P.op("dve", lambda e, o=ri1, i=a1[:, 128:129]: e.reciprocal(out=o, in_=i), r=[ra1], w=[rri1])
                            ts("dve", ri1, ri1, nlam, None, ALU.mult, None, r=[rri1, R_const], w=[rri1])
                            t1, rt1 = W_rot.get()
                            act(t1[:, 0:128], a1[:, 0:128], AF.Copy, r=[ra1, rri1], w=[rt1], scale=ri1)
                            stt(t1[:, 128:256], a0[:, 0:128], ri0, t1[:, 0:128], ALU.mult, ALU.add, r=[ra0, rri0, rt1], w=[rt1])
                            ss, rss = sm_rot.get()
                            act(t1[:, 256:384], t1[:, 128:256], AF.Square, r=[rt1], w=[rt1, rss], accum=ss)
                            rstd, rr = rstd_from_ss(ss, rss, 1.0 / 128)
                            xt, rx = xn_rot.get()
                            stt(xt[:, 0:128], t1[:, 128:256], rstd, gsub, ALU.mult, ALU.mult, r=[rt1, rr, R_const], w=[rx])
                            tr(ptr_bf[:, qs * 128:(qs + 1) * 128], xt[:, 0:128], ident_b, r=[rx, R_const], w=[rptr])
                        cp("act", big3[:, 16 + hd, :], ptr_bf[:, 0:512], r=[rptr], w=[R_big[16 + hd]])
                    stage(47)
                    for j in range(2):
                        g3, rg_, _ = wload(SL_GA + 2 * j)
                        o3, ro_, _ = wload(SL_AO + 2 * j)
                        for q in range(4):
                            db = 4 * j + q
                            pg, rpg, _ = ps_rot.get()
                            proj_fm(g3, rg_, q, 512, pg, rpg)
                            tg, rtg = W_rot.get()
                            act(tg[:, 0:512], pg[:, 0:512], AF.Tanh, r=[rpg], w=[rtg], scale=0.5)
                            py, rpy, _ = ps_rot.get()
                            for kc in range(8):
                                mm(py[:, :], o3[:, kc, q * 128:(q + 1) * 128], big3[:, 16 + kc, :], kc == 0, kc == 7, r=[ro_, R_big[16 + kc]], w=[rpy])
                            stt(tg[:, 0:512], tg[:, 0:512], 1.0, py[:, :], ALU.add, ALU.mult, r=[rtg, rpy], w=[rtg])
                            tt("pool", big3[:, 24 + db, :], tg[:, 0:512], big3[:, 24 + db, :], ALU.add, r=[rtg, R_big[24 + db]], w=[R_big[24 + db]])
                    stage(48)
                    for j in range(2):
                        s3, rs_, _ = wload(SL_O + j)
                        for tb in range(4):
                            pst, rps, _ = ps_rot.get()
                            for kc in range(8):
                                mm(pst[:, :], big3[:, 24 + kc, tb * 128:(tb + 1) * 128], s3[:, kc, :], kc == 0, kc == 7, r=[rs_, R_big[24 + kc]], w=[rps])
                            hv = h3[:, tb, j * 512:(j + 1) * 512]
                            stt(hv, pst[:, :], 0.5, hv, ALU.mult, ALU.add, r=[rps, R_h[tb]], w=[R_h[tb]])
                    stage(49)
                    norm_to_hnT([(h3[:, tb, :], R_h[tb]) for tb in range(4)], 128, g_mlp, 4)
                    for f in range(8):
                        s3, rs_, _ = wload(SL_F + f)
                        for q in range(4):
                            fc = 4 * f + q
                            pst, rps, _ = ps_rot.get()
                            proj_fm(s3, rs_, q, 512, pst, rps)
                            sq, rsq = W_rot.get()
                            act(sq[:, 0:512], pst[:, :], AF.Square, r=[rps], w=[rsq])
                            stt(big3[:, fc, :], pst[:, :], 0.0, sq[:, 0:512], ALU.is_gt, ALU.mult, r=[rps, rsq], w=[R_big[fc]])
                    for j in range(8):
                        _, rs_, flat = wload(SL_H + j)
                        s32 = flat.rearrange("p (f d) -> p f d", f=32)
                        pst, rps, _ = ps_rot.get()
                        for fc in range(32):
                            mm(pst[:, :], s32[:, fc, :], big3[:, fc, :], fc == 0, fc == 31, r=[rs_, R_big[fc]], w=[rps])
                        ot, rot_ = W_rot.get()
                        cp("act", ot[:, 0:512], pst[:, :], r=[rps], w=[rot_])
                        p2, rp2, _ = ps_rot.get()
                        for tb in range(4):
                            tr(p2[:, tb * 128:(tb + 1) * 128], ot[:, tb * 128:(tb + 1) * 128], identf[:, :], r=[rot_, R_const], w=[rp2])
                        hv = h3[:, :, j * 128:(j + 1) * 128]
                        tt("dve", hv, hv, p2[:, :].rearrange("p (t d) -> p t d", t=4), ALU.add, r=[rp2] + R_h, w=R_h)
                    stage(51)
                    for tb in range(4):
                        src = h3[:, tb, :]
                        xt, rx = xn_rot.get()
                        ss, rss = sm_rot.get()
                        act(xt[:, :], src, AF.Square, r=[R_h[tb]], w=[rx, rss], accum=ss)
                        rstd, rr = rstd_from_ss(ss, rss, 1.0 / D)
                        stt(src, src, rstd, gfin, ALU.mult, ALU.mult, r=[R_h[tb], rr, R_const], w=[R_h[tb]])
                        o = dma("pool", "y%d" % tb, y_d[b, r0 + tb * 128:r0 + (tb + 1) * 128, :], src, r=[R_h[tb]], w=[])
                        final_ops.append(o)

        except StopBuild:
            final_ops = [dma("pool", "y0", y_d[0, 0:128, :], hbuf[:, 0:1024], r=R_h, w=[])]
        P.emit(nc, final_ops[-4:])
    return nc


def host_consts(S):
    T = NMETA + S
    ident = np.eye(128, dtype=np.float32)
    kk = np.arange(128)
    tri = (kk[None, :] >= kk[:, None]).astype(np.float32)
    perm = np.zeros((128, 128), np.float32)
    perm[kk ^ 32, kk] = 1.0
    cmat = np.concatenate([ident, tri, perm], axis=1)
    inv = (1.0 / (np.float32(10000.0) ** (np.arange(0, 64, 2, dtype=np.float32) / np.float32(64)))).astype(np.float32)
    ang = (np.arange(T, dtype=np.float32)[:, None] * inv[None, :]).astype(np.float32)
    cos = np.cos(ang).astype(np.float32).T
    sin = np.sin(ang).astype(np.float32).T
    i = kk % 32
    sgn = np.where((kk % 64) < 32, -1.0, 1.0).astype(np.float32)
    rcos = np.ascontiguousarray(cos[i, :])
    rsin = np.ascontiguousarray(sin[i, :] * sgn[:, None])
    return cmat, rcos, rsin


def host_layout(inputs):
    f = lambda a: np.asarray(a, dtype=np.float32)
    fm = lambda v: np.ascontiguousarray(f(v).reshape(8, 128).T)
    pvec = np.zeros((128, 96), np.float32)
    pvec[:, 0:8] = fm(inputs["g_mix"][0])
    pvec[:, 8:16] = fm(inputs["g_mlp"][0])
    cw = f(inputs["conv_w"][0])
    for t in range(4):
        pvec[:, 16 + 8 * t:24 + 8 * t] = fm(cw[t])
    pvec[:, 48:56] = fm(inputs["conv_b"][0])
    pvec[:, 56:64] = fm(inputs["lru_lambda"][0])
    pvec[:, 64:72] = np.ascontiguousarray(f(inputs["b_a"][0]).T)
    pvec[:, 72:80] = np.ascontiguousarray(f(inputs["b_x"][0]).T)
    pbc = np.zeros((128, 1024 + 128 + 256), np.float32)
    pbc[:, 0:1024] = f(inputs["g_final"])[None, :]
    pbc[:, 1024:1152] = f(inputs["g_subln"][0])[None, :]
    pbc[:, 1152:1216] = f(inputs["lam_q1"][0])[None, :]
    pbc[:, 1216:1280] = f(inputs["lam_k1"][0])[None, :]
    pbc[:, 1280:1344] = f(inputs["lam_q2"][0])[None, :]
    pbc[:, 1344:1408] = f(inputs["lam_k2"][0])[None, :]
    wa = np.transpose(f(inputs["w_a"][0]), (1, 0, 2)).reshape(128, 1024)
    wx = np.transpose(f(inputs["w_x"][0]), (1, 0, 2)).reshape(128, 1024)
    wgate = np.ascontiguousarray(np.concatenate([wa, wx], axis=1))
    return pvec, pbc, wgate


_CACHE = {}


def run(inputs, S, NB, ncores):
    key = (S, NB)
    if key not in _CACHE:
        _CACHE[key] = build(S, NB)
    nc = _CACHE[key]
    cmat, rcos, rsin = host_consts(S)
    pvec, pbc, wgate = host_layout(inputs)
    f = lambda a: np.ascontiguousarray(np.asarray(a, dtype=np.float32))
    x = f(inputs["x"])
    common = {
        "meta": f(inputs["meta_tokens"]),
        "w_in": f(inputs["w_in"][0]), "w_rnn_out": f(inputs["w_rnn_out"][0]),
        "w_attn_out": f(inputs["w_attn_out"][0]), "w_o": f(inputs["w_o"][0]),
        "w_ff1": f(inputs["w_ff1"][0]), "w_ff2": f(inputs["w_ff2"][0]),
        "wgate": wgate, "pvec": pvec, "pbc": pbc, "cmat": cmat, "rcos": rcos, "rsin": rsin,
    }
    in_maps = []
    for c in range(ncores):
        m = dict(common)
        m["x"] = np.ascontiguousarray(x[c * NB:(c + 1) * NB])
        in_maps.append(m)
    res = run_bass_kernel_spmd(nc, in_maps, core_ids=list(range(ncores)))
    return np.concatenate([np.asarray(r["y"], dtype=np.float32) for r in res.results], axis=0)


def kernel(**inputs):
    return run(inputs, 4096, 2, 8)
```

```python
import contextlib
import numpy as np
import concourse.bass as bass
import concourse.mybir as mybir
from concourse.bass_utils import run_bass_kernel_spmd

F32 = mybir.dt.float32
BF16 = mybir.dt.bfloat16
AF = mybir.ActivationFunctionType
ALU = mybir.AluOpType
AX = mybir.AxisListType

D = 1024
NMETA = 16
EPS = 1e-6
LAM_INIT = 0.8 - 0.6 * 1.0
GC0 = 0.7978845608028654
GC1 = 0.044715
NSLOT = 36


class Reg:
    __slots__ = ("name", "w", "rs", "excl")

    def __init__(self, name, excl=False):
        self.name = name
        self.w = None
        self.rs = {}
        self.excl = excl


class Op:
    __slots__ = ("eng", "fn", "chan", "idx", "eidx", "deps", "sig", "tok", "stream")


class Prog:
    ENGS = ("pe", "act", "dve", "pool", "sp")

    def __init__(self):
        self.ops = []
        self.eng_ops = {e: [] for e in self.ENGS}
        self.chan_last = {}
        self.chan_cnt = {}

    def _add(self, eng, fn, r, w, chan=None):
        op = Op()
        op.eng = eng
        op.fn = fn
        op.chan = chan
        op.idx = len(self.ops)
        op.eidx = len(self.eng_ops[eng])
        op.sig = chan is not None
        op.tok = None
        op.stream = ("c", chan) if chan is not None else eng
        if any(R.excl for R in r):
            w = list(w) + [R for R in r if R.excl and R not in w]
            r = [R for R in r if not R.excl]
        deps = set()
        for R in r:
            if R.w is not None:
                deps.add(R.w)
        for R in w:
            if R.w is not None:
                deps.add(R.w)
            for x in R.rs.values():
                deps.add(x)
        if chan is not None and chan in self.chan_last:
            deps.add(self.chan_last[chan])
        deps.discard(op.idx)
        best = {}
        for d in deps:
            p = self.ops[d]
            if p.stream not in best or best[p.stream] < d:
                best[p.stream] = d
        final = []
        for st, d in best.items():
            p = self.ops[d]
            if chan is None and st == eng:
                if eng == "pe":
                    continue
            final.append(d)
            p.sig = True
        op.deps = final
        for R in r:
            R.rs[op.stream] = op.idx
        for R in w:
            R.w = op.idx
            R.rs = {}
        if chan is not None:
            self.chan_last[chan] = op.idx
            self.chan_cnt[chan] = self.chan_cnt.get(chan, 0) + 1
        self.ops.append(op)
        self.eng_ops[eng].append(op)
        return op

    def op(self, eng, fn, r=(), w=()):
        return self._add(eng, fn, r, w)

    def dma(self, eng, chan, fn, r=(), w=()):
        return self._add(eng, fn, r, w, chan=chan)

    def emit(self, nc, final_waits):
        with contextlib.ExitStack() as st:
            esem = {e: st.enter_context(nc.semaphore("s_" + e)) for e in ("pe", "act", "dve", "pool")}
            csem = {c: st.enter_context(nc.semaphore("c_" + c)) for c in self.chan_cnt}
            cnt = {e: 0 for e in esem}
            ccnt = {c: 0 for c in csem}
            for op in self.ops:
                if op.chan is not None:
                    ccnt[op.chan] += 16
                    op.tok = (csem[op.chan], ccnt[op.chan], "c_" + op.chan)
                elif op.sig:
                    cnt[op.eng] += 1
                    op.tok = (esem[op.eng], cnt[op.eng], "s_" + op.eng)
            ops = self.ops
            eng_ops = self.eng_ops

            def run(engname, e, extra=None):
                waited = {}
                for op in eng_ops[engname]:
                    for d in op.deps:
                        sem, val, key = ops[d].tok
                        if waited.get(key, 0) >= val:
                            continue
                        e.wait_ge(sem, val)
                        waited[key] = val
                    ins = op.fn(e)
                    if op.chan is not None:
                        ins.then_inc(op.tok[0], 16)
                    elif op.sig:
                        ins.then_inc(op.tok[0], 1)
                if extra:
                    for o in extra:
                        sem, val, key = o.tok
                        if waited.get(key, 0) >= val:
                            continue
                        e.wait_ge(sem, val)
                        waited[key] = val

            with nc.Block() as block:
                @block.tensor
                def _(e):
                    run("pe", e)

                @block.scalar
                def _(e):
                    run("act", e)

                @block.vector
                def _(e):
                    run("dve", e)

                @block.gpsimd
                def _(e):
                    run("pool", e)

                @block.sync
                def _(e):
                    run("sp", e, extra=final_waits)


class Rot:
    def __init__(self, items):
        self.items = items
        self.i = 0

    def get(self):
        it = self.items[self.i % len(self.items)]
        self.i += 1
        return it


class StopBuild(Exception):
    pass


def build(S, NB, dbg=0):
    NCH = S // 512
    NKB = S // 128
    T = NMETA + S
    nc = bass.Bass("TRN2", target_bir_lowering=False)
    P = Prog()

    def din(name, shape, dt=F32):
        return nc.dram_tensor(name, list(shape), dt, kind="ExternalInput").ap()

    x_d = din("x", [NB, S, D])
    meta_d = din("meta", [NMETA, D])
    w_in_d = din("w_in", [D, 7 * D])
    w_rnn_d = din("w_rnn_out", [D, D])
    w_att_d = din("w_attn_out", [D, D])
    w_o_d = din("w_o", [D, D])
    w_ff1_d = din("w_ff1", [D, 4 * D])
    w_ff2_d = din("w_ff2", [4 * D, D])
    wg_d = din("wgate", [128, 2 * 8 * 128])
    pv_d = din("pvec", [128, 96])
    pb_d = din("pbc", [128, 1024 + 128 + 256])
    cm_d = din("cmat", [128, 3 * 128])
    cos_d = din("rcos", [128, T])
    sin_d = din("rsin", [128, T])
    y_d = nc.dram_tensor("y", [NB, S, D], F32, kind="ExternalOutput").ap()
    wbf_d = nc.dram_tensor("wbf", [NSLOT, 128, 4096], BF16, kind="Internal").ap()
    kc_d = nc.dram_tensor("kcache", [8, 128, S], BF16, kind="Internal").ap()

    with contextlib.ExitStack() as es:
        def sb(name, cols, dt):
            return es.enter_context(nc.sbuf_tensor(name, [128, cols], dt))

        vc = sb("vc", NKB * 8 * 129, BF16)
        vmeta = sb("vmeta", 8 * 129, BF16)
        kmeta = sb("kmeta", 8 * 32, BF16)
        wg = sb("wg", 2 * 8 * 128, BF16)
        cb = sb("cb", 3 * 128, BF16)
        identf = sb("identf", 128, F32)
        pv = sb("pv", 96, F32)
        pd = sb("pd", 64, F32)
        pbt = sb("pbt", 1024 + 128 + 256, F32)
        kb = sb("kb", 2 * S, BF16)
        hbuf = sb("hbuf", 4 * 1024, F32)
        hnT = sb("hnT", 8 * 512, BF16)
        big = sb("big", 32 * 512, BF16)
        wr = sb("wr", 3 * 4096, BF16)
        WT = 520
        Wt = sb("Wt", 8 * WT, F32)
        rc = sb("rc", 2 * 512, F32)
        et = sb("et", 2 * 1024, BF16)
        sm = sb("sm", 64, F32)
        xn = sb("xn", 2 * 1024, BF16)
        kth = sb("kth", 2 * 512, BF16)
        xcbt = sb("xcbt", 2 * 512, BF16)
        state = sb("state", 8 + 24 + 8 + 24, F32)
        psb = [es.enter_context(nc.psum_tensor("ps%d" % i, [128, 512], F32)) for i in range(8)]

        R_ps = [Reg("ps%d" % i, excl=True) for i in range(8)]
        R_h = [Reg("h%d" % i) for i in range(4)]
        R_hnT = [Reg("hnT%d" % i) for i in range(8)]
        R_big = [Reg("big%d" % i) for i in range(32)]
        R_wr = [Reg("wr%d" % i) for i in range(3)]
        R_W = [Reg("W%d" % i) for i in range(8)]
        R_rc = Reg("rc")
        R_et = [Reg("et0"), Reg("et1")]
        R_xn = [Reg("xn0"), Reg("xn1")]
        R_kth = [Reg("kth0"), Reg("kth1")]
        R_xcb = [Reg("xcb0"), Reg("xcb1")]
        R_kb = [Reg("kb0"), Reg("kb1")]
        R_kd = [Reg("kd%d" % i) for i in range(8)]
        R_wbf = [Reg("wbf%d" % i) for i in range(NSLOT)]
        R_vc = [Reg("vc%d" % i) for i in range(NKB)]
        R_vmeta = Reg("vmeta")
        R_kmeta = Reg("kmeta")
        R_const = Reg("const")
        R_state = [Reg("st%d" % i) for i in range(8)]
        R_state0 = Reg("state0")
        R_sm = [Reg("sm%d" % i) for i in range(64)]

        h3 = hbuf[:, :].rearrange("p (t d) -> p t d", t=4)
        hnT3 = hnT[:, :].rearrange("p (k t) -> p k t", k=8)
        big3 = big[:, :].rearrange("p (i t) -> p i t", i=32)
        wr3 = [wr[:, s * 4096:(s + 1) * 4096] for s in range(3)]
        Wv = [Wt[:, i * WT:(i + 1) * WT] for i in range(8)]
        vc4 = vc[:, :].rearrange("p (b h e) -> p b h e", b=NKB, h=8)
        vmeta3 = vmeta[:, :].rearrange("p (h e) -> p h e", h=8)
        kmeta3 = kmeta[:, :].rearrange("p (h t) -> p h t", h=8)
        wg4 = wg[:, :].rearrange("p (w n d) -> p w n d", w=2, n=8)
        ident_b = cb[:, 0:128]
        tri_b = cb[:, 128:256]
        perm_b = cb[:, 256:384]
        gfin = pbt[:, 0:1024]
        gsub = pbt[:, 1024:1152]
        g_mix = pv[:, 0:8]
        g_mlp = pv[:, 8:16]
        conv_w = pv[:, 16:48].rearrange("p (t n) -> p t n", t=4)
        conv_b = pv[:, 48:56]
        hcl = pd[:, 0:8]
        hba = pd[:, 8:16]
        hbx = pd[:, 16:24]
        nlam = pd[:, 24:25]
        negh = pd[:, 25:26]
        halfc = pd[:, 26:27]
        hstate = state[:, 0:8]
        carry = state[:, 8:32].rearrange("p (n k) -> p n k", n=8)
        hstate0 = state[:, 32:40]
        carry0 = state[:, 40:64].rearrange("p (n k) -> p n k", n=8)
        ps_bf = [p[:, :].bitcast(BF16) for p in psb]

        sm_rot = Rot([(sm[:, i:i + 1], R_sm[i]) for i in range(64)])
        W_rot = Rot([(Wv[i], R_W[i]) for i in range(8)])
        ps_rot = Rot([(psb[i], R_ps[i], ps_bf[i]) for i in range(8)])
        xn_rot = Rot([(xn[:, i * 1024:(i + 1) * 1024], R_xn[i]) for i in range(2)])
        kth_rot = Rot([(kth[:, i * 512:(i + 1) * 512], R_kth[i]) for i in range(2)])

        def act(out, in_, func, r, w, bias=None, scale=None, accum=None):
            kw = {}
            if bias is not None:
                kw["bias"] = bias
            if scale is not None:
                kw["scale"] = scale
            if accum is not None:
                kw["accum_out"] = accum
            return P.op("act", lambda e: e.activation(out=out, in_=in_, func=func, **kw), r=r, w=w)

        def ts(eng, out, in0, s1, s2, op0, op1, r, w):
            if op1 is None:
                return P.op(eng, lambda e: e.tensor_scalar(out=out, in0=in0, scalar1=s1, scalar2=None, op0=op0), r=r, w=w)
            return P.op(eng, lambda e: e.tensor_scalar(out=out, in0=in0, scalar1=s1, scalar2=s2, op0=op0, op1=op1), r=r, w=w)

        def stt(out, in0, scalar, in1, op0, op1, r, w):
            return P.op("dve", lambda e: e.scalar_tensor_tensor(out=out, in0=in0, scalar=scalar, in1=in1, op0=op0, op1=op1), r=r, w=w)

        def tt(eng, out, in0, in1, op, r, w):
            return P.op(eng, lambda e: e.tensor_tensor(out=out, in0=in0, in1=in1, op=op), r=r, w=w)

        def cp(eng, out, in_, r, w):
            if eng == "act":
                return P.op("act", lambda e: e.copy(out=out, in_=in_), r=r, w=w)
            return P.op(eng, lambda e: e.tensor_copy(out=out, in_=in_), r=r, w=w)

        def mm(out, lhsT, rhs, start, stop, r, w, skip=False):
            return P.op("pe", lambda e: e.matmul(out, lhsT, rhs, start=start, stop=stop, skip_group_check=skip), r=r, w=w)

        def tr(out, in_, ident, r, w):
            return P.op("pe", lambda e: e.transpose(out, in_, ident), r=r, w=w)

        def dma(eng, chan, out, in_, r, w, slow=False):
            if slow:
                return P.dma(eng, chan, lambda e: e.dma_start(out=out, in_=in_, allow_slow_non_contiguous=True), r=r, w=w)
            return P.dma(eng, chan, lambda e: e.dma_start(out=out, in_=in_), r=r, w=w)

        def rstd_from_ss(ss_ap, ss_reg, inv_n, npart=128):
            v, rv = sm_rot.get()
            ts("pool", v[:npart, :], ss_ap, inv_n, EPS, ALU.mult, ALU.add, r=[ss_reg], w=[rv])
            o, ro = sm_rot.get()
            tt("pool", o[:npart, :], v[:npart, :], negh[:npart, :], ALU.pow, r=[rv, R_const], w=[ro])
            return o, ro

        final_ops = []
        def stage(k):
            if dbg == k:
                raise StopBuild()
        try:
            st_f = Wv[0][:, 0:384]
            dma("pool", "su0", st_f, cm_d, r=[], w=[R_W[0]])
            cp("dve", cb[:, :], st_f, r=[R_W[0]], w=[R_const])
            dma("pool", "su1", identf[:, :], cm_d[:, 0:128], r=[], w=[R_const])
            dma("pool", "su0", pv[:, :], pv_d, r=[], w=[R_const])
            dma("pool", "su1", pbt[:, :], pb_d, r=[], w=[R_const])
            for half in range(2):
                for q in range(2):
                    wt_, rw_ = W_rot.get()
                    dma("pool", "su%d" % q, wt_[:, 0:512], wg_d[:, half * 1024 + q * 512: half * 1024 + (q + 1) * 512], r=[], w=[rw_])
                    cp("dve", wg[:, half * 1024 + q * 512: half * 1024 + (q + 1) * 512], wt_[:, 0:512], r=[rw_], w=[R_const])
            P.op("dve", lambda e: e.memset(negh, -0.5), r=[], w=[R_const])
            P.op("dve", lambda e: e.memset(halfc, 0.5), r=[], w=[R_const])
            lam_raw = pv[:, 56:64]
            tmp8 = pd[:, 32:40]
            act(tmp8, lam_raw, AF.Exp, r=[R_const], w=[R_const], scale=-1.0)
            act(tmp8, tmp8, AF.Ln, r=[R_const], w=[R_const], bias=1.0)
            ts("dve", hcl, tmp8, -4.0, None, ALU.mult, None, r=[R_const], w=[R_const])
            ts("dve", hba, pv[:, 64:72], 0.5, None, ALU.mult, None, r=[R_const], w=[R_const])
            ts("dve", hbx, pv[:, 72:80], 0.5, None, ALU.mult, None, r=[R_const], w=[R_const])
            lq = pbt[:, 1152:1408]
            prod = pd[:, 40:42]
            tl = Wv[1][:, 0:128]
            tt("dve", tl[:, 0:64], lq[:, 0:64], lq[:, 64:128], ALU.mult, r=[R_const], w=[R_W[1]])
            tt("dve", tl[:, 64:128], lq[:, 128:192], lq[:, 192:256], ALU.mult, r=[R_const], w=[R_W[1]])
            P.op("dve", lambda e: e.tensor_reduce(out=prod, in_=tl.rearrange("p (a b) -> p a b", a=2), axis=AX.X, op=ALU.add), r=[R_W[1]], w=[R_const])
            act(prod, prod, AF.Exp, r=[R_const], w=[R_const])
            tt("dve", nlam, prod[:, 1:2], prod[:, 0:1], ALU.subtract, r=[R_const], w=[R_const])
            ts("dve", nlam, nlam, -LAM_INIT, None, ALU.add, None, r=[R_const], w=[R_const])
            ts("dve", gsub, gsub, 1.0 - LAM_INIT, None, ALU.mult, None, r=[R_const], w=[R_const])
            P.op("pool", lambda e: e.memset(vc4[:, :, :, 128:129], 1.0), r=[], w=R_vc)

            stage(1)
            w_in3 = w_in_d.rearrange("(k p) c -> p k c", p=128)
            w_rnn3 = w_rnn_d.rearrange("(k p) c -> p k c", p=128)
            w_att3 = w_att_d.rearrange("(k p) c -> p k c", p=128)
            w_o3 = w_o_d.rearrange("(k p) c -> p k c", p=128)
            w_ff13 = w_ff1_d.rearrange("(k p) c -> p k c", p=128)
            w_ff23 = w_ff2_d.rearrange("(k p) c -> p k c", p=128)

            def win(c0, n=512):
                return [(w_in3[:, :, c0:c0 + n], 8, n)]

            slot_src = []
            for j in range(4):
                slot_src.append([(w_in3[:, :, 256 * j:256 * j + 256], 8, 256), (w_in3[:, :, 1024 + 256 * j:1024 + 256 * j + 256], 8, 256)])
            for j in range(2):
                slot_src.append(win(5120 + 512 * j))
                slot_src.append([(w_rnn3[:, :, 512 * j:512 * j + 512], 8, 512)])
            for j in range(2):
                slot_src.append(win(2048 + 512 * j))
            for j in range(2):
                slot_src.append(win(3072 + 512 * j))
            for j in range(2):
                slot_src.append(win(4096 + 512 * j))
            for j in range(2):
                slot_src.append(win(6144 + 512 * j))
                slot_src.append([(w_att3[:, :, 512 * j:512 * j + 512], 8, 512)])
            for j in range(2):
                slot_src.append([(w_o3[:, :, 512 * j:512 * j + 512], 8, 512)])
            for j in range(8):
                slot_src.append([(w_ff13[:, :, 512 * j:512 * j + 512], 8, 512)])
            for j in range(8):
                slot_src.append([(w_ff23[:, 8 * q:8 * q + 8, 128 * j:128 * j + 128], 8, 128) for q in range(4)])
            assert len(slot_src) == NSLOT
            SL_A, SL_G, SL_R, SL_Q, SL_K, SL_V, SL_GA, SL_AO, SL_O, SL_F, SL_H = 0, 4, 5, 8, 10, 12, 14, 15, 18, 20, 28

            stg = [(hbuf[:, :], R_h), (Wt[:, 0:4096], R_W)]
            cast_eng = ["act", "dve", "pool"]
            for s in range(NSLOT):
                sbuf_f, regs_f = stg[s % 2]
                off = 0
                if s < 4:
                    st3 = sbuf_f[:, 0:4096].rearrange("p (a b) -> p a b", a=8)
                    for i_, (src, a, b) in enumerate(slot_src[s]):
                        dma("sp", "w%d" % (s % 2), st3[:, :, 256 * i_:256 * i_ + 256], src, r=[], w=regs_f)
                else:
                    for (src, a, b) in slot_src[s]:
                        dst = sbuf_f[:, off:off + a * b].rearrange("p (a b) -> p a b", a=a)
                        dma("sp", "w%d" % (s % 2), dst, src, r=[], w=regs_f)
                        off += a * b
                    assert off == 4096
                ring = s % 3
                for q in range(3):
                    lo, hi = [0, 1408, 2752][q], [1408, 2752, 4096][q]
                    cp(cast_eng[q], wr3[ring][:, lo:hi], sbuf_f[:, lo:hi], r=regs_f, w=[R_wr[ring]])
                dma("pool", "su%d" % (s % 2), wbf_d[s], wr3[ring], r=[R_wr[ring]], w=[R_wbf[s]])

            stage(2)
            wcount = [0]

            def wload(slot):
                ring = wcount[0] % 3
                wcount[0] += 1
                dma("sp", "w%d" % ring, wr3[ring], wbf_d[slot], r=[R_wbf[slot]], w=[R_wr[ring]])
                return wr3[ring].rearrange("p (k c) -> p k c", k=8), R_wr[ring], wr3[ring]

            def norm_to_hnT(src_tiles, ntok, gvec, nblk):
                def gen(tb):
                    src, rsrc = src_tiles[tb]
                    xt, rx = xn_rot.get()
                    ss, rss = sm_rot.get()
                    act(xt[:ntok, :], src, AF.Square, r=[rsrc], w=[rx, rss], accum=ss[:ntok, :])
                    yield
                    v, rv = sm_rot.get()
                    ts("pool", v[:ntok, :], ss[:ntok, :], 1.0 / D, EPS, ALU.mult, ALU.add, r=[rss], w=[rv])
                    yield
                    rstd, rr = sm_rot.get()
                    tt("pool", rstd[:ntok, :], v[:ntok, :], negh[:ntok, :], ALU.pow, r=[rv, R_const], w=[rr])
                    yield
                    ts("dve", xt[:ntok, :], src, rstd[:ntok, :], None, ALU.mult, None, r=[rsrc, rr], w=[rx])
                    yield
                    pst, rps, psbf = ps_rot.get()
                    for kc in range(8):
                        tr(psbf[:, kc * ntok:(kc + 1) * ntok], xt[:ntok, kc * 128:(kc + 1) * 128], ident_b[:ntok, :ntok], r=[rx, R_const], w=[rps])
                    yield
                    outv = hnT3[:, :, tb * 128:tb * 128 + ntok]
                    inv = psbf[:, 0:8 * ntok].rearrange("p (k t) -> p k t", k=8)
                    gb = gvec.unsqueeze(2).to_broadcast([128, 8, ntok])
                    tt("dve", outv, inv, gb, ALU.mult, r=[rps, R_const], w=R_hnT)
                    yield
                for t0 in range(0, nblk, 2):
                    run_all(*[gen(tb) for tb in range(t0, min(t0 + 2, nblk))])

            def proj_fm(slot3, rslot, cb_, ntok, pst, rps):
                for kc in range(8):
                    mm(pst[:, 0:ntok], slot3[:, kc, cb_ * 128:(cb_ + 1) * 128], hnT3[:, kc, 0:ntok], kc == 0, kc == 7, r=[rslot, R_hnT[kc]], w=[rps])

            def rnn_gen(n, ntok, xr_slot, xr_cb, gr_slot, gr_cb, rx_slot, rg_slot, with_y, lane):
                (T1, r1), (T2, r2), (T3, r3), (T4, r4) = lane["W"]
                xcb, rxcb = lane["xcb"]
                psr = lane["ps"]
                pst, rps, _ = psr.get()
                proj_fm(xr_slot, rx_slot, xr_cb, ntok, pst, rps)
                yield
                xr, rxr = T1, r1
                cp("pool", xr[:, 0:3], carry[:, n, :], r=[R_state[n]], w=[rxr])
                cp("act", xr[:, 3:3 + ntok], pst[:, 0:ntok], r=[rps], w=[rxr])
                yield
                xc, rxc = T2, r2
                ts("dve", xc[:, 0:ntok], xr[:, 3:3 + ntok], conv_w[:, 3, n:n + 1], conv_b[:, n:n + 1], ALU.mult, ALU.add, r=[rxr, R_const], w=[rxc])
                for k in range(3):
                    stt(xc[:, 0:ntok], xr[:, k:k + ntok], conv_w[:, k, n:n + 1], xc[:, 0:ntok], ALU.mult, ALU.add, r=[rxr, rxc, R_const], w=[rxc])
                cp("pool", carry[:, n, :], xr[:, ntok:ntok + 3], r=[rxr], w=[R_state[n]])
                cp("pool", xcb[:, 0:ntok], xc[:, 0:ntok], r=[rxc], w=[rxcb])
                yield
                pa, rpa, _ = psr.get()
                mm(pa[:, 0:ntok], wg4[:, 0, n, :], xcb[:, 0:ntok], True, True, r=[rxcb, R_const], w=[rpa])
                pi, rpi, _ = psr.get()
                mm(pi[:, 0:ntok], wg4[:, 1, n, :], xcb[:, 0:ntok], True, True, r=[rxcb, R_const], w=[rpi])
                yield
                ta, rta = T1, r1
                ti, rti = T3, r3
                act(ta[:, 0:ntok], pa[:, 0:ntok], AF.Tanh, r=[rpa, R_const], w=[rta], bias=hba[:, n:n + 1], scale=0.5)
                act(ti[:, 0:ntok], pi[:, 0:ntok], AF.Tanh, r=[rpi, R_const], w=[rti], bias=hbx[:, n:n + 1], scale=0.5)
                act(ta[:, 0:ntok], ta[:, 0:ntok], AF.Exp, r=[rta, R_const], w=[rta], bias=hcl[:, n:n + 1], scale=hcl[:, n:n + 1])
                yield
                sq, rsq = T4, r4
                tt("pool", sq[:, 0:ntok], ta[:, 0:ntok], ta[:, 0:ntok], ALU.mult, r=[rta], w=[rsq])
                act(sq[:, 0:ntok], sq[:, 0:ntok], AF.Sqrt, r=[rsq], w=[rsq], bias=1.0, scale=-1.0)
                stt(ti[:, 0:ntok], ti[:, 0:ntok], 1.0, xc[:, 0:ntok], ALU.add, ALU.mult, r=[rti, rxc], w=[rti])
                yield
                stt(ti[:, 0:ntok], ti[:, 0:ntok], 0.5, sq[:, 0:ntok], ALU.mult, ALU.mult, r=[rti, rsq], w=[rti])
                hh, rhh = T2, r2
                P.op("dve", lambda e: e.tensor_tensor_scan(out=hh[:, 0:ntok], data0=ta[:, 0:ntok], data1=ti[:, 0:ntok], initial=hstate[:, n:n + 1], op0=ALU.mult, op1=ALU.add),
                     r=[rta, rti, R_state[n]], w=[rhh])
                cp("pool", hstate[:, n:n + 1], hh[:, ntok - 1:ntok], r=[rhh], w=[R_state[n]])
                yield
                if not with_y:
                    return
                pg, rpg, _ = psr.get()
                proj_fm(gr_slot, rg_slot, gr_cb, ntok, pg, rpg)
                yield
                g2, rg2 = T4, r4
                act(g2[:, 0:ntok], pg[:, 0:ntok], AF.Square, r=[rpg], w=[rg2])
                ts("dve", g2[:, 0:ntok], g2[:, 0:ntok], GC1 * GC0, GC0, ALU.mult, ALU.add, r=[rg2], w=[rg2])
                tt("dve", g2[:, 0:ntok], g2[:, 0:ntok], pg[:, 0:ntok], ALU.mult, r=[rg2, rpg], w=[rg2])
                yield
                act(g2[:, 0:ntok], g2[:, 0:ntok], AF.Tanh, r=[rg2], w=[rg2])
                stt(g2[:, 0:ntok], g2[:, 0:ntok], 1.0, pg[:, 0:ntok], ALU.add, ALU.mult, r=[rg2, rpg], w=[rg2])
                stt(big3[:, 8 + n, :], g2[:, 0:ntok], 0.5, hh[:, 0:ntok], ALU.mult, ALU.mult, r=[rg2, rhh], w=[R_big[8 + n]])
                yield

            def lockstep(*gens):
                gens = list(gens)
                while gens:
                    for g in list(gens):
                        try:
                            next(g)
                        except StopIteration:
                            gens.remove(g)
                    yield

            def run_all(*gens):
                for _ in lockstep(*gens):
                    pass

            ring_owner = [None, None, None]

            def wload_g(slot, sid):
                for r_ in range(3):
                    if ring_owner[r_] == sid:
                        ring_owner[r_] = None
                while ring_owner[wcount[0] % 3] is not None:
                    yield
                ring = wcount[0] % 3
                res = wload(slot)
                ring_owner[ring] = sid
                return res

            def release(sid):
                for r_ in range(3):
                    if ring_owner[r_] == sid:
                        ring_owner[r_] = None

            def mk_lane(i):
                return {"W": [(Wv[4 * i + k], R_W[4 * i + k]) for k in range(4)],
                        "xcb": (xcbt[:, i * 512:(i + 1) * 512], R_xcb[i]),
                        "ps": Rot([(psb[3 * i + k], R_ps[3 * i + k], ps_bf[3 * i + k]) for k in range(3)])}
            lanes = [mk_lane(0), mk_lane(1)]
            et_f = et[:, :].bitcast(F32)
            qkv_res = {"W": Rot([(et_f[:, i * 512:(i + 1) * 512], R_et[i]) for i in range(2)]),
                       "ps": Rot([(psb[6 + k], R_ps[6 + k], ps_bf[6 + k]) for k in range(2)])}
            dflt_res = {"W": W_rot, "ps": ps_rot}

            def rope_block(pst, rps, ntok, dst, rdst, res=None):
                res = res or dflt_res
                qb, rqb = xn_rot.get()
                cp("act", qb[:, 0:ntok], pst[:, 0:ntok], r=[rps], w=[rqb])
                p2, rp2, _ = res["ps"].get()
                mm(p2[:, 0:ntok], perm_b, qb[:, 0:ntok], True, True, r=[rqb, R_const], w=[rp2])
                t1, rt1 = res["W"].get()
                tt("dve", t1[:, 0:ntok], rc[:, 0:ntok], pst[:, 0:ntok], ALU.mult, r=[rps, rqb, R_rc], w=[rt1])
                t2, rt2 = res["W"].get()
                tt("dve", t2[:, 0:ntok], rc[:, 512:512 + ntok], p2[:, 0:ntok], ALU.mult, r=[rp2, R_rc], w=[rt2])
                tt("pool", dst, t1[:, 0:ntok], t2[:, 0:ntok], ALU.add, r=[rt1, rt2], w=rdst)

            P.op("pool", lambda e: e.memset(hbuf[:32, 0:1024], 0.0), r=[], w=[R_h[0]])
            P.op("pool", lambda e: e.memset(kmeta[:, :], 0.0), r=[], w=[R_kmeta])
            P.op("pool", lambda e: e.memset(vmeta[:32, :], 0.0), r=[], w=[R_vmeta])
            P.op("pool", lambda e: e.memset(vmeta3[:NMETA, :, 128:129], 1.0), r=[], w=[R_vmeta])
            dma("pool", "x0", hbuf[:NMETA, 0:1024], meta_d, r=[], w=[R_h[0]])
            dma("pool", "rc", rc[:, 0:NMETA], cos_d[:, 0:NMETA], r=[], w=[R_rc])
            dma("pool", "rc", rc[:, 512:512 + NMETA], sin_d[:, 0:NMETA], r=[], w=[R_rc])
            P.op("pool", lambda e: e.memset(state[:, :], 0.0), r=[], w=R_state)
            norm_to_hnT([(hbuf[:32, 0:1024], R_h[0])], 32, g_mix, 1)
            stage(31)
            for j in range(4):
                s3, rs_, _ = wload(SL_A + j)
                for q in range(2):
                    run_all(rnn_gen(2 * j + q, NMETA, s3, q, None, None, rs_, None, False, lanes[q]))
            stage(33)
            cp("pool", state[:, 32:64], state[:, 0:32], r=R_state, w=[R_state0])
            for j in range(2):
                s3, rs_, _ = wload(SL_K + j)
                for q in range(4):
                    hd = 4 * j + q
                    pst, rps, _ = ps_rot.get()
                    proj_fm(s3, rs_, q, NMETA, pst, rps)
                    rope_block(pst, rps, NMETA, kmeta3[:, hd, 0:NMETA], [R_kmeta])
            stage(34)
            for j in range(2):
                s3, rs_, _ = wload(SL_V + j)
                pst, rps, _ = ps_rot.get()
                for kc in range(8):
                    mm(pst[:32, :], hnT3[:, kc, 0:32], s3[:, kc, :], kc == 0, kc == 7, r=[rs_, R_hnT[kc]], w=[rps])
                cp("act", vmeta3[:NMETA, 4 * j:4 * j + 4, 0:128], pst[:NMETA, :].rearrange("p (h e) -> p h e", h=4), r=[rps], w=[R_vmeta])

            stage(3)
            def oacc(c, qs):
                i = c * 4 + qs
                bank = 5 + i // 3
                off = (i % 3) * 129
                return psb[bank][:, off:off + 129], R_ps[bank], (i % 3 == 0)

            for b in range(NB):
                cp("pool", state[:, 0:32], state[:, 32:64], r=[R_state0], w=R_state)
                for ci in range(NCH):
                    r0 = 512 * ci
                    p0 = NMETA + r0
                    for tb in range(4):
                        dma("pool", "x%d" % tb, h3[:, tb, :], x_d[b, r0 + tb * 128:r0 + (tb + 1) * 128, :], r=[], w=[R_h[tb]])
                    dma("pool", "rc", rc[:, 0:512], cos_d[:, p0:p0 + 512], r=[], w=[R_rc])
                    dma("pool", "rc", rc[:, 512:1024], sin_d[:, p0:p0 + 512], r=[], w=[R_rc])
                    norm_to_hnT([(h3[:, tb, :], R_h[tb]) for tb in range(4)], 128, g_mix, 4)
                    stage(40)
                    def rnn_stream(sid):
                        for j in range(4):
                            s3, rs_, _ = yield from wload_g(SL_A + j, sid)
                            yield from lockstep(rnn_gen(2 * j, 512, s3, 0, s3, 2, rs_, rs_, True, lanes[0]),
                                                rnn_gen(2 * j + 1, 512, s3, 1, s3, 3, rs_, rs_, True, lanes[1]))
                        release(sid)

                    def qkv_stream(sid):
                        for j in range(2):
                            s3, rs_, _ = yield from wload_g(SL_Q + j, sid)
                            for q in range(4):
                                hd = 4 * j + q
                                pst, rps, _ = qkv_res["ps"].get()
                                proj_fm(s3, rs_, q, 512, pst, rps)
                                yield
                                rope_block(pst, rps, 512, big3[:, hd, :], [R_big[hd]], qkv_res)
                                yield
                        for j in range(2):
                            s3, rs_, _ = yield from wload_g(SL_K + j, sid)
                            for q in range(4):
                                hd = 4 * j + q
                                pst, rps, _ = qkv_res["ps"].get()
                                proj_fm(s3, rs_, q, 512, pst, rps)
                                yield
                                kt, rkt = kth_rot.get()
                                rope_block(pst, rps, 512, kt, [rkt], qkv_res)
                                dma("sp", "kw%d" % (hd % 2), kc_d[hd, :, r0:r0 + 512], kt, r=[rkt], w=[R_kd[hd]])
                                yield
                        for j in range(2):
                            s3, rs_, _ = yield from wload_g(SL_V + j, sid)
                            for tb in range(4):
                                pst, rps, _ = qkv_res["ps"].get()
                                for kc in range(8):
                                    mm(pst[:, :], hnT3[:, kc, tb * 128:(tb + 1) * 128], s3[:, kc, :], kc == 0, kc == 7, r=[rs_, R_hnT[kc]], w=[rps])
                                blk = 4 * ci + tb
                                cp("act" if tb % 2 == 0 else "dve", vc4[:, blk, 4 * j:4 * j + 4, 0:128], pst[:, :].rearrange("p (h e) -> p h e", h=4), r=[rps], w=[R_vc[blk]])
                                yield
                        release(sid)

                    run_all(rnn_stream(1), qkv_stream(2))
                    stage(41)
                    for j in range(2):
                        g3, rg_, _ = wload(SL_G + 2 * j)
                        o3, ro_, _ = wload(SL_R + 2 * j)
                        for q in range(4):
                            db = 4 * j + q
                            pg, rpg, _ = ps_rot.get()
                            proj_fm(g3, rg_, q, 512, pg, rpg)
                            tg, rtg = W_rot.get()
                            act(tg[:, 0:512], pg[:, 0:512], AF.Tanh, r=[rpg], w=[rtg], scale=0.5)
                            py, rpy, _ = ps_rot.get()
                            for kc in range(8):
                                mm(py[:, :], o3[:, kc, q * 128:(q + 1) * 128], big3[:, 8 + kc, :], kc == 0, kc == 7, r=[ro_, R_big[8 + kc]], w=[rpy])
                            stt(big3[:, 24 + db, :], tg[:, 0:512], 1.0, py[:, :], ALU.add, ALU.mult, r=[rtg, rpy], w=[R_big[24 + db]])
                    stage(42)
                    stage(45)
                    nreal = 4 * ci + 4
                    kend = r0 + 512
                    for hd in range(8):
                        ks = hd % 2
                        kbv = kb[:, ks * S:ks * S + kend]
                        dma("sp", "k%d" % ks, kbv, kc_d[hd, :, 0:kend], r=[R_kd[hd]], w=[R_kb[ks]])
                        qT = big3[:, hd, :]
                        rq = R_big[hd]
                        def blk_params(kbk):
                            if kbk < 0:
                                kk, q_lo, kr = 32, 0, [R_kmeta]
                            else:
                                jj = kbk - 4 * ci
                                q_lo = 128 * jj if jj > 0 else 0
                                kk, kr = 128, [R_kb[ks]]
                            eslot = (kbk + 1) % 2
                            ev = et[:, eslot * 1024:(eslot + 1) * 1024].rearrange("p (c t) -> p c t", c=2)
                            return kk, q_lo, kr, eslot, ev

                        def qk_stage(kbk):
                            kk, q_lo, kr, eslot, ev = blk_params(kbk)
                            nq = 512 - q_lo
                            for c in range(2):
                                sbank = 2 * eslot + c
                                if kbk < 0:
                                    lhs = kmeta3[c * 64:(c + 1) * 64, hd, :]
                                else:
                                    lhs = kbv[c * 64:(c + 1) * 64, kbk * 128:(kbk + 1) * 128]
                                mm(psb[sbank][:kk, 0:nq], lhs, qT[c * 64:(c + 1) * 64, q_lo:512], True, True, r=kr + [rq], w=[R_ps[sbank]])
                            for c in range(2):
                                sbank = 2 * eslot + c
                                act(ev[:kk, c, q_lo:512], psb[sbank][:kk, 0:nq], AF.Exp, r=[R_ps[sbank]], w=[R_et[eslot]], scale=0.125)
                            if kbk >= 0 and kbk >= 4 * ci:
                                jj = kbk - 4 * ci
                                for c in range(2):
                                    tt("pool", ev[:, c, jj * 128:(jj + 1) * 128], ev[:, c, jj * 128:(jj + 1) * 128], tri_b, ALU.mult, r=[R_et[eslot], R_const], w=[R_et[eslot]])

                        def pv_stage(kbk):
                            kk, q_lo, kr, eslot, ev = blk_params(kbk)
                            if kbk >= 0 and kbk >= 4 * ci:
                                qs_list = list(range(kbk - 4 * ci, 4))
                            else:
                                qs_list = [0, 1, 2, 3]
                            for c in range(2):
                                for qs in qs_list:
                                    oa, ro, first_in_bank = oacc(c, qs)
                                    if kbk < 0:
                                        rhs = vmeta3[:32, hd, :]
                                        rv = R_vmeta
                                    else:
                                        rhs = vc4[:, kbk, hd, :]
                                        rv = R_vc[kbk]
                                    last = (kbk == 4 * ci + qs)
                                    mm(oa, ev[:kk, c, qs * 128:(qs + 1) * 128], rhs, (kbk < 0) and first_in_bank, last, r=[R_et[eslot], rv], w=[ro], skip=True)

                        blocks = list(range(-1, nreal))
                        qk_stage(blocks[0])
                        for bi_, kbk in enumerate(blocks):
                            if bi_ + 1 < len(blocks):
                                qk_stage(blocks[bi_ + 1])
                            pv_stage(kbk)
                        stage(46)
                        ptr, rptr, ptr_bf = ps_rot.get()
                        while rptr in (R_ps[5], R_ps[6], R_ps[7]):
                            ptr, rptr, ptr_bf = ps_rot.get()
                        def fin_gen(qs):
                            a0, ra0, _ = oacc(0, qs)
                            a1, ra1, _ = oacc(1, qs)
                            ri0, rri0 = sm_rot.get()
                            P.op("dve", lambda e, o=ri0, i=a0[:, 128:129]: e.reciprocal(out=o, in_=i), r=[ra0], w=[rri0])
                            yield
                            ri1, rri1 = sm_rot.get()
                            P.op("dve", lambda e, o=ri1, i=a1[:, 128:129]: e.reciprocal(out=o, in_=i), r=[ra1], w=[rri1])
                            yield
                            ts("dve", ri1, ri1, nlam, None, ALU.mult, None, r=[rri1, R_const], w=[rri1])
                            yield
                            t1, rt1 = W_rot.get()
                            act(t1[:, 0:128], a1[:, 0:128], AF.Copy, r=[ra1, rri1], w=[rt1], scale=ri1)
                            yield
                            stt(t1[:, 128:256], a0[:, 0:128], ri0, t1[:, 0:128], ALU.mult, ALU.add, r=[ra0, rri0, rt1], w=[rt1])
                            yield
                            ss, rss = sm_rot.get()
                            act(t1[:, 256:384], t1[:, 128:256], AF.Square, r=[rt1], w=[rt1, rss], accum=ss)
                            yield
                            v_, rv_ = sm_rot.get()
                            ts("pool", v_, ss, 1.0 / 128, EPS, ALU.mult, ALU.add, r=[rss], w=[rv_])
                            yield
                            rstd, rr = sm_rot.get()
                            tt("pool", rstd, v_, negh, ALU.pow, r=[rv_, R_const], w=[rr])
                            yield
                            yab = t1[:, 384:448].bitcast(BF16)
                            stt(yab, t1[:, 128:256], rstd, gsub, ALU.mult, ALU.mult, r=[rt1, rr, R_const], w=[rt1])
                            yield
                            tr(ptr_bf[:, qs * 128:(qs + 1) * 128], yab, ident_b, r=[rt1, R_const], w=[rptr])
                            yield

                        run_all(*[fin_gen(qs) for qs in range(4)])
                        cp("act", big3[:, 16 + hd, :], ptr_bf[:, 0:512], r=[rptr], w=[R_big[16 + hd]])
                    stage(47)
                    for j in range(2):
                        g3, rg_, _ = wload(SL_GA + 2 * j)
                        o3, ro_, _ = wload(SL_AO + 2 * j)
                        for q in range(4):
                            db = 4 * j + q
                            pg, rpg, _ = ps_rot.get()
                            proj_fm(g3, rg_, q, 512, pg, rpg)
                            tg, rtg = W_rot.get()
                            act(tg[:, 0:512], pg[:, 0:512], AF.Tanh, r=[rpg], w=[rtg], scale=0.5)
                            py, rpy, _ = ps_rot.get()
                            for kc in range(8):
                                mm(py[:, :], o3[:, kc, q * 128:(q + 1) * 128], big3[:, 16 + kc, :], kc == 0, kc == 7, r=[ro_, R_big[16 + kc]], w=[rpy])
                            stt(tg[:, 0:512], tg[:, 0:512], 1.0, py[:, :], ALU.add, ALU.mult, r=[rtg, rpy], w=[rtg])
                            tt("pool", big3[:, 24 + db, :], tg[:, 0:512], big3[:, 24 + db, :], ALU.add, r=[rtg, R_big[24 + db]], w=[R_big[24 + db]])
                    stage(48)
                    for j in range(2):
                        s3, rs_, _ = wload(SL_O + j)
                        for tb in range(4):
                            pst, rps, _ = ps_rot.get()
                            for kc in range(8):
                                mm(pst[:, :], big3[:, 24 + kc, tb * 128:(tb + 1) * 128], s3[:, kc, :], kc == 0, kc == 7, r=[rs_, R_big[24 + kc]], w=[rps])
                            hv = h3[:, tb, j * 512:(j + 1) * 512]
                            stt(hv, pst[:, :], 0.5, hv, ALU.mult, ALU.add, r=[rps, R_h[tb]], w=[R_h[tb]])
                    stage(49)
                    norm_to_hnT([(h3[:, tb, :], R_h[tb]) for tb in range(4)], 128, g_mlp, 4)
                    for f in range(8):
                        s3, rs_, _ = wload(SL_F + f)
                        for q in range(4):
                            fc = 4 * f + q
                            pst, rps, _ = ps_rot.get()
                            proj_fm(s3, rs_, q, 512, pst, rps)
                            sq, rsq = W_rot.get()
                            act(sq[:, 0:512], pst[:, :], AF.Square, r=[rps], w=[rsq])
                            stt(big3[:, fc, :], pst[:, :], 0.0, sq[:, 0:512], ALU.is_gt, ALU.mult, r=[rps, rsq], w=[R_big[fc]])
                    for j in range(8):
                        _, rs_, flat = wload(SL_H + j)
                        s32 = flat.rearrange("p (f d) -> p f d", f=32)
                        pst, rps, _ = ps_rot.get()
                        for fc in range(32):
                            mm(pst[:, :], s32[:, fc, :], big3[:, fc, :], fc == 0, fc == 31, r=[rs_, R_big[fc]], w=[rps])
                        ot, rot_ = W_rot.get()
                        cp("act", ot[:, 0:512], pst[:, :], r=[rps], w=[rot_])
                        p2, rp2, _ = ps_rot.get()
                        for tb in range(4):
                            tr(p2[:, tb * 128:(tb + 1) * 128], ot[:, tb * 128:(tb + 1) * 128], identf[:, :], r=[rot_, R_const], w=[rp2])
                        hv = h3[:, :, j * 128:(j + 1) * 128]
                        tt("dve", hv, hv, p2[:, :].rearrange("p (t d) -> p t d", t=4), ALU.add, r=[rp2] + R_h, w=R_h)
                    stage(51)
                    def fnorm_gen(tb):
                        src = h3[:, tb, :]
                        xt, rx = xn_rot.get()
                        ss, rss = sm_rot.get()
                        act(xt[:, :], src, AF.Square, r=[R_h[tb]], w=[rx, rss], accum=ss)
                        yield
                        v, rv = sm_rot.get()
                        ts("pool", v, ss, 1.0 / D, EPS, ALU.mult, ALU.add, r=[rss], w=[rv])
                        yield
                        rstd, rr = sm_rot.get()
                        tt("pool", rstd, v, negh, ALU.pow, r=[rv, R_const], w=[rr])
                        yield
                        stt(src, src, rstd, gfin, ALU.mult, ALU.mult, r=[R_h[tb], rr, R_const], w=[R_h[tb]])
                        yield
                        o = dma("pool", "y%d" % tb, y_d[b, r0 + tb * 128:r0 + (tb + 1) * 128, :], src, r=[R_h[tb]], w=[])
                        final_ops.append(o)
                        yield
                    run_all(fnorm_gen(0), fnorm_gen(1))
                    run_all(fnorm_gen(2), fnorm_gen(3))

        except StopBuild:
            final_ops = [dma("pool", "y0", y_d[0, 0:128, :], hbuf[:, 0:1024], r=R_h, w=[])]
        P.emit(nc, final_ops[-4:])
    return nc


def host_consts(S):
    T = NMETA + S
    ident = np.eye(128, dtype=np.float32)
    kk = np.arange(128)
    tri = (kk[None, :] >= kk[:, None]).astype(np.float32)
    perm = np.zeros((128, 128), np.float32)
    perm[kk ^ 32, kk] = 1.0
    cmat = np.concatenate([ident, tri, perm], axis=1)
    inv = (1.0 / (np.float32(10000.0) ** (np.arange(0, 64, 2, dtype=np.float32) / np.float32(64)))).astype(np.float32)
    ang = (np.arange(T, dtype=np.float32)[:, None] * inv[None, :]).astype(np.float32)
    cos = np.cos(ang).astype(np.float32).T
    sin = np.sin(ang).astype(np.float32).T
    i = kk % 32
    sgn = np.where((kk % 64) < 32, -1.0, 1.0).astype(np.float32)
    rcos = np.ascontiguousarray(cos[i, :])
    rsin = np.ascontiguousarray(sin[i, :] * sgn[:, None])
    return cmat, rcos, rsin


def host_layout(inputs):
    f = lambda a: np.asarray(a, dtype=np.float32)
    fm = lambda v: np.ascontiguousarray(f(v).reshape(8, 128).T)
    pvec = np.zeros((128, 96), np.float32)
    pvec[:, 0:8] = fm(inputs["g_mix"][0])
    pvec[:, 8:16] = fm(inputs["g_mlp"][0])
    cw = f(inputs["conv_w"][0])
    for t in range(4):
        pvec[:, 16 + 8 * t:24 + 8 * t] = fm(cw[t])
    pvec[:, 48:56] = fm(inputs["conv_b"][0])
    pvec[:, 56:64] = fm(inputs["lru_lambda"][0])
    pvec[:, 64:72] = np.ascontiguousarray(f(inputs["b_a"][0]).T)
    pvec[:, 72:80] = np.ascontiguousarray(f(inputs["b_x"][0]).T)
    pbc = np.zeros((128, 1024 + 128 + 256), np.float32)
    pbc[:, 0:1024] = f(inputs["g_final"])[None, :]
    pbc[:, 1024:1152] = f(inputs["g_subln"][0])[None, :]
    pbc[:, 1152:1216] = f(inputs["lam_q1"][0])[None, :]
    pbc[:, 1216:1280] = f(inputs["lam_k1"][0])[None, :]
    pbc[:, 1280:1344] = f(inputs["lam_q2"][0])[None, :]
    pbc[:, 1344:1408] = f(inputs["lam_k2"][0])[None, :]
    wa = np.transpose(f(inputs["w_a"][0]), (1, 0, 2)).reshape(128, 1024)
    wx = np.transpose(f(inputs["w_x"][0]), (1, 0, 2)).reshape(128, 1024)
    wgate = np.ascontiguousarray(np.concatenate([wa, wx], axis=1))
    return pvec, pbc, wgate


_CACHE = {}


def run(inputs, S, NB, ncores):
    key = (S, NB)
    if key not in _CACHE:
        _CACHE[key] = build(S, NB)
    nc = _CACHE[key]
    cmat, rcos, rsin = host_consts(S)
    pvec, pbc, wgate = host_layout(inputs)
    f = lambda a: np.ascontiguousarray(np.asarray(a, dtype=np.float32))
    x = f(inputs["x"])
    common = {
        "meta": f(inputs["meta_tokens"]),
        "w_in": f(inputs["w_in"][0]), "w_rnn_out": f(inputs["w_rnn_out"][0]),
        "w_attn_out": f(inputs["w_attn_out"][0]), "w_o": f(inputs["w_o"][0]),
        "w_ff1": f(inputs["w_ff1"][0]), "w_ff2": f(inputs["w_ff2"][0]),
        "wgate": wgate, "pvec": pvec, "pbc": pbc, "cmat": cmat, "rcos": rcos, "rsin": rsin,
    }
    in_maps = []
    for c in range(ncores):
        m = dict(common)
        m["x"] = np.ascontiguousarray(x[c * NB:(c + 1) * NB])
        in_maps.append(m)
    res = run_bass_kernel_spmd(nc, in_maps, core_ids=list(range(ncores)))
    return np.concatenate([np.asarray(r["y"], dtype=np.float32) for r in res.results], axis=0)


def kernel(**inputs):
    return run(inputs, 4096, 2, 8)
```

```python
import contextlib
import numpy as np
import concourse.bass as bass
import concourse.mybir as mybir
from concourse.bass_utils import run_bass_kernel_spmd

F32 = mybir.dt.float32
BF16 = mybir.dt.bfloat16
AF = mybir.ActivationFunctionType
ALU = mybir.AluOpType
AX = mybir.AxisListType

D = 1024
NMETA = 16
EPS = 1e-6
LAM_INIT = 0.8 - 0.6 * 1.0
GC0 = 0.7978845608028654
GC1 = 0.044715
NSLOT = 36


class Reg:
    __slots__ = ("name", "w", "rs", "excl", "also")

    def __init__(self, name, excl=False):
        self.name = name
        self.w = None
        self.rs = {}
        self.also = ()
        self.excl = excl


class Op:
    __slots__ = ("eng", "fn", "chan", "idx", "eidx", "deps", "sig", "tok", "stream")


class Prog:
    ENGS = ("pe", "act", "dve", "pool", "sp")

    def __init__(self):
        self.ops = []
        self.eng_ops = {e: [] for e in self.ENGS}
        self.chan_last = {}
        self.chan_cnt = {}

    def _add(self, eng, fn, r, w, chan=None):
        op = Op()
        op.eng = eng
        op.fn = fn
        op.chan = chan
        op.idx = len(self.ops)
        op.eidx = len(self.eng_ops[eng])
        op.sig = chan is not None
        op.tok = None
        op.stream = ("c", chan) if chan is not None else eng
        r = list(r) + [a for R in r for a in R.also]
        w = list(w) + [a for R in w for a in R.also]
        if any(R.excl for R in r):
            w = list(w) + [R for R in r if R.excl and R not in w]
            r = [R for R in r if not R.excl]
        deps = set()
        for R in r:
            if R.w is not None:
                deps.add(R.w)
        for R in w:
            if R.w is not None:
                deps.add(R.w)
            for x in R.rs.values():
                deps.add(x)
        if chan is not None and chan in self.chan_last:
            deps.add(self.chan_last[chan])
        deps.discard(op.idx)
        best = {}
        for d in deps:
            p = self.ops[d]
            if p.stream not in best or best[p.stream] < d:
                best[p.stream] = d
        final = []
        for st, d in best.items():
            p = self.ops[d]
            if chan is None and st == eng:
                if eng == "pe":
                    continue
            final.append(d)
            p.sig = True
        op.deps = final
        for R in r:
            R.rs[op.stream] = op.idx
        for R in w:
            R.w = op.idx
            R.rs = {}
        if chan is not None:
            self.chan_last[chan] = op.idx
            self.chan_cnt[chan] = self.chan_cnt.get(chan, 0) + 1
        self.ops.append(op)
        self.eng_ops[eng].append(op)
        return op

    def op(self, eng, fn, r=(), w=()):
        return self._add(eng, fn, r, w)

    def dma(self, eng, chan, fn, r=(), w=()):
        return self._add(eng, fn, r, w, chan=chan)

    def emit(self, nc, final_waits):
        with contextlib.ExitStack() as st:
            esem = {e: st.enter_context(nc.semaphore("s_" + e)) for e in ("pe", "act", "dve", "pool")}
            csem = {c: st.enter_context(nc.semaphore("c_" + c)) for c in self.chan_cnt}
            cnt = {e: 0 for e in esem}
            ccnt = {c: 0 for c in csem}
            for op in self.ops:
                if op.chan is not None:
                    ccnt[op.chan] += 16
                    op.tok = (csem[op.chan], ccnt[op.chan], "c_" + op.chan)
                elif op.sig:
                    cnt[op.eng] += 1
                    op.tok = (esem[op.eng], cnt[op.eng], "s_" + op.eng)
            ops = self.ops
            eng_ops = self.eng_ops

            def run(engname, e, extra=None):
                waited = {}
                for op in eng_ops[engname]:
                    for d in op.deps:
                        sem, val, key = ops[d].tok
                        if waited.get(key, 0) >= val:
                            continue
                        e.wait_ge(sem, val)
                        waited[key] = val
                    ins = op.fn(e)
                    if op.chan is not None:
                        ins.then_inc(op.tok[0], 16)
                    elif op.sig:
                        ins.then_inc(op.tok[0], 1)
                if extra:
                    for o in extra:
                        sem, val, key = o.tok
                        if waited.get(key, 0) >= val:
                            continue
                        e.wait_ge(sem, val)
                        waited[key] = val

            with nc.Block() as block:
                @block.tensor
                def _(e):
                    run("pe", e)

                @block.scalar
                def _(e):
                    run("act", e)

                @block.vector
                def _(e):
                    run("dve", e)

                @block.gpsimd
                def _(e):
                    run("pool", e)

                @block.sync
                def _(e):
                    run("sp", e, extra=final_waits)


class Rot:
    def __init__(self, items):
        self.items = items
        self.i = 0

    def get(self):
        it = self.items[self.i % len(self.items)]
        self.i += 1
        return it


class StopBuild(Exception):
    pass


def build(S, NB, dbg=0):
    NCH = S // 512
    NKB = S // 128
    T = NMETA + S
    nc = bass.Bass("TRN2", target_bir_lowering=False)
    P = Prog()

    def din(name, shape, dt=F32):
        return nc.dram_tensor(name, list(shape), dt, kind="ExternalInput").ap()

    x_d = din("x", [NB, S, D])
    meta_d = din("meta", [NMETA, D])
    w_in_d = din("w_in", [D, 7 * D])
    w_rnn_d = din("w_rnn_out", [D, D])
    w_att_d = din("w_attn_out", [D, D])
    w_o_d = din("w_o", [D, D])
    w_ff1_d = din("w_ff1", [D, 4 * D])
    w_ff2_d = din("w_ff2", [4 * D, D])
    wg_d = din("wgate", [128, 2 * 8 * 128])
    pv_d = din("pvec", [128, 96])
    pb_d = din("pbc", [128, 1024 + 128 + 256])
    cm_d = din("cmat", [128, 3 * 128])
    cos_d = din("rcos", [128, T])
    sin_d = din("rsin", [128, T])
    y_d = nc.dram_tensor("y", [NB, S, D], F32, kind="ExternalOutput").ap()
    wbf_d = nc.dram_tensor("wbf", [NSLOT, 128, 4096], BF16, kind="Internal").ap()
    kc_d = nc.dram_tensor("kcache", [8, 128, S], BF16, kind="Internal").ap()

    with contextlib.ExitStack() as es:
        def sb(name, cols, dt):
            return es.enter_context(nc.sbuf_tensor(name, [128, cols], dt))

        vc = sb("vc", NKB * 8 * 129, BF16)
        vmeta = sb("vmeta", 8 * 129, BF16)
        kmeta = sb("kmeta", 8 * 32, BF16)
        wg = sb("wg", 2 * 8 * 128, BF16)
        cb = sb("cb", 3 * 128, BF16)
        identf = sb("identf", 128, F32)
        pv = sb("pv", 96, F32)
        pd = sb("pd", 64, F32)
        pbt = sb("pbt", 1024 + 128 + 256, F32)
        kb = sb("kb", 2 * S, BF16)
        hbuf = sb("hbuf", 4 * 1024, F32)
        hnT = sb("hnT", 8 * 512, BF16)
        big = sb("big", 32 * 512, BF16)
        wr = sb("wr", 3 * 4096, BF16)
        WT = 520
        Wt = sb("Wt", 8 * WT, F32)
        rc = sb("rc", 2 * 512, F32)
        et = sb("et", 2 * 1024, BF16)
        sm = sb("sm", 64, F32)
        xn = sb("xn", 2 * 1024, BF16)
        kth = sb("kth", 2 * 512, BF16)
        xcbt = sb("xcbt", 2 * 512, BF16)
        state = sb("state", 8 + 24 + 8 + 24, F32)
        psb = [es.enter_context(nc.psum_tensor("ps%d" % i, [128, 512], F32)) for i in range(8)]

        R_ps = [Reg("ps%d" % i, excl=True) for i in range(8)]
        R_h = [Reg("h%d" % i) for i in range(4)]
        R_hnT = [Reg("hnT%d" % i) for i in range(8)]
        R_big = [Reg("big%d" % i) for i in range(32)]
        R_wr = [Reg("wr%d" % i) for i in range(3)]
        R_W = [Reg("W%d" % i) for i in range(8)]
        R_rc = Reg("rc")
        R_et = [Reg("et0"), Reg("et1")]
        R_etc = [[Reg("et%d_%d" % (i, c)) for c in range(2)] for i in range(2)]
        for i_ in range(2):
            R_et[i_].also = tuple(R_etc[i_])
        R_xn = [Reg("xn0"), Reg("xn1")]
        R_kth = [Reg("kth0"), Reg("kth1")]
        R_xcb = [Reg("xcb0"), Reg("xcb1")]
        R_kb = [Reg("kb0"), Reg("kb1")]
        R_kd = [Reg("kd%d" % i) for i in range(8)]
        R_wbf = [Reg("wbf%d" % i) for i in range(NSLOT)]
        R_vc = [Reg("vc%d" % i) for i in range(NKB)]
        R_vmeta = Reg("vmeta")
        R_kmeta = Reg("kmeta")
        R_const = Reg("const")
        R_state = [Reg("st%d" % i) for i in range(8)]
        R_state0 = Reg("state0")
        R_sm = [Reg("sm%d" % i) for i in range(64)]

        h3 = hbuf[:, :].rearrange("p (t d) -> p t d", t=4)
        hnT3 = hnT[:, :].rearrange("p (k t) -> p k t", k=8)
        big3 = big[:, :].rearrange("p (i t) -> p i t", i=32)
        wr3 = [wr[:, s * 4096:(s + 1) * 4096] for s in range(3)]
        Wv = [Wt[:, i * WT:(i + 1) * WT] for i in range(8)]
        vc4 = vc[:, :].rearrange("p (b h e) -> p b h e", b=NKB, h=8)
        vmeta3 = vmeta[:, :].rearrange("p (h e) -> p h e", h=8)
        kmeta3 = kmeta[:, :].rearrange("p (h t) -> p h t", h=8)
        wg4 = wg[:, :].rearrange("p (w n d) -> p w n d", w=2, n=8)
        ident_b = cb[:, 0:128]
        tri_b = cb[:, 128:256]
        perm_b = cb[:, 256:384]
        gfin = pbt[:, 0:1024]
        gsub = pbt[:, 1024:1152]
        g_mix = pv[:, 0:8]
        g_mlp = pv[:, 8:16]
        conv_w = pv[:, 16:48].rearrange("p (t n) -> p t n", t=4)
        conv_b = pv[:, 48:56]
        hcl = pd[:, 0:8]
        hba = pd[:, 8:16]
        hbx = pd[:, 16:24]
        nlam = pd[:, 24:25]
        negh = pd[:, 25:26]
        halfc = pd[:, 26:27]
        hstate = state[:, 0:8]
        carry = state[:, 8:32].rearrange("p (n k) -> p n k", n=8)
        hstate0 = state[:, 32:40]
        carry0 = state[:, 40:64].rearrange("p (n k) -> p n k", n=8)
        ps_bf = [p[:, :].bitcast(BF16) for p in psb]

        sm_rot = Rot([(sm[:, i:i + 1], R_sm[i]) for i in range(64)])
        W_rot = Rot([(Wv[i], R_W[i]) for i in range(8)])
        ps_rot = Rot([(psb[i], R_ps[i], ps_bf[i]) for i in range(8)])
        xn_rot = Rot([(xn[:, i * 1024:(i + 1) * 1024], R_xn[i]) for i in range(2)])
        kth_rot = Rot([(kth[:, i * 512:(i + 1) * 512], R_kth[i]) for i in range(2)])

        def act(out, in_, func, r, w, bias=None, scale=None, accum=None):
            kw = {}
            if bias is not None:
                kw["bias"] = bias
            if scale is not None:
                kw["scale"] = scale
            if accum is not None:
                kw["accum_out"] = accum
            return P.op("act", lambda e: e.activation(out=out, in_=in_, func=func, **kw), r=r, w=w)

        def ts(eng, out, in0, s1, s2, op0, op1, r, w):
            if op1 is None:
                return P.op(eng, lambda e: e.tensor_scalar(out=out, in0=in0, scalar1=s1, scalar2=None, op0=op0), r=r, w=w)
            return P.op(eng, lambda e: e.tensor_scalar(out=out, in0=in0, scalar1=s1, scalar2=s2, op0=op0, op1=op1), r=r, w=w)

        def stt(out, in0, scalar, in1, op0, op1, r, w):
            return P.op("dve", lambda e: e.scalar_tensor_tensor(out=out, in0=in0, scalar=scalar, in1=in1, op0=op0, op1=op1), r=r, w=w)

        def tt(eng, out, in0, in1, op, r, w):
            return P.op(eng, lambda e: e.tensor_tensor(out=out, in0=in0, in1=in1, op=op), r=r, w=w)

        def cp(eng, out, in_, r, w):
            if eng == "act":
                return P.op("act", lambda e: e.copy(out=out, in_=in_), r=r, w=w)
            return P.op(eng, lambda e: e.tensor_copy(out=out, in_=in_), r=r, w=w)

        def mm(out, lhsT, rhs, start, stop, r, w, skip=False):
            return P.op("pe", lambda e: e.matmul(out, lhsT, rhs, start=start, stop=stop, skip_group_check=skip), r=r, w=w)

        def tr(out, in_, ident, r, w):
            return P.op("pe", lambda e: e.transpose(out, in_, ident), r=r, w=w)

        def dma(eng, chan, out, in_, r, w, slow=False):
            if slow:
                return P.dma(eng, chan, lambda e: e.dma_start(out=out, in_=in_, allow_slow_non_contiguous=True), r=r, w=w)
            return P.dma(eng, chan, lambda e: e.dma_start(out=out, in_=in_), r=r, w=w)

        def rstd_from_ss(ss_ap, ss_reg, inv_n, npart=128):
            v, rv = sm_rot.get()
            ts("pool", v[:npart, :], ss_ap, inv_n, EPS, ALU.mult, ALU.add, r=[ss_reg], w=[rv])
            o, ro = sm_rot.get()
            tt("pool", o[:npart, :], v[:npart, :], negh[:npart, :], ALU.pow, r=[rv, R_const], w=[ro])
            return o, ro

        final_ops = []
        def stage(k):
            if dbg == k:
                raise StopBuild()
        try:
            st_f = Wv[0][:, 0:384]
            dma("pool", "su0", st_f, cm_d, r=[], w=[R_W[0]])
            cp("dve", cb[:, :], st_f, r=[R_W[0]], w=[R_const])
            dma("pool", "su1", identf[:, :], cm_d[:, 0:128], r=[], w=[R_const])
            dma("pool", "su0", pv[:, :], pv_d, r=[], w=[R_const])
            dma("pool", "su1", pbt[:, :], pb_d, r=[], w=[R_const])
            for half in range(2):
                for q in range(2):
                    wt_, rw_ = W_rot.get()
                    dma("pool", "su%d" % q, wt_[:, 0:512], wg_d[:, half * 1024 + q * 512: half * 1024 + (q + 1) * 512], r=[], w=[rw_])
                    cp("dve", wg[:, half * 1024 + q * 512: half * 1024 + (q + 1) * 512], wt_[:, 0:512], r=[rw_], w=[R_const])
            P.op("dve", lambda e: e.memset(negh, -0.5), r=[], w=[R_const])
            P.op("dve", lambda e: e.memset(halfc, 0.5), r=[], w=[R_const])
            lam_raw = pv[:, 56:64]
            tmp8 = pd[:, 32:40]
            act(tmp8, lam_raw, AF.Exp, r=[R_const], w=[R_const], scale=-1.0)
            act(tmp8, tmp8, AF.Ln, r=[R_const], w=[R_const], bias=1.0)
            ts("dve", hcl, tmp8, -4.0, None, ALU.mult, None, r=[R_const], w=[R_const])
            ts("dve", hba, pv[:, 64:72], 0.5, None, ALU.mult, None, r=[R_const], w=[R_const])
            ts("dve", hbx, pv[:, 72:80], 0.5, None, ALU.mult, None, r=[R_const], w=[R_const])
            lq = pbt[:, 1152:1408]
            prod = pd[:, 40:42]
            tl = Wv[1][:, 0:128]
            tt("dve", tl[:, 0:64], lq[:, 0:64], lq[:, 64:128], ALU.mult, r=[R_const], w=[R_W[1]])
            tt("dve", tl[:, 64:128], lq[:, 128:192], lq[:, 192:256], ALU.mult, r=[R_const], w=[R_W[1]])
            P.op("dve", lambda e: e.tensor_reduce(out=prod, in_=tl.rearrange("p (a b) -> p a b", a=2), axis=AX.X, op=ALU.add), r=[R_W[1]], w=[R_const])
            act(prod, prod, AF.Exp, r=[R_const], w=[R_const])
            tt("dve", nlam, prod[:, 1:2], prod[:, 0:1], ALU.subtract, r=[R_const], w=[R_const])
            ts("dve", nlam, nlam, -LAM_INIT, None, ALU.add, None, r=[R_const], w=[R_const])
            ts("dve", gsub, gsub, 1.0 - LAM_INIT, None, ALU.mult, None, r=[R_const], w=[R_const])
            P.op("pool", lambda e: e.memset(vc4[:, :, :, 128:129], 1.0), r=[], w=R_vc)

            stage(1)
            w_in3 = w_in_d.rearrange("(k p) c -> p k c", p=128)
            w_rnn3 = w_rnn_d.rearrange("(k p) c -> p k c", p=128)
            w_att3 = w_att_d.rearrange("(k p) c -> p k c", p=128)
            w_o3 = w_o_d.rearrange("(k p) c -> p k c", p=128)
            w_ff13 = w_ff1_d.rearrange("(k p) c -> p k c", p=128)
            w_ff23 = w_ff2_d.rearrange("(k p) c -> p k c", p=128)

            def win(c0, n=512):
                return [(w_in3[:, :, c0:c0 + n], 8, n)]

            slot_src = []
            for j in range(4):
                slot_src.append([(w_in3[:, :, 256 * j:256 * j + 256], 8, 256), (w_in3[:, :, 1024 + 256 * j:1024 + 256 * j + 256], 8, 256)])
            for j in range(2):
                slot_src.append(win(5120 + 512 * j))
                slot_src.append([(w_rnn3[:, :, 512 * j:512 * j + 512], 8, 512)])
            for j in range(2):
                slot_src.append(win(2048 + 512 * j))
            for j in range(2):
                slot_src.append(win(3072 + 512 * j))
            for j in range(2):
                slot_src.append(win(4096 + 512 * j))
            for j in range(2):
                slot_src.append(win(6144 + 512 * j))
                slot_src.append([(w_att3[:, :, 512 * j:512 * j + 512], 8, 512)])
            for j in range(2):
                slot_src.append([(w_o3[:, :, 512 * j:512 * j + 512], 8, 512)])
            for j in range(8):
                slot_src.append([(w_ff13[:, :, 512 * j:512 * j + 512], 8, 512)])
            for j in range(8):
                slot_src.append([(w_ff23[:, 8 * q:8 * q + 8, 128 * j:128 * j + 128], 8, 128) for q in range(4)])
            assert len(slot_src) == NSLOT
            SL_A, SL_G, SL_R, SL_Q, SL_K, SL_V, SL_GA, SL_AO, SL_O, SL_F, SL_H = 0, 4, 5, 8, 10, 12, 14, 15, 18, 20, 28

            stg = [(hbuf[:, :], R_h), (Wt[:, 0:4096], R_W)]
            cast_eng = ["act", "dve", "pool"]
            for s in range(NSLOT):
                sbuf_f, regs_f = stg[s % 2]
                off = 0
                if s < 4:
                    st3 = sbuf_f[:, 0:4096].rearrange("p (a b) -> p a b", a=8)
                    for i_, (src, a, b) in enumerate(slot_src[s]):
                        dma("sp", "w%d" % (s % 2), st3[:, :, 256 * i_:256 * i_ + 256], src, r=[], w=regs_f)
                else:
                    for (src, a, b) in slot_src[s]:
                        dst = sbuf_f[:, off:off + a * b].rearrange("p (a b) -> p a b", a=a)
                        dma("sp", "w%d" % (s % 2), dst, src, r=[], w=regs_f)
                        off += a * b
                    assert off == 4096
                ring = s % 3
                for q in range(3):
                    lo, hi = [0, 1408, 2752][q], [1408, 2752, 4096][q]
                    cp(cast_eng[q], wr3[ring][:, lo:hi], sbuf_f[:, lo:hi], r=regs_f, w=[R_wr[ring]])
                dma("pool", "su%d" % (s % 2), wbf_d[s], wr3[ring], r=[R_wr[ring]], w=[R_wbf[s]])

            stage(2)
            wcount = [0]

            def wload(slot):
                ring = wcount[0] % 3
                wcount[0] += 1
                dma("sp", "w%d" % ring, wr3[ring], wbf_d[slot], r=[R_wbf[slot]], w=[R_wr[ring]])
                return wr3[ring].rearrange("p (k c) -> p k c", k=8), R_wr[ring], wr3[ring]

            def norm_to_hnT(src_tiles, ntok, gvec, nblk):
                def gen(tb):
                    src, rsrc = src_tiles[tb]
                    xt, rx = xn_rot.get()
                    ss, rss = sm_rot.get()
                    act(xt[:ntok, :], src, AF.Square, r=[rsrc], w=[rx, rss], accum=ss[:ntok, :])
                    yield
                    v, rv = sm_rot.get()
                    ts("pool", v[:ntok, :], ss[:ntok, :], 1.0 / D, EPS, ALU.mult, ALU.add, r=[rss], w=[rv])
                    yield
                    rstd, rr = sm_rot.get()
                    tt("pool", rstd[:ntok, :], v[:ntok, :], negh[:ntok, :], ALU.pow, r=[rv, R_const], w=[rr])
                    yield
                    ts("dve", xt[:ntok, :], src, rstd[:ntok, :], None, ALU.mult, None, r=[rsrc, rr], w=[rx])
                    yield
                    pst, rps, psbf = ps_rot.get()
                    for kc in range(8):
                        tr(psbf[:, kc * ntok:(kc + 1) * ntok], xt[:ntok, kc * 128:(kc + 1) * 128], ident_b[:ntok, :ntok], r=[rx, R_const], w=[rps])
                    yield
                    outv = hnT3[:, :, tb * 128:tb * 128 + ntok]
                    inv = psbf[:, 0:8 * ntok].rearrange("p (k t) -> p k t", k=8)
                    gb = gvec.unsqueeze(2).to_broadcast([128, 8, ntok])
                    tt("dve", outv, inv, gb, ALU.mult, r=[rps, R_const], w=R_hnT)
                    yield
                for t0 in range(0, nblk, 2):
                    run_all(*[gen(tb) for tb in range(t0, min(t0 + 2, nblk))])

            def proj_fm(slot3, rslot, cb_, ntok, pst, rps):
                for kc in range(8):
                    mm(pst[:, 0:ntok], slot3[:, kc, cb_ * 128:(cb_ + 1) * 128], hnT3[:, kc, 0:ntok], kc == 0, kc == 7, r=[rslot, R_hnT[kc]], w=[rps])

            def rnn_gen(n, ntok, xr_slot, xr_cb, gr_slot, gr_cb, rx_slot, rg_slot, with_y, lane):
                (T1, r1), (T2, r2), (T3, r3), (T4, r4) = lane["W"]
                xcb, rxcb = lane["xcb"]
                psr = lane["ps"]
                pst, rps, _ = psr.get()
                proj_fm(xr_slot, rx_slot, xr_cb, ntok, pst, rps)
                yield
                xr, rxr = T1, r1
                cp("pool", xr[:, 0:3], carry[:, n, :], r=[R_state[n]], w=[rxr])
                cp("act", xr[:, 3:3 + ntok], pst[:, 0:ntok], r=[rps], w=[rxr])
                yield
                xc, rxc = T2, r2
                ts("dve", xc[:, 0:ntok], xr[:, 3:3 + ntok], conv_w[:, 3, n:n + 1], conv_b[:, n:n + 1], ALU.mult, ALU.add, r=[rxr, R_const], w=[rxc])
                for k in range(3):
                    stt(xc[:, 0:ntok], xr[:, k:k + ntok], conv_w[:, k, n:n + 1], xc[:, 0:ntok], ALU.mult, ALU.add, r=[rxr, rxc, R_const], w=[rxc])
                cp("pool", carry[:, n, :], xr[:, ntok:ntok + 3], r=[rxr], w=[R_state[n]])
                cp("pool", xcb[:, 0:ntok], xc[:, 0:ntok], r=[rxc], w=[rxcb])
                yield
                pa, rpa, _ = psr.get()
                mm(pa[:, 0:ntok], wg4[:, 0, n, :], xcb[:, 0:ntok], True, True, r=[rxcb, R_const], w=[rpa])
                pi, rpi, _ = psr.get()
                mm(pi[:, 0:ntok], wg4[:, 1, n, :], xcb[:, 0:ntok], True, True, r=[rxcb, R_const], w=[rpi])
                yield
                ta, rta = T1, r1
                ti, rti = T3, r3
                act(ta[:, 0:ntok], pa[:, 0:ntok], AF.Tanh, r=[rpa, R_const], w=[rta], bias=hba[:, n:n + 1], scale=0.5)
                act(ti[:, 0:ntok], pi[:, 0:ntok], AF.Tanh, r=[rpi, R_const], w=[rti], bias=hbx[:, n:n + 1], scale=0.5)
                act(ta[:, 0:ntok], ta[:, 0:ntok], AF.Exp, r=[rta, R_const], w=[rta], bias=hcl[:, n:n + 1], scale=hcl[:, n:n + 1])
                yield
                sq, rsq = T4, r4
                tt("pool", sq[:, 0:ntok], ta[:, 0:ntok], ta[:, 0:ntok], ALU.mult, r=[rta], w=[rsq])
                act(sq[:, 0:ntok], sq[:, 0:ntok], AF.Sqrt, r=[rsq], w=[rsq], bias=1.0, scale=-1.0)
                stt(ti[:, 0:ntok], ti[:, 0:ntok], 1.0, xc[:, 0:ntok], ALU.add, ALU.mult, r=[rti, rxc], w=[rti])
                yield
                stt(ti[:, 0:ntok], ti[:, 0:ntok], 0.5, sq[:, 0:ntok], ALU.mult, ALU.mult, r=[rti, rsq], w=[rti])
                hh, rhh = T2, r2
                P.op("dve", lambda e: e.tensor_tensor_scan(out=hh[:, 0:ntok], data0=ta[:, 0:ntok], data1=ti[:, 0:ntok], initial=hstate[:, n:n + 1], op0=ALU.mult, op1=ALU.add),
                     r=[rta, rti, R_state[n]], w=[rhh])
                cp("pool", hstate[:, n:n + 1], hh[:, ntok - 1:ntok], r=[rhh], w=[R_state[n]])
                yield
                if not with_y:
                    return
                pg, rpg, _ = psr.get()
                proj_fm(gr_slot, rg_slot, gr_cb, ntok, pg, rpg)
                yield
                g2, rg2 = T4, r4
                act(g2[:, 0:ntok], pg[:, 0:ntok], AF.Square, r=[rpg], w=[rg2])
                ts("dve", g2[:, 0:ntok], g2[:, 0:ntok], GC1 * GC0, GC0, ALU.mult, ALU.add, r=[rg2], w=[rg2])
                tt("dve", g2[:, 0:ntok], g2[:, 0:ntok], pg[:, 0:ntok], ALU.mult, r=[rg2, rpg], w=[rg2])
                yield
                act(g2[:, 0:ntok], g2[:, 0:ntok], AF.Tanh, r=[rg2], w=[rg2])
                stt(g2[:, 0:ntok], g2[:, 0:ntok], 1.0, pg[:, 0:ntok], ALU.add, ALU.mult, r=[rg2, rpg], w=[rg2])
                stt(big3[:, 8 + n, :], g2[:, 0:ntok], 0.5, hh[:, 0:ntok], ALU.mult, ALU.mult, r=[rg2, rhh], w=[R_big[8 + n]])
                yield

            def lockstep(*gens):
                gens = list(gens)
                while gens:
                    for g in list(gens):
                        try:
                            next(g)
                        except StopIteration:
                            gens.remove(g)
                    yield

            def run_all(*gens):
                for _ in lockstep(*gens):
                    pass

            ring_owner = [None, None, None]

            def wload_g(slot, sid):
                for r_ in range(3):
                    if ring_owner[r_] == sid:
                        ring_owner[r_] = None
                while ring_owner[wcount[0] % 3] is not None:
                    yield
                ring = wcount[0] % 3
                res = wload(slot)
                ring_owner[ring] = sid
                return res

            def release(sid):
                for r_ in range(3):
                    if ring_owner[r_] == sid:
                        ring_owner[r_] = None

            def mk_lane(i):
                return {"W": [(Wv[4 * i + k], R_W[4 * i + k]) for k in range(4)],
                        "xcb": (xcbt[:, i * 512:(i + 1) * 512], R_xcb[i]),
                        "ps": Rot([(psb[3 * i + k], R_ps[3 * i + k], ps_bf[3 * i + k]) for k in range(3)])}
            lanes = [mk_lane(0), mk_lane(1)]
            et_f = et[:, :].bitcast(F32)
            qkv_res = {"W": Rot([(et_f[:, i * 512:(i + 1) * 512], R_et[i]) for i in range(2)]),
                       "ps": Rot([(psb[6 + k], R_ps[6 + k], ps_bf[6 + k]) for k in range(2)])}
            dflt_res = {"W": W_rot, "ps": ps_rot}

            def rope_block(pst, rps, ntok, dst, rdst, res=None):
                res = res or dflt_res
                qb, rqb = xn_rot.get()
                cp("act", qb[:, 0:ntok], pst[:, 0:ntok], r=[rps], w=[rqb])
                p2, rp2, _ = res["ps"].get()
                mm(p2[:, 0:ntok], perm_b, qb[:, 0:ntok], True, True, r=[rqb, R_const], w=[rp2])
                t1, rt1 = res["W"].get()
                tt("dve", t1[:, 0:ntok], rc[:, 0:ntok], pst[:, 0:ntok], ALU.mult, r=[rps, rqb, R_rc], w=[rt1])
                t2, rt2 = res["W"].get()
                tt("dve", t2[:, 0:ntok], rc[:, 512:512 + ntok], p2[:, 0:ntok], ALU.mult, r=[rp2, R_rc], w=[rt2])
                tt("pool", dst, t1[:, 0:ntok], t2[:, 0:ntok], ALU.add, r=[rt1, rt2], w=rdst)

            P.op("pool", lambda e: e.memset(hbuf[:32, 0:1024], 0.0), r=[], w=[R_h[0]])
            P.op("pool", lambda e: e.memset(kmeta[:, :], 0.0), r=[], w=[R_kmeta])
            P.op("pool", lambda e: e.memset(vmeta[:32, :], 0.0), r=[], w=[R_vmeta])
            P.op("pool", lambda e: e.memset(vmeta3[:NMETA, :, 128:129], 1.0), r=[], w=[R_vmeta])
            dma("pool", "x0", hbuf[:NMETA, 0:1024], meta_d, r=[], w=[R_h[0]])
            dma("pool", "rc", rc[:, 0:NMETA], cos_d[:, 0:NMETA], r=[], w=[R_rc])
            dma("pool", "rc", rc[:, 512:512 + NMETA], sin_d[:, 0:NMETA], r=[], w=[R_rc])
            P.op("pool", lambda e: e.memset(state[:, :], 0.0), r=[], w=R_state)
            norm_to_hnT([(hbuf[:32, 0:1024], R_h[0])], 32, g_mix, 1)
            stage(31)
            for j in range(4):
                s3, rs_, _ = wload(SL_A + j)
                for q in range(2):
                    run_all(rnn_gen(2 * j + q, NMETA, s3, q, None, None, rs_, None, False, lanes[q]))
            stage(33)
            cp("pool", state[:, 32:64], state[:, 0:32], r=R_state, w=[R_state0])
            for j in range(2):
                s3, rs_, _ = wload(SL_K + j)
                for q in range(4):
                    hd = 4 * j + q
                    pst, rps, _ = ps_rot.get()
                    proj_fm(s3, rs_, q, NMETA, pst, rps)
                    rope_block(pst, rps, NMETA, kmeta3[:, hd, 0:NMETA], [R_kmeta])
            stage(34)
            for j in range(2):
                s3, rs_, _ = wload(SL_V + j)
                pst, rps, _ = ps_rot.get()
                for kc in range(8):
                    mm(pst[:32, :], hnT3[:, kc, 0:32], s3[:, kc, :], kc == 0, kc == 7, r=[rs_, R_hnT[kc]], w=[rps])
                cp("act", vmeta3[:NMETA, 4 * j:4 * j + 4, 0:128], pst[:NMETA, :].rearrange("p (h e) -> p h e", h=4), r=[rps], w=[R_vmeta])

            stage(3)
            def oacc(c, qs):
                i = c * 4 + qs
                bank = 5 + i // 3
                off = (i % 3) * 129
                return psb[bank][:, off:off + 129], R_ps[bank], (i % 3 == 0)

            for b in range(NB):
                cp("pool", state[:, 0:32], state[:, 32:64], r=[R_state0], w=R_state)
                for ci in range(NCH):
                    r0 = 512 * ci
                    p0 = NMETA + r0
                    for tb in range(4):
                        dma("pool", "x%d" % tb, h3[:, tb, :], x_d[b, r0 + tb * 128:r0 + (tb + 1) * 128, :], r=[], w=[R_h[tb]])
                    dma("pool", "rc", rc[:, 0:512], cos_d[:, p0:p0 + 512], r=[], w=[R_rc])
                    dma("pool", "rc", rc[:, 512:1024], sin_d[:, p0:p0 + 512], r=[], w=[R_rc])
                    norm_to_hnT([(h3[:, tb, :], R_h[tb]) for tb in range(4)], 128, g_mix, 4)
                    stage(40)
                    def rnn_stream(sid):
                        for j in range(4):
                            s3, rs_, _ = yield from wload_g(SL_A + j, sid)
                            yield from lockstep(rnn_gen(2 * j, 512, s3, 0, s3, 2, rs_, rs_, True, lanes[0]),
                                                rnn_gen(2 * j + 1, 512, s3, 1, s3, 3, rs_, rs_, True, lanes[1]))
                        release(sid)

                    def qkv_stream(sid):
                        for j in range(2):
                            s3, rs_, _ = yield from wload_g(SL_Q + j, sid)
                            for q in range(4):
                                hd = 4 * j + q
                                pst, rps, _ = qkv_res["ps"].get()
                                proj_fm(s3, rs_, q, 512, pst, rps)
                                yield
                                rope_block(pst, rps, 512, big3[:, hd, :], [R_big[hd]], qkv_res)
                                yield
                        for j in range(2):
                            s3, rs_, _ = yield from wload_g(SL_K + j, sid)
                            for q in range(4):
                                hd = 4 * j + q
                                pst, rps, _ = qkv_res["ps"].get()
                                proj_fm(s3, rs_, q, 512, pst, rps)
                                yield
                                kt, rkt = kth_rot.get()
                                rope_block(pst, rps, 512, kt, [rkt], qkv_res)
                                dma("sp", "kw%d" % (hd % 2), kc_d[hd, :, r0:r0 + 512], kt, r=[rkt], w=[R_kd[hd]])
                                yield
                        for j in range(2):
                            s3, rs_, _ = yield from wload_g(SL_V + j, sid)
                            for tb in range(4):
                                pst, rps, _ = qkv_res["ps"].get()
                                for kc in range(8):
                                    mm(pst[:, :], hnT3[:, kc, tb * 128:(tb + 1) * 128], s3[:, kc, :], kc == 0, kc == 7, r=[rs_, R_hnT[kc]], w=[rps])
                                blk = 4 * ci + tb
                                cp("act" if tb % 2 == 0 else "dve", vc4[:, blk, 4 * j:4 * j + 4, 0:128], pst[:, :].rearrange("p (h e) -> p h e", h=4), r=[rps], w=[R_vc[blk]])
                                yield
                        release(sid)

                    run_all(rnn_stream(1), qkv_stream(2))
                    stage(41)
                    for j in range(2):
                        g3, rg_, _ = wload(SL_G + 2 * j)
                        o3, ro_, _ = wload(SL_R + 2 * j)
                        for q in range(4):
                            db = 4 * j + q
                            pg, rpg, _ = ps_rot.get()
                            proj_fm(g3, rg_, q, 512, pg, rpg)
                            tg, rtg = W_rot.get()
                            act(tg[:, 0:512], pg[:, 0:512], AF.Tanh, r=[rpg], w=[rtg], scale=0.5)
                            py, rpy, _ = ps_rot.get()
                            for kc in range(8):
                                mm(py[:, :], o3[:, kc, q * 128:(q + 1) * 128], big3[:, 8 + kc, :], kc == 0, kc == 7, r=[ro_, R_big[8 + kc]], w=[rpy])
                            stt(big3[:, 24 + db, :], tg[:, 0:512], 1.0, py[:, :], ALU.add, ALU.mult, r=[rtg, rpy], w=[R_big[24 + db]])
                    stage(42)
                    stage(45)
                    nreal = 4 * ci + 4
                    kend = r0 + 512
                    for hd in range(8):
                        ks = hd % 2
                        kbv = kb[:, ks * S:ks * S + kend]
                        dma("sp", "k%d" % ks, kbv, kc_d[hd, :, 0:kend], r=[R_kd[hd]], w=[R_kb[ks]])
                        qT = big3[:, hd, :]
                        rq = R_big[hd]
                        def blk_params(kbk):
                            if kbk < 0:
                                kk, q_lo, kr = 32, 0, [R_kmeta]
                            else:
                                jj = kbk - 4 * ci
                                q_lo = 128 * jj if jj > 0 else 0
                                kk, kr = 128, [R_kb[ks]]
                            eslot = (kbk + 1) % 2
                            ev = et[:, eslot * 1024:(eslot + 1) * 1024].rearrange("p (c t) -> p c t", c=2)
                            return kk, q_lo, kr, eslot, ev

                        def qk_stage(kbk):
                            kk, q_lo, kr, eslot, ev = blk_params(kbk)
                            nq = 512 - q_lo
                            for c in range(2):
                                sbank = 2 * eslot + c
                                if kbk < 0:
                                    lhs = kmeta3[c * 64:(c + 1) * 64, hd, :]
                                else:
                                    lhs = kbv[c * 64:(c + 1) * 64, kbk * 128:(kbk + 1) * 128]
                                mm(psb[sbank][:kk, 0:nq], lhs, qT[c * 64:(c + 1) * 64, q_lo:512], True, True, r=kr + [rq], w=[R_ps[sbank]])
                            for c in range(2):
                                sbank = 2 * eslot + c
                                act(ev[:kk, c, q_lo:512], psb[sbank][:kk, 0:nq], AF.Exp, r=[R_ps[sbank]], w=[R_etc[eslot][c]], scale=0.125)
                            if kbk >= 0 and kbk >= 4 * ci:
                                jj = kbk - 4 * ci
                                for c in range(2):
                                    tt("pool", ev[:, c, jj * 128:(jj + 1) * 128], ev[:, c, jj * 128:(jj + 1) * 128], tri_b, ALU.mult, r=[R_const], w=[R_etc[eslot][c]])

                        def pv_stage(kbk):
                            kk, q_lo, kr, eslot, ev = blk_params(kbk)
                            if kbk >= 0 and kbk >= 4 * ci:
                                qs_list = list(range(kbk - 4 * ci, 4))
                            else:
                                qs_list = [0, 1, 2, 3]
                            for c in range(2):
                                for qs in qs_list:
                                    oa, ro, first_in_bank = oacc(c, qs)
                                    if kbk < 0:
                                        rhs = vmeta3[:32, hd, :]
                                        rv = R_vmeta
                                    else:
                                        rhs = vc4[:, kbk, hd, :]
                                        rv = R_vc[kbk]
                                    last = (kbk == 4 * ci + qs)
                                    mm(oa, ev[:kk, c, qs * 128:(qs + 1) * 128], rhs, (kbk < 0) and first_in_bank, last, r=[R_etc[eslot][c], rv], w=[ro], skip=True)

                        blocks = list(range(-1, nreal))
                        qk_stage(blocks[0])
                        for bi_, kbk in enumerate(blocks):
                            if bi_ + 1 < len(blocks):
                                qk_stage(blocks[bi_ + 1])
                            pv_stage(kbk)
                        stage(46)
                        ptr, rptr, ptr_bf = ps_rot.get()
                        while rptr in (R_ps[5], R_ps[6], R_ps[7]):
                            ptr, rptr, ptr_bf = ps_rot.get()
                        def fin_gen(qs):
                            a0, ra0, _ = oacc(0, qs)
                            a1, ra1, _ = oacc(1, qs)
                            ri0, rri0 = sm_rot.get()
                            P.op("dve", lambda e, o=ri0, i=a0[:, 128:129]: e.reciprocal(out=o, in_=i), r=[ra0], w=[rri0])
                            yield
                            ri1, rri1 = sm_rot.get()
                            P.op("dve", lambda e, o=ri1, i=a1[:, 128:129]: e.reciprocal(out=o, in_=i), r=[ra1], w=[rri1])
                            yield
                            ts("dve", ri1, ri1, nlam, None, ALU.mult, None, r=[rri1, R_const], w=[rri1])
                            yield
                            t1, rt1 = W_rot.get()
                            act(t1[:, 0:128], a1[:, 0:128], AF.Copy, r=[ra1, rri1], w=[rt1], scale=ri1)
                            yield
                            stt(t1[:, 128:256], a0[:, 0:128], ri0, t1[:, 0:128], ALU.mult, ALU.add, r=[ra0, rri0, rt1], w=[rt1])
                            yield
                            ss, rss = sm_rot.get()
                            act(t1[:, 256:384], t1[:, 128:256], AF.Square, r=[rt1], w=[rt1, rss], accum=ss)
                            yield
                            v_, rv_ = sm_rot.get()
                            ts("pool", v_, ss, 1.0 / 128, EPS, ALU.mult, ALU.add, r=[rss], w=[rv_])
                            yield
                            rstd, rr = sm_rot.get()
                            tt("pool", rstd, v_, negh, ALU.pow, r=[rv_, R_const], w=[rr])
                            yield
                            yab = t1[:, 384:448].bitcast(BF16)
                            stt(yab, t1[:, 128:256], rstd, gsub, ALU.mult, ALU.mult, r=[rt1, rr, R_const], w=[rt1])
                            yield
                            tr(ptr_bf[:, qs * 128:(qs + 1) * 128], yab, ident_b, r=[rt1, R_const], w=[rptr])
                            yield

                        run_all(*[fin_gen(qs) for qs in range(4)])
                        cp("act", big3[:, 16 + hd, :], ptr_bf[:, 0:512], r=[rptr], w=[R_big[16 + hd]])
                    stage(47)
                    for j in range(2):
                        g3, rg_, _ = wload(SL_GA + 2 * j)
                        o3, ro_, _ = wload(SL_AO + 2 * j)
                        for q in range(4):
                            db = 4 * j + q
                            pg, rpg, _ = ps_rot.get()
                            proj_fm(g3, rg_, q, 512, pg, rpg)
                            tg, rtg = W_rot.get()
                            act(tg[:, 0:512], pg[:, 0:512], AF.Tanh, r=[rpg], w=[rtg], scale=0.5)
                            py, rpy, _ = ps_rot.get()
                            for kc in range(8):
                                mm(py[:, :], o3[:, kc, q * 128:(q + 1) * 128], big3[:, 16 + kc, :], kc == 0, kc == 7, r=[ro_, R_big[16 + kc]], w=[rpy])
                            stt(tg[:, 0:512], tg[:, 0:512], 1.0, py[:, :], ALU.add, ALU.mult, r=[rtg, rpy], w=[rtg])
                            tt("pool", big3[:, 24 + db, :], tg[:, 0:512], big3[:, 24 + db, :], ALU.add, r=[rtg, R_big[24 + db]], w=[R_big[24 + db]])
                    stage(48)
                    for j in range(2):
                        s3, rs_, _ = wload(SL_O + j)
                        for tb in range(4):
                            pst, rps, _ = ps_rot.get()
                            for kc in range(8):
                                mm(pst[:, :], big3[:, 24 + kc, tb * 128:(tb + 1) * 128], s3[:, kc, :], kc == 0, kc == 7, r=[rs_, R_big[24 + kc]], w=[rps])
                            hv = h3[:, tb, j * 512:(j + 1) * 512]
                            stt(hv, pst[:, :], 0.5, hv, ALU.mult, ALU.add, r=[rps, R_h[tb]], w=[R_h[tb]])
                    stage(49)
                    norm_to_hnT([(h3[:, tb, :], R_h[tb]) for tb in range(4)], 128, g_mlp, 4)
                    for f in range(8):
                        s3, rs_, _ = wload(SL_F + f)
                        for q in range(4):
                            fc = 4 * f + q
                            pst, rps, _ = ps_rot.get()
                            proj_fm(s3, rs_, q, 512, pst, rps)
                            sq, rsq = W_rot.get()
                            act(sq[:, 0:512], pst[:, :], AF.Square, r=[rps], w=[rsq])
                            stt(big3[:, fc, :], pst[:, :], 0.0, sq[:, 0:512], ALU.is_gt, ALU.mult, r=[rps, rsq], w=[R_big[fc]])
                    for j in range(8):
                        _, rs_, flat = wload(SL_H + j)
                        s32 = flat.rearrange("p (f d) -> p f d", f=32)
                        pst, rps, _ = ps_rot.get()
                        for fc in range(32):
                            mm(pst[:, :], s32[:, fc, :], big3[:, fc, :], fc == 0, fc == 31, r=[rs_, R_big[fc]], w=[rps])
                        ot, rot_ = W_rot.get()
                        cp("act", ot[:, 0:512], pst[:, :], r=[rps], w=[rot_])
                        p2, rp2, _ = ps_rot.get()
                        for tb in range(4):
                            tr(p2[:, tb * 128:(tb + 1) * 128], ot[:, tb * 128:(tb + 1) * 128], identf[:, :], r=[rot_, R_const], w=[rp2])
                        hv = h3[:, :, j * 128:(j + 1) * 128]
                        tt("dve", hv, hv, p2[:, :].rearrange("p (t d) -> p t d", t=4), ALU.add, r=[rp2] + R_h, w=R_h)
                    stage(51)
                    def fnorm_gen(tb):
                        src = h3[:, tb, :]
                        xt, rx = xn_rot.get()
                        ss, rss = sm_rot.get()
                        act(xt[:, :], src, AF.Square, r=[R_h[tb]], w=[rx, rss], accum=ss)
                        yield
                        v, rv = sm_rot.get()
                        ts("pool", v, ss, 1.0 / D, EPS, ALU.mult, ALU.add, r=[rss], w=[rv])
                        yield
                        rstd, rr = sm_rot.get()
                        tt("pool", rstd, v, negh, ALU.pow, r=[rv, R_const], w=[rr])
                        yield
                        stt(src, src, rstd, gfin, ALU.mult, ALU.mult, r=[R_h[tb], rr, R_const], w=[R_h[tb]])
                        yield
                        o = dma("pool", "y%d" % tb, y_d[b, r0 + tb * 128:r0 + (tb + 1) * 128, :], src, r=[R_h[tb]], w=[])
                        final_ops.append(o)
                        yield
                    run_all(fnorm_gen(0), fnorm_gen(1))
                    run_all(fnorm_gen(2), fnorm_gen(3))

        except StopBuild:
            final_ops = [dma("pool", "y0", y_d[0, 0:128, :], hbuf[:, 0:1024], r=R_h, w=[])]
        P.emit(nc, final_ops[-4:])
    return nc


def host_consts(S):
    T = NMETA + S
    ident = np.eye(128, dtype=np.float32)
    kk = np.arange(128)
    tri = (kk[None, :] >= kk[:, None]).astype(np.float32)
    perm = np.zeros((128, 128), np.float32)
    perm[kk ^ 32, kk] = 1.0
    cmat = np.concatenate([ident, tri, perm], axis=1)
    inv = (1.0 / (np.float32(10000.0) ** (np.arange(0, 64, 2, dtype=np.float32) / np.float32(64)))).astype(np.float32)
    ang = (np.arange(T, dtype=np.float32)[:, None] * inv[None, :]).astype(np.float32)
    cos = np.cos(ang).astype(np.float32).T
    sin = np.sin(ang).astype(np.float32).T
    i = kk % 32
    sgn = np.where((kk % 64) < 32, -1.0, 1.0).astype(np.float32)
    rcos = np.ascontiguousarray(cos[i, :])
    rsin = np.ascontiguousarray(sin[i, :] * sgn[:, None])
    return cmat, rcos, rsin


def host_layout(inputs):
    f = lambda a: np.asarray(a, dtype=np.float32)
    fm = lambda v: np.ascontiguousarray(f(v).reshape(8, 128).T)
    pvec = np.zeros((128, 96), np.float32)
    pvec[:, 0:8] = fm(inputs["g_mix"][0])
    pvec[:, 8:16] = fm(inputs["g_mlp"][0])
    cw = f(inputs["conv_w"][0])
    for t in range(4):
        pvec[:, 16 + 8 * t:24 + 8 * t] = fm(cw[t])
    pvec[:, 48:56] = fm(inputs["conv_b"][0])
    pvec[:, 56:64] = fm(inputs["lru_lambda"][0])
    pvec[:, 64:72] = np.ascontiguousarray(f(inputs["b_a"][0]).T)
    pvec[:, 72:80] = np.ascontiguousarray(f(inputs["b_x"][0]).T)
    pbc = np.zeros((128, 1024 + 128 + 256), np.float32)
    pbc[:, 0:1024] = f(inputs["g_final"])[None, :]
    pbc[:, 1024:1152] = f(inputs["g_subln"][0])[None, :]
    pbc[:, 1152:1216] = f(inputs["lam_q1"][0])[None, :]
    pbc[:, 1216:1280] = f(inputs["lam_k1"][0])[None, :]
    pbc[:, 1280:1344] = f(inputs["lam_q2"][0])[None, :]
    pbc[:, 1344:1408] = f(inputs["lam_k2"][0])[None, :]
    wa = np.transpose(f(inputs["w_a"][0]), (1, 0, 2)).reshape(128, 1024)
    wx = np.transpose(f(inputs["w_x"][0]), (1, 0, 2)).reshape(128, 1024)
    wgate = np.ascontiguousarray(np.concatenate([wa, wx], axis=1))
    return pvec, pbc, wgate


_CACHE = {}


def run(inputs, S, NB, ncores):
    key = (S, NB)
    if key not in _CACHE:
        _CACHE[key] = build(S, NB)
    nc = _CACHE[key]
    cmat, rcos, rsin = host_consts(S)
    pvec, pbc, wgate = host_layout(inputs)
    f = lambda a: np.ascontiguousarray(np.asarray(a, dtype=np.float32))
    x = f(inputs["x"])
    common = {
        "meta": f(inputs["meta_tokens"]),
        "w_in": f(inputs["w_in"][0]), "w_rnn_out": f(inputs["w_rnn_out"][0]),
        "w_attn_out": f(inputs["w_attn_out"][0]), "w_o": f(inputs["w_o"][0]),
        "w_ff1": f(inputs["w_ff1"][0]), "w_ff2": f(inputs["w_ff2"][0]),
        "wgate": wgate, "pvec": pvec, "pbc": pbc, "cmat": cmat, "rcos": rcos, "rsin": rsin,
    }
    in_maps = []
    for c in range(ncores):
        m = dict(common)
        m["x"] = np.ascontiguousarray(x[c * NB:(c + 1) * NB])
        in_maps.append(m)
    res = run_bass_kernel_spmd(nc, in_maps, core_ids=list(range(ncores)))
    return np.concatenate([np.asarray(r["y"], dtype=np.float32) for r in res.results], axis=0)


def kernel(**inputs):
    return run(inputs, 4096, 2, 8)
```
